# Optimizing a Trainium2 kernel written in Bass

```python
import math
import jax, jax.numpy as jnp
from jax import lax
import numpy as np

D_MODEL = 1024
BATCH = 4
SEQ = 8192
DEPTH = 2

D_MIX = D_MODEL
A_WIDTH = D_MIX // 2
A_HEAD_DIM = 128
A_HEADS = A_WIDTH // A_HEAD_DIM
B_WIDTH = D_MIX - A_WIDTH
B_GROUPS = 4
B_GROUP_DIM = B_WIDTH // B_GROUPS
D_IN = 4 * A_WIDTH + 2 * B_WIDTH
D_FF = int(math.ceil(8 * D_MODEL / 3 / 128)) * 128
SPATIAL_CHUNK = 128
RECUR_CHUNK = 64
FFN_RES = 0.5
EPS = 1e-6
F_MIN = 1e-20

kernel_name = "hybrid_hgrn2_gmlp_macaron"


def rmsnorm(x, g):
    xf = x.astype(jnp.float32)
    y = xf * lax.rsqrt(jnp.mean(xf * xf, axis=-1, keepdims=True) + EPS)
    return (y * g.astype(jnp.float32)).astype(x.dtype)


def layernorm(x, g, b):
    xf = x.astype(jnp.float32)
    mu = jnp.mean(xf, axis=-1, keepdims=True)
    var = jnp.mean(jnp.square(xf - mu), axis=-1, keepdims=True)
    y = (xf - mu) * lax.rsqrt(var + EPS)
    return (y * g.astype(jnp.float32) + b.astype(jnp.float32)).astype(x.dtype)


def swiglu_ffn(h, w_gate, w_up, w_down):
    return (jax.nn.silu(h @ w_gate) * (h @ w_up)) @ w_down


def hgrn2_mixer(q_in, f_in, i_in, g_in, lb, norm_g):
    bsz, seq, _ = q_in.shape
    dt = q_in.dtype
    n_chunks = seq // RECUR_CHUNK
    zf = f_in.astype(jnp.float32)
    lb32 = lb.astype(jnp.float32)
    f = lb32 + (1.0 - lb32) * jax.nn.sigmoid(zf)
    log_f = jnp.log(jnp.maximum(f, F_MIN))
    k_in = (1.0 - lb32) * jax.nn.sigmoid(-zf)

    def to_chunks(t):
        return t.astype(jnp.float32).reshape(bsz, n_chunks, RECUR_CHUNK, A_HEADS, A_HEAD_DIM).transpose(1, 0, 3, 2, 4)

    causal = jnp.tril(jnp.ones((RECUR_CHUNK, RECUR_CHUNK), dtype=bool))[:, :, None]

    def step(state, inp):
        q, k, lf, v = inp
        a = jnp.cumsum(lf, axis=2)
        rel = a[:, :, :, None, :] - a[:, :, None, :, :]
        decay = jnp.where(causal, jnp.exp(jnp.where(causal, rel, 0.0)), 0.0)
        scores = jnp.einsum('bhtk,bhsk,bhtsk->bhts', q, k, decay)
        o = jnp.einsum('bhts,bhsv->bhtv', scores, v)
        o = o + jnp.einsum('bhtk,bhkv->bhtv', q * jnp.exp(a), state)
        a_last = a[:, :, -1:, :]
        k_dec = k * jnp.exp(a_last - a)
        new_state = jnp.exp(a_last[:, :, 0, :])[..., None] * state + jnp.einsum('bhsk,bhsv->bhkv', k_dec, v)
        return new_state, o

    s0 = jnp.zeros((bsz, A_HEADS, A_HEAD_DIM, A_HEAD_DIM), jnp.float32)
    _, o = lax.scan(step, s0, (to_chunks(q_in), to_chunks(k_in), to_chunks(log_f), to_chunks(i_in)))
    o = o.transpose(1, 0, 3, 2, 4).reshape(bsz, seq, A_HEADS, A_HEAD_DIM)
    o = o * lax.rsqrt(jnp.mean(o * o, axis=-1, keepdims=True) + EPS)
    o = o.reshape(bsz, seq, A_WIDTH) * norm_g.astype(jnp.float32)
    o = o * jax.nn.silu(g_in.astype(jnp.float32))
    return o.astype(dt)


def gmlp_mixer(u_in, v_in, ln_g, ln_b, w_spatial, b_spatial):
    bsz, seq, _ = u_in.shape
    u = jax.nn.gelu(u_in)
    v = layernorm(jax.nn.gelu(v_in), ln_g, ln_b)
    vr = v.reshape(bsz, seq // SPATIAL_CHUNK, SPATIAL_CHUNK, B_GROUPS, B_GROUP_DIM)
    w = jnp.where(jnp.tril(jnp.ones((SPATIAL_CHUNK, SPATIAL_CHUNK), dtype=bool))[None], w_spatial, 0.0).astype(v.dtype)
    s = jnp.einsum('gts,bnsgc->bntgc', w, vr) + b_spatial.T[None, None, :, :, None].astype(v.dtype)
    return u * s.reshape(bsz, seq, B_WIDTH)


def setup_inputs(seed: int = 0) -> dict:
    key = jax.random.key(seed)
    ks = jax.random.split(key, 24)
    f32 = jnp.float32

    def nrm(k, shape, scale):
        return jax.random.normal(k, shape, f32) * scale

    def gain(k, shape):
        return 1.0 + 0.01 * jax.random.normal(k, shape, f32)

    return {
        "x": nrm(ks[0], (BATCH, SEQ, D_MODEL), 1.0),
        "norm_ffn1": gain(ks[1], (DEPTH, D_MODEL)),
        "ffn1_w_gate": nrm(ks[2], (DEPTH, D_MODEL, D_FF), D_MODEL ** -0.5),
        "ffn1_w_up": nrm(ks[3], (DEPTH, D_MODEL, D_FF), D_MODEL ** -0.5),
        "ffn1_w_down": nrm(ks[4], (DEPTH, D_FF, D_MODEL), D_FF ** -0.5),
        "norm_mix": gain(ks[5], (DEPTH, D_MODEL)),
        "w_in": nrm(ks[6], (DEPTH, D_MODEL, D_IN), D_MODEL ** -0.5),
        "lb_param": nrm(ks[7], (DEPTH, A_WIDTH), 0.1),
        "hgrn_norm": gain(ks[8], (DEPTH, A_WIDTH)),
        "ln_v_gain": gain(ks[9], (DEPTH, B_WIDTH)),
        "ln_v_bias": nrm(ks[10], (DEPTH, B_WIDTH), 0.01),
        "w_spatial": nrm(ks[11], (DEPTH, B_GROUPS, SPATIAL_CHUNK, SPATIAL_CHUNK), 0.5 * SPATIAL_CHUNK ** -0.5),
        "b_spatial": gain(ks[12], (DEPTH, B_GROUPS, SPATIAL_CHUNK)),
        "w_out": nrm(ks[13], (DEPTH, D_MIX, D_MODEL), D_MIX ** -0.5),
        "norm_ffn2": gain(ks[14], (DEPTH, D_MODEL)),
        "ffn2_w_gate": nrm(ks[15], (DEPTH, D_MODEL, D_FF), D_MODEL ** -0.5),
        "ffn2_w_up": nrm(ks[16], (DEPTH, D_MODEL, D_FF), D_MODEL ** -0.5),
        "ffn2_w_down": nrm(ks[17], (DEPTH, D_FF, D_MODEL), D_FF ** -0.5),
        "norm_final": gain(ks[18], (D_MODEL,)),
    }


def reference(x, norm_ffn1, ffn1_w_gate, ffn1_w_up, ffn1_w_down, norm_mix, w_in, lb_param, hgrn_norm,
              ln_v_gain, ln_v_bias, w_spatial, b_spatial, w_out, norm_ffn2, ffn2_w_gate, ffn2_w_up,
              ffn2_w_down, norm_final):
    p = jax.nn.softmax(lb_param.astype(jnp.float32), axis=0)
    lbs = jnp.cumsum(p, axis=0) - p[0]
    split_idx = [A_WIDTH, 2 * A_WIDTH, 3 * A_WIDTH, 4 * A_WIDTH, 4 * A_WIDTH + B_WIDTH]
    for l in range(DEPTH):
        x = x + FFN_RES * swiglu_ffn(rmsnorm(x, norm_ffn1[l]), ffn1_w_gate[l], ffn1_w_up[l], ffn1_w_down[l])
        h = rmsnorm(x, norm_mix[l])
        q_a, f_a, i_a, g_a, u_b, v_b = jnp.split(h @ w_in[l], split_idx, axis=-1)
        o_a = hgrn2_mixer(q_a, f_a, i_a, g_a, lbs[l], hgrn_norm[l])
        o_b = gmlp_mixer(u_b, v_b, ln_v_gain[l], ln_v_bias[l], w_spatial[l], b_spatial[l])
        x = x + jnp.concatenate([o_a, o_b], axis=-1) @ w_out[l]
        x = x + FFN_RES * swiglu_ffn(rmsnorm(x, norm_ffn2[l]), ffn2_w_gate[l], ffn2_w_up[l], ffn2_w_down[l])
    return rmsnorm(x, norm_final)
```

```python
import numpy as np
from contextlib import ExitStack

import concourse.bass as bass
import concourse.mybir as mybir
from concourse.bass_utils import run_bass_kernel_spmd

F32 = mybir.dt.float32
BF16 = mybir.dt.bfloat16
AF = mybir.ActivationFunctionType
ALU = mybir.AluOpType

NCORES = 8
D = 1024
DC = 8
FF = 2816
FC = 22
T = 512
NT = 8
TOK = NT * T
DEPTH = 2
EPS = 1e-6
F_MIN = 1e-20
NH = 4
CH = 64

ENGS = ("pe", "act", "dve", "pool", "sp")


class Prog:
    def __init__(self, nc, stack):
        self.nc = nc
        self.q = {e: [] for e in ENGS}
        self.sem = {}
        self.cnt = {}
        self.stack = stack
        for e in ("pe", "act", "dve", "pool"):
            self.sem[e] = stack.enter_context(nc.semaphore("sem_" + e))
            self.cnt[e] = 0
        self.lastw = {}
        self.readers = {}
        self.waited = {e: {} for e in ENGS}
        self.scope = None
        self.use_scopes = False
        self.alias = {}
        self._uid = 0
        self._rot = {}
        self.MAXFLY = 8

    def subkey(self, group):
        self._uid += 1
        lst = self.alias.setdefault(group, [])
        if len(lst) < self.MAXFLY:
            k = "%s#%d" % (group, self._uid)
            self.slot = len(lst)
            lst.append(k)
            self._rot[group] = 0
            return k
        i = self._rot.get(group, 0)
        self._rot[group] = (i + 1) % len(lst)
        self.slot = i
        return lst[i]

    def reset_group(self, group):
        self.alias[group] = []

    def _sem(self, key):
        if key not in self.sem:
            self.sem[key] = self.stack.enter_context(self.nc.semaphore("sem_" + key))
            self.cnt[key] = 0
        return self.sem[key]

    def _expand(self, keys):
        out = []
        for k in keys:
            if k in self.alias:
                out.extend(self.alias[k])
            else:
                out.append(k)
        return out

    def op(self, eng, fn, reads=(), writes=(), dma=None, inc=16):
        reads = self._expand(reads)
        writes = self._expand(writes)
        evs = []
        for k in reads:
            if k in self.lastw:
                evs.append(self.lastw[k])
        for k in writes:
            if k in self.lastw:
                evs.append(self.lastw[k])
            evs.extend(self.readers.get(k, ()))
        best = {}
        for (s, v, src) in evs:
            if eng == "pe" and src == "pe":
                continue
            if v > best.get(s, 0):
                best[s] = v
        waits = []
        for s, v in best.items():
            if v > self.waited[eng].get(s, 0):
                self.waited[eng][s] = v
                waits.append((s, v))
        if dma is None:
            self.cnt[eng] += 1
            ev = (eng, self.cnt[eng], eng)
        else:
            self._sem(dma)
            self.cnt[dma] += inc
            ev = (dma, self.cnt[dma], "dma%d" % inc)
        self.q[eng].append((fn, waits, ev, self.scope))
        for k in reads:
            self.readers.setdefault(k, []).append(ev)
        for k in writes:
            self.lastw[k] = ev
            self.readers[k] = []
        return ev

    def barrier(self):
        allev = []
        for e in ("pe", "act", "dve", "pool"):
            if self.cnt[e] > 0:
                allev.append((e, self.cnt[e]))
        for k in self.cnt:
            if k not in ("pe", "act", "dve", "pool") and self.cnt[k] > 0 and not k.startswith("bg_") and not k.startswith("cc"):
                allev.append((k, self.cnt[k]))
        for eng in ENGS:
            waits = []
            for s, v in allev:
                if s == eng:
                    continue
                if v > self.waited[eng].get(s, 0):
                    self.waited[eng][s] = v
                    waits.append((s, v))
            if waits:
                self.q[eng].append((None, waits, None, None))
        self.lastw = {k: v for k, v in self.lastw.items() if k.startswith("wsc") or k.startswith("ccout")}
        self.readers = {k: v for k, v in self.readers.items() if k.startswith("wsc") or k.startswith("ccout")}

    def final_wait(self, eng="sp"):
        waits = []
        for k in self.cnt:
            if self.cnt[k] > 0 and k != eng:
                waits.append((k, self.cnt[k]))
        self.q[eng].append((None, waits, None, None))

    def _run(self, e, engine):
        cur = None
        for fn, waits, ev, scope in self.q[e]:
            if self.use_scopes and fn is not None and scope != cur:
                if cur is not None:
                    self.nc.leave_named_scope(cur, cur_id, False)
                cur = scope
                if cur is not None:
                    cur_id, _ = self.nc.enter_named_scope(cur, False)
            for s, v in waits:
                engine.wait_ge(self.sem[s], v)
            if fn is None:
                continue
            ins = fn(engine)
            if ev[2] == "dma16":
                ins.then_inc(self.sem[ev[0]], 16)
            else:
                ins.then_inc(self.sem[ev[0]], 1)
        if self.use_scopes and cur is not None:
            self.nc.leave_named_scope(cur, cur_id, False)

    def emit(self):
        with self.nc.Block() as blk:

            @blk.tensor
            def _(e):
                self._run("pe", e)

            @blk.scalar
            def _(e):
                self._run("act", e)

            @blk.vector
            def _(e):
                self._run("dve", e)

            @blk.gpsimd
            def _(e):
                self._run("pool", e)

            @blk.sync
            def _(e):
                self._run("sp", e)


class Arena:
    def __init__(self, t, ncols):
        self.t = t
        self.n = ncols
        self.off = 0
        self.marks = []

    def f32(self, cols):
        assert self.off + cols <= self.n, ("arena overflow", self.off, cols, self.n)
        ap = self.t[:, self.off:self.off + cols]
        self.off += cols
        return ap

    def bf16(self, cols):
        c32 = (cols + 1) // 2
        ap = self.f32(c32)
        return ap.bitcast(BF16)

    def mark(self):
        self.marks.append(self.off)

    def release(self):
        self.off = self.marks.pop()


def _mm_group(pe, out, pairs):
    n = len(pairs)
    ins = None
    for idx, (l, r) in enumerate(pairs):
        ins = pe.matmul(out, l, r, start=(idx == 0), stop=(idx == n - 1))
    return ins


class Builder:
    def __init__(self, phases, debug_raw_out=False, xs_in=False, xs_out=False, fused=False, scopes=False, prefetch=True):
        self.phases = phases
        self.debug_raw_out = debug_raw_out
        self.use_xs_in = xs_in
        self.scopes = scopes
        self.prefetch = prefetch
        self.use_xs_out = xs_out
        self.fused = fused
        if fused:
            self.nc = bass.Bass("TRN2", target_bir_lowering=False, num_devices=NCORES)
        else:
            self.nc = bass.Bass("TRN2", target_bir_lowering=False)
        nc = self.nc
        self.stack = ExitStack()
        dt = nc.dram_tensor
        need_f = set(ph[2] for ph in phases if ph[0] == "ffn")
        need_m = any(ph[0] == "mix" for ph in phases)
        self.need_f, self.need_m = need_f, need_m
        self.x_in = dt("x_in", [NT, 128, DC * T], F32, kind="ExternalInput").ap()
        self.out = dt("out", [NT, 128, DC * T], F32, kind="ExternalOutput").ap()
        self.xs = dt("xs", [NT, 128, DC * T], F32, kind="ExternalOutput" if xs_out else "Internal").ap()
        self.xs_in = dt("xs_in", [NT, 128, DC * T], F32, kind="ExternalInput").ap() if xs_in else None
        self.xs_src = self.xs_in if xs_in else self.xs
        self.win = dt("w_in", [DEPTH, D, 3072], F32, kind="ExternalInput").ap() if need_m else None
        self.wout = dt("w_out", [DEPTH, D, D], F32, kind="ExternalInput").ap() if need_m else None
        self.mixp_d = dt("mixp", [128, 32], F32, kind="ExternalInput").ap()
        self.wst_d = dt("wst", [DEPTH, 128, 512], F32, kind="ExternalInput").ap()
        self.bsp_d = dt("bsp", [DEPTH, 128, 512], F32, kind="ExternalInput").ap()
        self.sc_kT = dt("sc_kT", [NT, 128, NH * T], BF16, kind="Internal").ap()
        self.sc_kd = dt("sc_kd", [NT, 64, 8 * 512], BF16, kind="Internal").ap()
        self.sc_v = dt("sc_v", [NT, 64, 8 * 512], BF16, kind="Internal").ap()
        self.sc_ea = dt("sc_ea", [NT, 128, NH * T], F32, kind="Internal").ap()
        self.sc_E = dt("sc_E", [NT, 128, NH * 8], F32, kind="Internal").ap()
        self.wsc_g = dt("wsc_g", [D, FF], BF16, kind="Internal").ap()
        self.wsc_u = dt("wsc_u", [D, FF], BF16, kind="Internal").ap()
        self.wsc_d = dt("wsc_d", [FF, D], BF16, kind="Internal").ap()
        self.wsc_in = dt("wsc_in", [D, 3072], BF16, kind="Internal").ap()
        self.wsc_out = dt("wsc_out", [D, D], BF16, kind="Internal").ap()
        if fused:
            self.cc_in = [dt("cc_in%d" % l, [128, 512], F32, kind="Internal").ap() for l in range(DEPTH)]
            self.cc_out = [dt("cc_out%d" % l, [256, 512], F32, kind="Internal").ap() for l in range(DEPTH)]
            self.role_d = dt("role", [128, 1], F32, kind="ExternalInput").ap()
        else:
            self.s_in = dt("s_in", [128, 512], F32, kind="ExternalInput").ap()
            self.s_out = dt("s_out", [128, 512], F32, kind="ExternalOutput").ap()
        self.wg = [dt("wg%d" % k, [DEPTH, D, FF], F32, kind="ExternalInput").ap() if k in need_f else None for k in (1, 2)]
        self.wu = [dt("wu%d" % k, [DEPTH, D, FF], F32, kind="ExternalInput").ap() if k in need_f else None for k in (1, 2)]
        self.wd = [dt("wd%d" % k, [DEPTH, FF, D], F32, kind="ExternalInput").ap() if k in need_f else None for k in (1, 2)]
        self.gains_d = dt("gains", [128, 7 * DC], F32, kind="ExternalInput").ap()
        self.cst_d = dt("cst", [128, 1152], F32, kind="ExternalInput").ap()

    def build(self):
        nc = self.nc
        st = self.stack
        with st:
            NCOL = 52000
            arena_t = st.enter_context(nc.sbuf_tensor("arena", [128, NCOL], F32))
            self.A = Arena(arena_t, NCOL)
            self.ps = [st.enter_context(nc.psum_tensor("ps%d" % k, [128, 512], F32)) for k in range(8)]
            self.P = Prog(nc, st)
            self.setup_persistent()
            self.P.use_scopes = self.scopes
            self.conv_done = set()
            for pi, ph in enumerate(self.phases):
                self.P.barrier()
                self.P.scope = "p%d_%s" % (pi, "_".join(str(v) for v in ph))
                self.P.reset_group("W")
                self.P.reset_group("Wa")
                self.P.reset_group("Wb")
                self.P.reset_group("Wo")
                self.A.mark()
                kind = ph[0]
                self.cur_pi = pi
                if kind == "ffn":
                    self.ffn_phase(ph[1], ph[2], first=ph[3])
                elif kind == "post":
                    self.post_phase()
                elif kind == "mix":
                    if ph[2]:
                        self.state_phase(ph[1])
                    else:
                        self.mix_phase2(ph[1])
                else:
                    raise ValueError(kind)
                self.A.release()
                if kind == "ffn" or (kind == "mix" and not ph[2]):
                    self.xs_src = self.xs
            self.P.final_wait("sp")
            self.P.emit()
        return nc

    def convert_ffn(self, l, which):
        P = self.P
        wg_d, wu_d, wd_d = self.wg[which - 1], self.wu[which - 1], self.wd[which - 1]
        oldk = list(P.alias.get("wscF", []))
        P.reset_group("wscF")
        for c in range(DC):
            r = slice(c * 128, (c + 1) * 128)
            P.op("pool", lambda e, r=r: e.dma_start(out=self.wsc_g[r, :], in_=wg_d[l, r, :]), reads=["W", "Wa", "Wb", "Wo"],
                 writes=[P.subkey("wscF")] + (oldk if c == 0 else []), dma="bg_F%d" % P.slot)
            P.op("pool", lambda e, r=r: e.dma_start(out=self.wsc_u[r, :], in_=wu_d[l, r, :]), reads=["W", "Wa", "Wb", "Wo"], writes=[P.subkey("wscF")], dma="bg_F%d" % P.slot)
        for j in range(0, FC, 2):
            r = slice(j * 128, (j + 2) * 128)
            P.op("pool", lambda e, r=r: e.dma_start(out=self.wsc_d[r, :], in_=wd_d[l, r, :]), reads=["W", "Wa", "Wb", "Wo"], writes=[P.subkey("wscF")], dma="bg_F%d" % P.slot)
        self.conv_done.add(("ffn", l, which))

    def convert_mix(self, l):
        P = self.P
        oldk = list(P.alias.get("wscM", []))
        P.reset_group("wscM")
        for c in range(DC):
            r = slice(c * 128, (c + 1) * 128)
            P.op("pool", lambda e, r=r: e.dma_start(out=self.wsc_in[r, :], in_=self.win[l, r, :]), reads=["W", "Wa", "Wb", "Wo"],
                 writes=[P.subkey("wscM")] + (oldk if c == 0 else []), dma="bg_M%d" % P.slot)
        for c in range(0, DC, 2):
            r = slice(c * 128, (c + 2) * 128)
            P.op("pool", lambda e, r=r: e.dma_start(out=self.wsc_out[r, :], in_=self.wout[l, r, :]), reads=["W", "Wa", "Wb", "Wo"], writes=[P.subkey("wscM")], dma="bg_M%d" % P.slot)
        self.conv_done.add(("mix", l))

    def schedule_conversion(self, pi):
        if not self.prefetch:
            return
        for ph in self.phases[pi + 1:]:
            if ph[0] == "ffn":
                key = ("ffn", ph[1], ph[2])
                if key not in self.conv_done:
                    if self.phases[pi][0] == "mix" and self.phases[pi][2]:
                        return
                    self.convert_ffn(ph[1], ph[2])
                return
            if ph[0] == "mix":
                key = ("mix", ph[1])
                if key not in self.conv_done:
                    self.convert_mix(ph[1])
                    return
                continue
            return

    def setup_persistent(self):
        A, P = self.A, self.P
        self.cst = A.f32(1152)
        self.tri64 = self.cst[0:64, 256:512]
        self.tri128 = self.cst[:, 512:640]
        self.rmask = self.cst[:, 640:1152]
        self.ident_f = self.cst[:, 0:128]
        self.ident_b = self.cst[:, 128:192].bitcast(BF16)
        self.ones_b = self.cst[:, 192:256].bitcast(BF16)
        self.gains = A.f32(7 * DC)
        self.g32 = A.f32(7 * DC)
        self.xt = [A.f32(DC * T), A.f32(DC * T)]
        self.hb = A.bf16(DC * T)
        self.rstd = A.f32(T)
        P.op("sp", lambda e: e.dma_start(out=self.cst, in_=self.cst_d), writes=["cst"], dma="ld_c")
        P.op("sp", lambda e: e.dma_start(out=self.gains, in_=self.gains_d), writes=["gains"], dma="ld_g")
        if self.fused:
            self.role = A.f32(1)
            P.op("sp", lambda e: e.dma_start(out=self.role, in_=self.role_d), writes=["role"], dma="ld_role")
        self.mixp = A.f32(32)
        sc = A.f32(64)
        self.lb = A.f32(8)
        self.homl = A.f32(8)
        self.nhoml = A.f32(8)
        self.lbh = A.f32(8)
        self.gnp = A.f32(8)
        P.op("sp", lambda e: e.dma_start(out=self.mixp, in_=self.mixp_d), writes=["mixp"], dma="ld_mp")
        l0, l1 = self.mixp[:, 0:4], self.mixp[:, 4:8]
        m, d0, d1, e0, e1, ss_, r_, p0, p1, c1 = [sc[:, 4 * k:4 * k + 4] for k in range(10)]
        dv = lambda fn, rd, wr: P.op("dve", fn, reads=rd, writes=wr)
        dv(lambda e: e.tensor_tensor(m, l0, l1, ALU.max), ["mixp"], ["sc"])
        dv(lambda e: e.tensor_tensor(d0, l0, m, ALU.subtract), ["mixp", "sc"], ["sc"])
        dv(lambda e: e.tensor_tensor(d1, l1, m, ALU.subtract), ["mixp", "sc"], ["sc"])
        P.op("act", lambda e: e.activation(e0, d0, AF.Exp), reads=["sc"], writes=["sc"])
        P.op("act", lambda e: e.activation(e1, d1, AF.Exp), reads=["sc"], writes=["sc"])
        dv(lambda e: e.tensor_tensor(ss_, e0, e1, ALU.add), ["sc"], ["sc"])
        dv(lambda e: e.reciprocal(r_, ss_), ["sc"], ["sc"])
        dv(lambda e: e.tensor_tensor(p0, e0, r_, ALU.mult), ["sc"], ["sc"])
        dv(lambda e: e.tensor_tensor(p1, e1, r_, ALU.mult), ["sc"], ["sc"])
        dv(lambda e: e.tensor_tensor(c1, p0, p1, ALU.add), ["sc"], ["sc"])
        dv(lambda e: e.tensor_tensor(self.lb[:, 0:4], p0, p0, ALU.subtract), ["sc"], ["lbp"])
        dv(lambda e: e.tensor_tensor(self.lb[:, 4:8], c1, p0, ALU.subtract), ["sc"], ["lbp"])
        dv(lambda e: e.tensor_scalar(self.homl, self.lb, -0.5, 0.5, ALU.mult, ALU.add), ["lbp"], ["lbq"])
        dv(lambda e: e.tensor_scalar(self.nhoml, self.lb, 0.5, -0.5, ALU.mult, ALU.add), ["lbp"], ["lbq"])
        dv(lambda e: e.tensor_scalar(self.lbh, self.lb, 0.5, 0.5, ALU.mult, ALU.add), ["lbp"], ["lbq"])
        dv(lambda e: e.tensor_scalar(self.gnp, self.mixp[:, 8:16], float(0.5 * np.sqrt(128.0)), None, ALU.mult),
           ["mixp"], ["lbq"])
        P.op("dve", lambda e: e.tensor_scalar(self.g32, self.gains, 32.0, None, ALU.mult),
             reads=["gains"], writes=["g32"])

    def load_x(self, i, first):
        P = self.P
        b = i % 2
        xt = self.xt[b]
        if not first:
            src = self.xs_src
            P.op("sp", lambda e: e.dma_start(out=xt, in_=src[i]), writes=["xt%d" % b], dma="ldx%d" % b)
            return
        P.op("sp", lambda e: e.dma_start(out=xt, in_=self.x_in[i]), writes=["xt%d" % b], dma="ldx%d" % b)

    def store_x(self, i):
        P = self.P
        b = i % 2
        xt = self.xt[b]
        P.op("sp", lambda e: e.dma_start(out=self.xs[i], in_=xt), reads=["xt%d" % b], dma="stx%d" % b)

    def rstd_from(self, src, skey, c, dst=None, dkey="rstd"):
        P = self.P
        if dst is None:
            dst = self.rstd
        P.op("dve", lambda e: e.tensor_scalar(dst, src, c, None, ALU.add), reads=[skey], writes=[dkey])
        P.op("act", lambda e: e.activation(dst, dst, AF.Ln), reads=[dkey], writes=[dkey])
        P.op("act", lambda e: e.activation(dst, dst, AF.Exp, scale=-0.5), reads=[dkey], writes=[dkey])

    def norm_stage(self, i, gidx):
        P = self.P
        b = i % 2
        xt = self.xt[b]
        hb = self.hb
        xk = "xt%d" % b
        hkeys = ["hb%d" % c for c in range(DC)]
        P.op("act", lambda e: e.activation(hb, xt, AF.Square), reads=[xk], writes=hkeys)
        P.op("pe", lambda e: _mm_group(e, self.ps[0][:, :], [
            (self.ones_b, hb[:, c * T:(c + 1) * T]) for c in range(DC)]),
            reads=hkeys + ["cst"], writes=["pb0"])
        self.rstd_from(self.ps[0][:, :], "pb0", float(D * EPS))
        for c in range(DC):
            P.op("dve", lambda e, c=c: e.scalar_tensor_tensor(
                out=hb[:, c * T:(c + 1) * T], in0=xt[:, c * T:(c + 1) * T],
                scalar=self.g32[:, gidx * DC + c:gidx * DC + c + 1], in1=self.rstd,
                op0=ALU.mult, op1=ALU.mult),
                reads=[xk, "rstd", "g32"], writes=["hb%d" % c])

    def ffn_phase(self, l, which, first=False):
        A, P = self.A, self.P
        gidx = l * 3 + (0 if which == 1 else 2)
        wg_d, wu_d, wd_d = self.wg[which - 1], self.wu[which - 1], self.wd[which - 1]
        Wg = A.bf16(DC * FF)
        Wu = A.bf16(DC * FF)
        Wd = A.bf16(FC * D)
        HF = FC // 2
        act = A.bf16(HF * T)
        stmp = [A.f32(T), A.f32(T)]
        if ("ffn", l, which) in self.conv_done:
            for c in range(DC):
                P.op("sp", lambda e, c=c: e.dma_start(out=Wg[:, c * FF:(c + 1) * FF], in_=self.wsc_g[c * 128:(c + 1) * 128, :]),
                     reads=["wscF"], writes=[P.subkey("Wa")], dma="ldwh%d" % P.slot)
                P.op("sp", lambda e, c=c: e.dma_start(out=Wu[:, c * FF:(c + 1) * FF], in_=self.wsc_u[c * 128:(c + 1) * 128, :]),
                     reads=["wscF"], writes=[P.subkey("Wa")], dma="ldwh%d" % P.slot)
            for j in range(FC):
                P.op("sp", lambda e, j=j: e.dma_start(out=Wd[:, j * D:(j + 1) * D], in_=self.wsc_d[j * 128:(j + 1) * 128, :]),
                     reads=["wscF"], writes=[P.subkey("Wb")], dma="ldwhb%d" % P.slot)
        else:
            for c in range(DC):
                P.op("pool", lambda e, c=c: e.dma_start(out=Wg[:, c * FF:(c + 1) * FF], in_=wg_d[l, c * 128:(c + 1) * 128, :]),
                     writes=[P.subkey("Wa")], dma="ldw%d" % P.slot)
                P.op("pool", lambda e, c=c: e.dma_start(out=Wu[:, c * FF:(c + 1) * FF], in_=wu_d[l, c * 128:(c + 1) * 128, :]),
                     writes=[P.subkey("Wa")], dma="ldw%d" % P.slot)
            for j in range(FC):
                P.op("pool", lambda e, j=j: e.dma_start(out=Wd[:, j * D:(j + 1) * D], in_=wd_d[l, j * 128:(j + 1) * 128, :]),
                     writes=[P.subkey("Wb")], dma="ldwb%d" % P.slot)
        self.schedule_conversion(self.cur_pi)
        ps = self.ps
        G = [ps[1], ps[3]]
        U = [ps[2], ps[4]]
        Y = [ps[5], ps[6], ps[7]]
        hb = self.hb
        hkeys = ["hb%d" % c for c in range(DC)]

        def gate_up(i, j, jj):
            par = jj % 2
            P.op("pe", lambda e: _mm_group(e, G[par][:, :], [
                (Wg[:, c * FF + j * 128:c * FF + (j + 1) * 128], hb[:, c * T:(c + 1) * T]) for c in range(DC)]),
                reads=hkeys + ["Wa"], writes=["pb%d" % (1 + 2 * par)])
            P.op("pe", lambda e: _mm_group(e, U[par][:, :], [
                (Wu[:, c * FF + j * 128:c * FF + (j + 1) * 128], hb[:, c * T:(c + 1) * T]) for c in range(DC)]),
                reads=hkeys + ["Wa"], writes=["pb%d" % (2 + 2 * par)])
            P.op("act", lambda e: e.activation(stmp[par], G[par][:, :], AF.Silu),
                 reads=["pb%d" % (1 + 2 * par)], writes=["stmp%d" % par])
            P.op("dve", lambda e: e.tensor_tensor(act[:, jj * T:(jj + 1) * T], U[par][:, :], stmp[par], ALU.mult),
                 reads=["pb%d" % (2 + 2 * par), "stmp%d" % par], writes=["act%d" % jj])

        def down(i, half, dc):
            b = i % 2
            xt = self.xt[b]
            k = dc % 3
            P.op("pe", lambda e: _mm_group(e, Y[k][:, :], [
                (Wd[:, (half * HF + jj) * D + dc * 128:(half * HF + jj) * D + (dc + 1) * 128],
                 act[:, jj * T:(jj + 1) * T]) for jj in range(HF)]),
                reads=["act%d" % jj for jj in range(HF)] + ["Wb"], writes=["pb%d" % (5 + k)])
            P.op("dve", lambda e: e.scalar_tensor_tensor(
                out=xt[:, dc * T:(dc + 1) * T], in0=Y[k][:, :], scalar=0.5, in1=xt[:, dc * T:(dc + 1) * T],
                op0=ALU.mult, op1=ALU.add),
                reads=["pb%d" % (5 + k), "xt%d" % b], writes=["xt%d" % b])

        self.load_x(0, first)
        if NT > 1:
            self.load_x(1, first)
        self.norm_stage(0, gidx)
        for i in range(NT):
            for half in range(2):
                for jj in range(HF):
                    gate_up(i, half * HF + jj, jj)
                for dc in range(DC):
                    down(i, half, dc)
                    if half == 1 and dc == 1 and i + 1 < NT:
                        self.norm_stage(i + 1, gidx)
            self.store_x(i)
            if i + 2 < NT:
                self.load_x(i + 2, first)

    def state_phase(self, l):
        A, P, ps = self.A, self.P, self.ps
        gidx = l * 3 + 1
        WIN = 3072
        Win = A.bf16(DC * WIN)
        B1 = [A.f32(NH * T), A.f32(NH * T)]
        B2 = [A.f32(NH * T), A.f32(NH * T)]
        B3 = [A.f32(NH * T), A.f32(NH * T)]
        Ecol = [A.f32(NH * 8), A.f32(NH * 8)]
        kT = [A.bf16(NH * T), A.bf16(NH * T)]
        kdT = [A.bf16(NH * T), A.bf16(NH * T)]
        vtm = [A.bf16(8 * 512), A.bf16(8 * 512)]
        kdtm = [A.bf16(8 * 512), A.bf16(8 * 512)]
        S = A.f32(512)
        hkeys = ["hb%d" % c for c in range(DC)]
        skeys = ["S%d" % h for h in range(NH)]
        lcol = lambda t, hd: t[:, l * 4 + hd:l * 4 + hd + 1]
        trbs = [ps[5][:, :].bitcast(BF16), ps[4][:, :].bitcast(BF16)]
        trk = ["pb5", "pb4"]
        dsb = [ps[7], ps[6]]
        dsk = ["pb7", "pb6"]

        pre = ("mix", l) in self.conv_done
        wq = "sp" if pre else "pool"
        win_src = self.wsc_in if pre else self.win[l]
        wrd = ["wscM"] if pre else []
        for c in range(DC):
            P.op(wq, lambda e, c=c: e.dma_start(out=Win[:, c * WIN + 512:c * WIN + 1536],
                                                in_=win_src[c * 128:(c + 1) * 128, 512:1536]),
                 reads=wrd, writes=[P.subkey("W")], dma=("ldwh%d" if pre else "ldw%d") % P.slot)
        P.op("dve", lambda e: e.memset(S, 0.0), writes=skeys)
        self.schedule_conversion(self.cur_pi)

        pj = [0]

        def pjbank():
            k = 1 + pj[0] % 3
            pj[0] += 1
            return ps[k], "pb%d" % k

        def stage_a(i):
            p = i % 2
            b1, b2, b3 = B1[p], B2[p], B3[p]
            k1 = ["sB1_%d_%d" % (p, h) for h in range(NH)]
            k2 = ["sB2_%d_%d" % (p, h) for h in range(NH)]
            k3 = ["sB3_%d_%d" % (p, h) for h in range(NH)]
            for hd in range(NH):
                sl = slice(hd * T, (hd + 1) * T)
                bank, bk = pjbank()
                P.op("pe", lambda e, hd=hd, bank=bank: _mm_group(e, bank[:, :], [
                    (Win[:, c * WIN + 512 + hd * 128:c * WIN + 512 + (hd + 1) * 128], self.hb[:, c * T:(c + 1) * T])
                    for c in range(DC)]), reads=hkeys + ["W"], writes=[bk])
                P.op("act", lambda e, sl=sl, bank=bank: e.activation(b1[:, sl], bank[:, :], AF.Tanh, scale=0.5),
                     reads=[bk], writes=[k1[hd]])
                P.op("dve", lambda e, sl=sl, hd=hd: e.tensor_scalar(b2[:, sl], b1[:, sl], lcol(self.nhoml, hd),
                                                                   lcol(self.homl, hd), ALU.mult, ALU.add),
                     reads=[k1[hd], "lbq"], writes=[k2[hd]])
            for cc in range(8):
                bank, bk = pjbank()
                P.op("pe", lambda e, cc=cc, bank=bank: _mm_group(e, bank[0:64, :], [
                    (self.hb[:, c * T + cc * CH:c * T + (cc + 1) * CH], Win[:, c * WIN + 1024:c * WIN + 1536])
                    for c in range(DC)]), reads=hkeys + ["W"], writes=[bk])
                P.op("act", lambda e, cc=cc, bank=bank: e.activation(vtm[p][0:64, cc * 512:(cc + 1) * 512], bank[0:64, :], AF.Copy),
                     reads=[bk], writes=["svtm%d_%d" % (p, cc)])
            if i + 1 < NT:
                self.norm_stage(i + 1, gidx)
            HS = [(hd, slice(hd * T, (hd + 1) * T)) for hd in range(NH)]
            for hd, sl in HS:
                P.op("act", lambda e, sl=sl, hd=hd: e.activation(b1[:, sl], b1[:, sl], AF.Ln, bias=lcol(self.lbh, hd),
                                                                scale=lcol(self.homl, hd)),
                     reads=[k1[hd], "lbq"], writes=[k1[hd]])
            for hd, sl in HS:
                P.op("dve", lambda e, sl=sl: e.tensor_scalar_max(b1[:, sl], b1[:, sl], float(np.log(F_MIN))),
                     reads=[k1[hd]], writes=[k1[hd]])
                P.op("dve", lambda e, sl=sl: e.tensor_tensor_scan(b3[:, sl], self.rmask, b1[:, sl], 0.0, ALU.mult, ALU.add),
                     reads=[k1[hd], "cst"], writes=[k3[hd]])
            for hd, sl in HS:
                P.op("act", lambda e, sl=sl: e.activation(b1[:, sl], b3[:, sl], AF.Exp), reads=[k3[hd]], writes=[k1[hd]])
                P.op("act", lambda e, sl=sl: e.activation(b3[:, sl], b3[:, sl], AF.Exp, scale=-1.0),
                     reads=[k3[hd]], writes=[k3[hd]])
            for hd, sl in HS:
                P.op("dve", lambda e, hd=hd: e.tensor_copy(Ecol[p][:, hd * 8:(hd + 1) * 8], b1[:, hd * T + CH - 1:(hd + 1) * T:CH]),
                     reads=[k1[hd]], writes=["sEcol%d_%d" % (p, hd)])
                P.op("dve", lambda e, sl=sl: e.tensor_tensor(b2[:, sl], b2[:, sl], b3[:, sl], ALU.mult),
                     reads=[k2[hd], k3[hd]], writes=[k2[hd]])
                P.op("dve", lambda e, hd=hd, sl=sl: e.tensor_tensor(
                    kdT[p][:, sl].rearrange("p (c t) -> p c t", t=CH), b2[:, sl].rearrange("p (c t) -> p c t", t=CH),
                    Ecol[p][:, hd * 8:(hd + 1) * 8].unsqueeze(2).broadcast_to([128, 8, CH]), ALU.mult),
                    reads=[k2[hd], "sEcol%d_%d" % (p, hd)], writes=["skdT%d_%d" % (p, hd)])
            for hd, sl in HS:
                P.op("act", lambda e, sl=sl: e.activation(kT[p][:, sl], b2[:, sl], AF.Copy), reads=[k2[hd]], writes=["skT%d_%d" % (p, hd)])
            P.op("sp", lambda e, i=i: e.dma_start(out=self.sc_kT[i], in_=kT[p]), reads=["skT%d_%d" % (p, h) for h in range(NH)], dma="st_kT%d" % p)
            P.op("sp", lambda e, i=i: e.dma_start(out=self.sc_v[i], in_=vtm[p][0:64, :]), reads=["svtm%d_%d" % (p, c) for c in range(8)], dma="st_v%d" % p)
            P.op("sp", lambda e, i=i: e.dma_start(out=self.sc_ea[i], in_=b1), reads=k1, dma="st_ea%d" % p)
            P.op("sp", lambda e, i=i: e.dma_start(out=self.sc_E[i], in_=Ecol[p]), reads=["sEcol%d_%d" % (p, h) for h in range(NH)], dma="st_E%d" % p)

        def stage_b(i):
            p = i % 2
            for cc in range(8):
                half = cc % 2
                trb = trbs[half]

                def trf(e, cc=cc, trb=trb):
                    ins = None
                    for hd in range(NH):
                        ins = e.transpose(trb[0:64, hd * 128:(hd + 1) * 128],
                                          kdT[p][:, hd * T + cc * CH:hd * T + (cc + 1) * CH], self.ident_b)
                    return ins
                P.op("pe", trf, reads=["skdT%d_%d" % (p, hd) for hd in range(NH)] + ["cst"], writes=[trk[half]])
                if cc % 2 == 0:
                    P.op("act", lambda e, cc=cc, trb=trb: e.activation(
                        kdtm[p][0:64, cc * 512:(cc + 1) * 512], trb[0:64, 0:512], AF.Copy),
                        reads=[trk[half]], writes=["skdtm%d_%d" % (p, cc)])
                else:
                    P.op("dve", lambda e, cc=cc, trb=trb: e.tensor_copy(
                        kdtm[p][0:64, cc * 512:(cc + 1) * 512], trb[0:64, 0:512]),
                        reads=[trk[half]], writes=["skdtm%d_%d" % (p, cc)])
            P.op("sp", lambda e, i=i: e.dma_start(out=self.sc_kd[i], in_=kdtm[p][0:64, :]),
                 reads=["skdtm%d_%d" % (p, c) for c in range(8)], dma="st_kd%d" % p)
            for cc in range(8):
                bank, bkey = dsb[cc % 2], dsk[cc % 2]

                def dsf(e, cc=cc, bank=bank):
                    ins = None
                    for hd in range(NH):
                        ins = e.matmul(bank[:, hd * 128:(hd + 1) * 128], kdtm[p][0:64, cc * 512 + hd * 128:cc * 512 + (hd + 1) * 128],
                                       vtm[p][0:64, cc * 512 + hd * 128:cc * 512 + (hd + 1) * 128], start=True, stop=True)
                    return ins
                P.op("pe", dsf, reads=["skdtm%d_%d" % (p, cc), "svtm%d_%d" % (p, cc)], writes=[bkey])
                for hd in range(NH):
                    P.op("dve", lambda e, hd=hd, cc=cc, bank=bank: e.scalar_tensor_tensor(
                        out=S[:, hd * 128:(hd + 1) * 128], in0=S[:, hd * 128:(hd + 1) * 128],
                        scalar=Ecol[p][:, hd * 8 + cc:hd * 8 + cc + 1], in1=bank[:, hd * 128:(hd + 1) * 128],
                        op0=ALU.mult, op1=ALU.add),
                        reads=["S%d" % hd, bkey, "sEcol%d_%d" % (p, hd)], writes=["S%d" % hd])

        self.load_x(0, False)
        if NT > 1:
            self.load_x(1, False)
        self.norm_stage(0, gidx)
        stage_a(0)
        for i in range(NT):
            if i + 2 < NT:
                self.load_x(i + 2, False)
            if i + 1 < NT:
                stage_a(i + 1)
            stage_b(i)
        if self.fused:
            groups = [[2 * k, 2 * k + 1] for k in range(NCORES // 2)]
            P.op("sp", lambda e: e.dma_start(out=self.cc_in[l], in_=S), reads=skeys, writes=["ccin%d" % l], dma="st_s")
            P.op("pool", lambda e: e.collective_compute("AllGather", ALU.bypass, replica_groups=groups,
                                                        ins=[self.cc_in[l]], outs=[self.cc_out[l]]),
                 reads=["ccin%d" % l], writes=["ccout%d" % l], dma="cc%d" % l, inc=1)
        else:
            P.op("sp", lambda e: e.dma_start(out=self.s_out, in_=S), reads=skeys, dma="st_s")

    def mix_phase2(self, l):
        A, P, ps = self.A, self.P, self.ps
        gidx = l * 3 + 1
        WIN = 3072
        Win = A.bf16(DC * WIN)
        B1 = A.f32(NH * T)
        B2 = A.f32(NH * T)
        B3 = A.f32(NH * T)
        Ecol = A.f32(NH * 8)
        kT = A.bf16(NH * T)
        osq = A.bf16(NH * T)
        vtm = A.bf16(8 * 512)
        kdtm = A.bf16(8 * 512)
        S = A.f32(512)
        S2 = A.f32(512)
        Sbr = [A.bf16(512), A.bf16(512)]
        Wout = A.bf16(DC * D)
        qT = A.bf16(NH * T)
        sgs = [A.bf16(NH * T), A.bf16(NH * T)]
        thg = [A.f32(T), A.f32(T)]
        uT = A.bf16(NH * T)
        vln = A.bf16(4 * 512)
        Bc = A.f32(512)
        WmT = A.bf16(512)
        stats = A.f32(24)
        mv = A.f32(8)
        rs4 = A.f32(4)
        ocat = A.bf16(DC * T)
        bsp = ocat.bitcast(F32)[:, 0:512]
        b1k = ["B1_%d" % h for h in range(NH)]
        b2k = ["B2_%d" % h for h in range(NH)]
        b3k = ["B3_%d" % h for h in range(NH)]
        hkeys = ["hb%d" % c for c in range(DC)]
        lcol = lambda t, hd: t[:, l * 4 + hd:l * 4 + hd + 1]

        pre = ("mix", l) in self.conv_done
        wq = "sp" if pre else "pool"
        win_src = self.wsc_in if pre else self.win[l]
        wout_src = self.wsc_out if pre else self.wout[l]
        wrd = ["wscM"] if pre else []
        for c in range(DC):
            P.op(wq, lambda e, c=c: e.dma_start(out=Win[:, c * WIN:(c + 1) * WIN], in_=win_src[c * 128:(c + 1) * 128, :]),
                 reads=wrd, writes=[P.subkey("W")], dma=("ldwh%d" if pre else "ldw%d") % P.slot)
        for c in range(DC):
            P.op(wq, lambda e, c=c: e.dma_start(out=Wout[:, c * D:(c + 1) * D], in_=wout_src[c * 128:(c + 1) * 128, :]),
                 reads=wrd, writes=[P.subkey("Wo")], dma=("ldwho%d" if pre else "ldwo%d") % P.slot)
        P.op("sp", lambda e: e.dma_start(out=bsp, in_=self.bsp_d[l]), writes=["bsp", "ocat0", "ocat1"], dma="ld_bsp")
        P.op("sp", lambda e: e.dma_start(out=B1[:, 0:512], in_=self.wst_d[l]), writes=[b1k[0]], dma="ld_wst")
        P.op("dve", lambda e: e.tensor_tensor(
            WmT.rearrange("p (g t) -> p g t", g=4), B1[:, 0:512].rearrange("p (g t) -> p g t", g=4),
            self.tri128.unsqueeze(1).broadcast_to([128, 4, 128]), ALU.mult),
            reads=[b1k[0], "cst"], writes=["WmT"])
        for g in range(4):
            P.op("pe", lambda e, g=g: e.matmul(ps[1][:, g * 128:(g + 1) * 128], self.ones_b, WmT[:, g * 128:(g + 1) * 128],
                                                start=True, stop=True),
                 reads=["WmT", "cst"], writes=["pb1"])
            P.op("dve", lambda e, g=g: e.scalar_tensor_tensor(
                out=Bc[:, g * 128:(g + 1) * 128], in0=ps[1][:, g * 128:(g + 1) * 128],
                scalar=self.mixp[:, 24 + l * 4 + g:24 + l * 4 + g + 1], in1=bsp[:, g * 128:(g + 1) * 128],
                op0=ALU.mult, op1=ALU.add),
                reads=["pb1", "bsp", "ocat0", "ocat1", "mixp"], writes=["Bc"])
        sk0 = ["Sp0_%d" % h for h in range(NH)]
        if self.fused:
            P.op("sp", lambda e: e.dma_start(out=S, in_=self.cc_out[l][0:128, :]), reads=["ccout%d" % l], writes=sk0, dma="ld_S")
            P.op("dve", lambda e: e.tensor_scalar(S, S, self.role[:, 0:1], None, ALU.mult), reads=sk0 + ["role"], writes=sk0)
        else:
            P.op("sp", lambda e: e.dma_start(out=S, in_=self.s_in), writes=sk0, dma="ld_S")
        P.op("act", lambda e: e.activation(Sbr[1], S, AF.Copy), reads=sk0, writes=["Sbr1"])
        self.schedule_conversion(self.cur_pi)

        pj = [0]

        def pjbank():
            k = 1 + pj[0] % 2
            pj[0] += 1
            return ps[k], "pb%d" % k

        def proj_fm(col0):
            bank, bk = pjbank()
            P.op("pe", lambda e: _mm_group(e, bank[:, :], [
                (Win[:, c * WIN + col0:c * WIN + col0 + 128], self.hb[:, c * T:(c + 1) * T]) for c in range(DC)]),
                reads=hkeys + ["W"], writes=[bk])
            return bank, bk

        def front(i):
            sg = sgs[i % 2]
            P.op("sp", lambda e: e.dma_start(out=kT, in_=self.sc_kT[i]), writes=["kT%d" % h for h in range(NH)], dma="ld_kT")
            P.op("sp", lambda e: e.dma_start(out=kdtm[0:64, :], in_=self.sc_kd[i]), writes=["kdtm%d" % c for c in range(8)], dma="ld_kd")
            P.op("sp", lambda e: e.dma_start(out=vtm[0:64, :], in_=self.sc_v[i]), writes=["vtm%d" % c for c in range(8)], dma="ld_v")
            P.op("sp", lambda e: e.dma_start(out=B1, in_=self.sc_ea[i]), writes=b1k, dma="ld_ea")
            P.op("sp", lambda e: e.dma_start(out=Ecol, in_=self.sc_E[i]), writes=["Ecol%d" % h for h in range(NH)], dma="ld_E")
            for hd in range(NH):
                sl = slice(hd * T, (hd + 1) * T)
                th = thg[hd % 2]
                bank, bk = proj_fm(1536 + hd * 128)
                P.op("act", lambda e, th=th, bank=bank: e.activation(th, bank[:, :], AF.Tanh, scale=0.5),
                     reads=[bk], writes=["thg%d" % (hd % 2)])
                P.op("dve", lambda e, sl=sl, th=th, bank=bank, sg=sg: e.scalar_tensor_tensor(
                    out=sg[:, sl], in0=th, scalar=1.0, in1=bank[:, :], op0=ALU.add, op1=ALU.mult),
                    reads=[bk, "thg%d" % (hd % 2)], writes=["sg%d_%d" % (i % 2, hd)])
            for hd in range(NH):
                sl = slice(hd * T, (hd + 1) * T)
                bank, bk = proj_fm(hd * 128)
                P.op("dve", lambda e, sl=sl, bank=bank: e.tensor_tensor(qT[:, sl], bank[:, :], B1[:, sl], ALU.mult),
                     reads=[bk, b1k[hd]], writes=["qT%d" % hd])
            for g in range(4):
                bank, bk = proj_fm(2048 + g * 128)
                P.op("act", lambda e, g=g, bank=bank: e.activation(uT[:, g * T:(g + 1) * T], bank[:, :], AF.Gelu_apprx_tanh),
                     reads=[bk], writes=["uT%d" % g])
            for s in range(4):
                sl = slice(s * T, (s + 1) * T)
                bank, bk = pjbank()
                P.op("pe", lambda e, s=s, bank=bank: _mm_group(e, bank[:, :], [
                    (self.hb[:, c * T + s * 128:c * T + (s + 1) * 128], Win[:, c * WIN + 2560:c * WIN + 3072])
                    for c in range(DC)]),
                    reads=hkeys + ["W"], writes=[bk])
                P.op("act", lambda e, sl=sl, bank=bank: e.activation(B1[:, sl], bank[:, :], AF.Gelu_apprx_tanh),
                     reads=[bk], writes=[b1k[s]])
                P.op("dve", lambda e, s=s, sl=sl: e.bn_stats(stats[:, s * 6:(s + 1) * 6], B1[:, sl]),
                     reads=[b1k[s]], writes=["stats%d" % s])
                P.op("dve", lambda e, s=s: e.bn_aggr(mv[:, s * 2:(s + 1) * 2], stats[:, s * 6:(s + 1) * 6]),
                     reads=["stats%d" % s], writes=["mv"])
            P.op("dve", lambda e: e.tensor_scalar(rs4, mv[:, 1:8:2], float(EPS), None, ALU.add), reads=["mv"], writes=["rs4"])
            P.op("act", lambda e: e.activation(rs4, rs4, AF.Ln), reads=["rs4"], writes=["rs4"])
            P.op("act", lambda e: e.activation(rs4, rs4, AF.Exp, scale=-0.5), reads=["rs4"], writes=["rs4"])
            for s in range(4):
                sl = slice(s * T, (s + 1) * T)
                P.op("dve", lambda e, s=s, sl=sl: e.tensor_scalar(vln[:, sl], B1[:, sl], mv[:, 2 * s:2 * s + 1], rs4[:, s:s + 1],
                                                                 ALU.subtract, ALU.mult),
                     reads=[b1k[s], "mv", "rs4"], writes=["vln%d" % s])

        def post_b(g):
            bank, bk = pjbank()

            def spf(e, g=g, bank=bank):
                ins = None
                for s in range(4):
                    ins = e.matmul(bank[:, s * 128:(s + 1) * 128], vln[:, s * 512 + g * 128:s * 512 + (g + 1) * 128],
                                   WmT[:, g * 128:(g + 1) * 128], start=True, stop=True)
                return ins
            P.op("pe", spf, reads=["vln%d" % s for s in range(4)] + ["WmT"], writes=[bk])
            gl = slice(g * T, (g + 1) * T)
            P.op("dve", lambda e, g=g, gl=gl, bank=bank: e.scalar_tensor_tensor(
                out=B1[:, gl].rearrange("p (s t) -> p s t", s=4), in0=bank[:, :].rearrange("p (s t) -> p s t", s=4),
                scalar=self.mixp[:, 16 + l * 4 + g:16 + l * 4 + g + 1],
                in1=Bc[:, g * 128:(g + 1) * 128].unsqueeze(1).broadcast_to([128, 4, 128]),
                op0=ALU.mult, op1=ALU.add),
                reads=[bk, "Bc", "mixp"], writes=[b1k[g]])
            P.op("dve", lambda e, g=g, gl=gl: e.tensor_tensor(ocat[:, (4 + g) * T:(5 + g) * T], B1[:, gl], uT[:, gl], ALU.mult),
                 reads=[b1k[g], "uT%d" % g], writes=["ocat%d" % (4 + g)])

        PTv = B2.bitcast(BF16)
        PT = [PTv[0:64, cc * 256:(cc + 1) * 256] for cc in range(8)]
        scb = [(ps[4], "pb4"), (ps[5], "pb5")]
        dsb = [(ps[7], "pb7"), (ps[3], "pb3")]
        obb = [(ps[6], "pb6"), (ps[0], "pb0")]
        Sp = [S, S2]
        csl = lambda hd, cc: slice(hd * T + cc * CH, hd * T + (cc + 1) * CH)

        def recur(i):
            for cc in range(8):
                bank, bkey = scb[cc % 2]

                def scf(e, cc=cc, bank=bank):
                    ins = None
                    for hd in range(NH):
                        ins = e.matmul(bank[0:64, hd * CH:(hd + 1) * CH], kT[:, csl(hd, cc)], qT[:, csl(hd, cc)], start=True, stop=True)
                    return ins
                P.op("pe", scf, reads=["kT%d" % h for h in range(NH)] + ["qT%d" % h for h in range(NH)], writes=[bkey])
                P.op("dve", lambda e, cc=cc, bank=bank: e.tensor_tensor(PT[cc], bank[0:64, 0:256], self.tri64, ALU.mult),
                     reads=[bkey, "cst"], writes=["PT%d" % cc, b2k[0], b2k[1]])

            def emit_ds(cc):
                bank, bkey = dsb[cc % 2]

                def dsf(e, cc=cc, bank=bank):
                    ins = None
                    for hd in range(NH):
                        ins = e.matmul(bank[:, hd * 128:(hd + 1) * 128], kdtm[0:64, cc * 512 + hd * 128:cc * 512 + (hd + 1) * 128],
                                       vtm[0:64, cc * 512 + hd * 128:cc * 512 + (hd + 1) * 128], start=True, stop=True)
                    return ins
                P.op("pe", dsf, reads=["kdtm%d" % cc, "vtm%d" % cc], writes=[bkey])

            emit_ds(0)
            emit_ds(1)
            for cc in range(8):
                OB, okey = obb[cc % 2]
                sbp = Sbr[(cc - 1) % 2]

                def of(e, cc=cc, OB=OB, sbp=sbp):
                    ins = None
                    for hd in range(NH):
                        o = OB[:, hd * CH:(hd + 1) * CH]
                        e.matmul(o, sbp[:, hd * 128:(hd + 1) * 128], qT[:, csl(hd, cc)], start=True, stop=False)
                        ins = e.matmul(o, vtm[0:64, cc * 512 + hd * 128:cc * 512 + (hd + 1) * 128],
                                       PT[cc][:, hd * CH:(hd + 1) * CH], start=False, stop=True)
                    return ins
                P.op("pe", of, reads=["Sbr%d" % ((cc - 1) % 2), "vtm%d" % cc, "PT%d" % cc] + ["qT%d" % h for h in range(NH)],
                     writes=[okey])
                P.op("act", lambda e, cc=cc, OB=OB: e.activation(
                    B3.rearrange("p (h t) -> p h t", h=NH)[:, :, cc * CH:(cc + 1) * CH],
                    OB[:, 0:256].rearrange("p (h t) -> p h t", h=NH), AF.Copy),
                    reads=[okey], writes=b3k)
                bank, bkey = dsb[cc % 2]
                src, dst = Sp[cc % 2], Sp[(cc + 1) % 2]
                for hd in range(NH):
                    P.op("dve", lambda e, hd=hd, cc=cc, bank=bank, src=src, dst=dst: e.scalar_tensor_tensor(
                        out=dst[:, hd * 128:(hd + 1) * 128], in0=src[:, hd * 128:(hd + 1) * 128],
                        scalar=Ecol[:, hd * 8 + cc:hd * 8 + cc + 1], in1=bank[:, hd * 128:(hd + 1) * 128],
                        op0=ALU.mult, op1=ALU.add),
                        reads=["Sp%d_%d" % (cc % 2, hd), bkey, "Ecol%d" % hd], writes=["Sp%d_%d" % ((cc + 1) % 2, hd)])
                P.op("act", lambda e, cc=cc, dst=dst: e.activation(Sbr[cc % 2], dst, AF.Copy),
                     reads=["Sp%d_%d" % ((cc + 1) % 2, h) for h in range(NH)], writes=["Sbr%d" % (cc % 2)])
                if cc + 2 < 8:
                    emit_ds(cc + 2)
                if cc % 2 == 1:
                    post_b(cc // 2)

        def back(i):
            b = i % 2
            xt = self.xt[b]
            xk = "xt%d" % b
            sg = sgs[i % 2]
            HS = [(hd, slice(hd * T, (hd + 1) * T)) for hd in range(NH)]
            ssb = [(ps[0], "pb0"), (ps[4], "pb4"), (ps[7], "pb7"), (ps[6], "pb6")]
            for hd, sl in HS:
                P.op("act", lambda e, sl=sl: e.activation(osq[:, sl], B3[:, sl], AF.Square), reads=[b3k[hd]], writes=["osq%d" % hd])
            for hd, sl in HS:
                P.op("pe", lambda e, sl=sl, hd=hd: e.matmul(ssb[hd][0][:, :], self.ones_b, osq[:, sl], start=True, stop=True),
                     reads=["osq%d" % hd, "cst"], writes=[ssb[hd][1]])
            for hd, sl in HS:
                P.op("dve", lambda e, sl=sl, hd=hd: e.tensor_scalar(B2[:, sl], ssb[hd][0][:, :], float(128 * EPS), None, ALU.add),
                     reads=[ssb[hd][1]], writes=[b2k[hd]])
            for hd, sl in HS:
                P.op("act", lambda e, sl=sl: e.activation(B2[:, sl], B2[:, sl], AF.Ln), reads=[b2k[hd]], writes=[b2k[hd]])
            for hd, sl in HS:
                P.op("act", lambda e, sl=sl: e.activation(B2[:, sl], B2[:, sl], AF.Exp, scale=-0.5), reads=[b2k[hd]], writes=[b2k[hd]])
            for hd, sl in HS:
                P.op("dve", lambda e, sl=sl: e.tensor_tensor(B2[:, sl], B3[:, sl], B2[:, sl], ALU.mult),
                     reads=[b3k[hd], b2k[hd]], writes=[b2k[hd]])
                P.op("dve", lambda e, sl=sl, hd=hd, sg=sg: e.scalar_tensor_tensor(
                    out=ocat[:, sl], in0=B2[:, sl], scalar=lcol(self.gnp, hd), in1=sg[:, sl], op0=ALU.mult, op1=ALU.mult),
                    reads=[b2k[hd], "sg%d_%d" % (i % 2, hd), "lbq"], writes=["ocat%d" % hd])
            okeys = ["ocat%d" % c for c in range(DC)]
            for dc in range(DC):
                bank, bk = pjbank()
                P.op("pe", lambda e, dc=dc, bank=bank: _mm_group(e, bank[:, :], [
                    (Wout[:, c * D + dc * 128:c * D + (dc + 1) * 128], ocat[:, c * T:(c + 1) * T]) for c in range(DC)]),
                    reads=okeys + ["Wo"], writes=[bk])
                P.op("dve", lambda e, dc=dc, bank=bank, xt=xt: e.tensor_tensor(
                    xt[:, dc * T:(dc + 1) * T], xt[:, dc * T:(dc + 1) * T], bank[:, :], ALU.add),
                    reads=[bk, xk], writes=[xk])
            self.store_x(i)

        self.load_x(0, False)
        if NT > 1:
            self.load_x(1, False)
        self.norm_stage(0, gidx)
        front(0)
        if NT > 1:
            self.norm_stage(1, gidx)
        for i in range(NT):
            recur(i)
            if i + 1 < NT:
                front(i + 1)
            back(i)
            if i + 2 < NT:
                self.load_x(i + 2, False)
                self.norm_stage(i + 2, gidx)

    def mix_phase(self, l, state_only=False):
        A, P, ps = self.A, self.P, self.ps
        full = not state_only
        gidx = l * 3 + 1
        WIN = 3072
        Win = A.bf16(DC * WIN)
        B1 = A.f32(NH * T)
        B2 = A.f32(NH * T)
        B3 = A.f32(NH * T)
        Ecol = A.f32(NH * 8)
        kT = A.bf16(NH * T)
        osq = A.bf16(NH * T)
        kdT = osq
        vtm = A.bf16(8 * 512)
        kdtm = A.bf16(8 * 512)
        S = A.f32(512)
        Sb = A.bf16(512)
        S2 = A.f32(512)
        Sb2 = A.bf16(512)
        if full:
            Wout = A.bf16(DC * D)
            qT = A.bf16(NH * T)
            sg = A.bf16(NH * T)
            uT = A.bf16(NH * T)
            vln = A.bf16(4 * 512)
            Bc = A.f32(512)
            WmT = A.bf16(512)
            stats = A.f32(24)
            mv = A.f32(8)
            rs4 = A.f32(4)
            ocat = A.bf16(DC * T)
            bsp = ocat.bitcast(F32)[:, 0:512]
            PTb = [A.bf16(256), A.bf16(256)]
        b1k = ["B1_%d" % h for h in range(NH)]
        b2k = ["B2_%d" % h for h in range(NH)]
        b3k = ["B3_%d" % h for h in range(NH)]
        hkeys = ["hb%d" % c for c in range(DC)]
        lcol = lambda t, hd: t[:, l * 4 + hd:l * 4 + hd + 1]
        OBs = [ps[6], ps[5]]
        obk = ["pb6", "pb5"]
        trbs = [ps[5][:, :].bitcast(BF16), ps[4][:, :].bitcast(BF16)]
        trk = ["pb5", "pb4"]

        pre = ("mix", l) in self.conv_done
        wq = "sp" if pre else "pool"
        win_src = self.wsc_in if pre else self.win[l]
        wout_src = self.wsc_out if pre else self.wout[l]
        wrd = ["wscM"] if pre else []
        if full:
            for c in range(DC):
                P.op(wq, lambda e, c=c: e.dma_start(out=Win[:, c * WIN:(c + 1) * WIN],
                                                    in_=win_src[c * 128:(c + 1) * 128, :]),
                     reads=wrd, writes=[P.subkey("W")], dma=("ldwh%d" if pre else "ldw%d") % P.slot)
            for c in range(DC):
                P.op(wq, lambda e, c=c: e.dma_start(out=Wout[:, c * D:(c + 1) * D],
                                                    in_=wout_src[c * 128:(c + 1) * 128, :]),
                     reads=wrd, writes=[P.subkey("W")], dma=("ldwh%d" if pre else "ldw%d") % P.slot)
            P.op("sp", lambda e: e.dma_start(out=bsp, in_=self.bsp_d[l]), writes=["bsp", "ocat0", "ocat1"], dma="ld_bsp")
            P.op("sp", lambda e: e.dma_start(out=B1[:, 0:512], in_=self.wst_d[l]), writes=[b1k[0]], dma="ld_wst")
            P.op("dve", lambda e: e.tensor_tensor(
                WmT.rearrange("p (g t) -> p g t", g=4), B1[:, 0:512].rearrange("p (g t) -> p g t", g=4),
                self.tri128.unsqueeze(1).broadcast_to([128, 4, 128]), ALU.mult),
                reads=[b1k[0], "cst"], writes=["WmT"])
            for g in range(4):
                P.op("pe", lambda e, g=g: e.matmul(ps[1][:, g * 128:(g + 1) * 128], self.ones_b, WmT[:, g * 128:(g + 1) * 128],
                                                    start=True, stop=True),
                     reads=["WmT", "cst"], writes=["pb1"])
                P.op("dve", lambda e, g=g: e.scalar_tensor_tensor(
                    out=Bc[:, g * 128:(g + 1) * 128], in0=ps[1][:, g * 128:(g + 1) * 128],
                    scalar=self.mixp[:, 24 + l * 4 + g:24 + l * 4 + g + 1], in1=bsp[:, g * 128:(g + 1) * 128],
                    op0=ALU.mult, op1=ALU.add),
                    reads=["pb1", "bsp", "ocat0", "ocat1", "mixp"], writes=["Bc"])
            if self.fused:
                P.op("sp", lambda e: e.dma_start(out=S, in_=self.cc_out[l][0:128, :]), reads=["ccout%d" % l],
                     writes=["Sp0_%d" % h for h in range(NH)], dma="ld_S")
                P.op("dve", lambda e: e.tensor_scalar(S, S, self.role[:, 0:1], None, ALU.mult),
                     reads=["Sp0_%d" % h for h in range(NH)] + ["role"], writes=["Sp0_%d" % h for h in range(NH)])
            else:
                P.op("sp", lambda e: e.dma_start(out=S, in_=self.s_in), writes=["Sp0_%d" % h for h in range(NH)], dma="ld_S")
        else:
            for c in range(DC):
                P.op(wq, lambda e, c=c: e.dma_start(out=Win[:, c * WIN + 512:c * WIN + 1536],
                                                    in_=win_src[c * 128:(c + 1) * 128, 512:1536]),
                     reads=wrd, writes=[P.subkey("W")], dma=("ldwh%d" if pre else "ldw%d") % P.slot)
            P.op("dve", lambda e: e.memset(S, 0.0), writes=["S%d" % h for h in range(NH)])
        skeys = ["Sp0_%d" % h for h in range(NH)]
        Sbr = [Sb, Sb2]
        P.op("act", lambda e: e.activation(Sbr[1], S, AF.Copy), reads=skeys, writes=["Sbr1"])
        self.schedule_conversion(self.cur_pi)

        pj = [0]

        def pjbank():
            k = 1 + pj[0] % 2
            pj[0] += 1
            return ps[k], "pb%d" % k

        def proj_fm(col0):
            bank, bk = pjbank()
            P.op("pe", lambda e: _mm_group(e, bank[:, :], [
                (Win[:, c * WIN + col0:c * WIN + col0 + 128], self.hb[:, c * T:(c + 1) * T]) for c in range(DC)]),
                reads=hkeys + ["W"], writes=[bk])
            return bank, bk

        self.load_x(0, False)
        if NT > 1:
            self.load_x(1, False)
        self.norm_stage(0, gidx)
        for i in range(NT):
            b = i % 2
            xt = self.xt[b]
            xk = "xt%d" % b
            if state_only:
                for hd in range(NH):
                    sl = slice(hd * T, (hd + 1) * T)
                    bank, bk = proj_fm(512 + hd * 128)
                    P.op("act", lambda e, sl=sl, bank=bank: e.activation(B1[:, sl], bank[:, :], AF.Tanh, scale=0.5),
                         reads=[bk], writes=[b1k[hd]])
                    P.op("dve", lambda e, sl=sl, hd=hd: e.tensor_scalar(B2[:, sl], B1[:, sl], lcol(self.nhoml, hd),
                                                                       lcol(self.homl, hd), ALU.mult, ALU.add),
                         reads=[b1k[hd], "lbq"], writes=[b2k[hd]])
                for cc in range(8):
                    bank, bk = pjbank()
                    P.op("pe", lambda e, cc=cc, bank=bank: _mm_group(e, bank[0:64, :], [
                        (self.hb[:, c * T + cc * CH:c * T + (cc + 1) * CH], Win[:, c * WIN + 1024:c * WIN + 1536])
                        for c in range(DC)]),
                        reads=hkeys + ["W"], writes=[bk])
                    P.op("act", lambda e, cc=cc, bank=bank: e.activation(vtm[0:64, cc * 512:(cc + 1) * 512], bank[0:64, :], AF.Copy),
                         reads=[bk], writes=["vtm%d" % cc])
                for hd in range(NH):
                    sl = slice(hd * T, (hd + 1) * T)
                    P.op("act", lambda e, sl=sl, hd=hd: e.activation(B1[:, sl], B1[:, sl], AF.Ln, bias=lcol(self.lbh, hd),
                                                                    scale=lcol(self.homl, hd)),
                         reads=[b1k[hd], "lbq"], writes=[b1k[hd]])
                    P.op("dve", lambda e, sl=sl: e.tensor_scalar_max(B1[:, sl], B1[:, sl], float(np.log(F_MIN))),
                         reads=[b1k[hd]], writes=[b1k[hd]])
                    P.op("dve", lambda e, sl=sl: e.tensor_tensor_scan(B3[:, sl], self.rmask, B1[:, sl], 0.0, ALU.mult, ALU.add),
                         reads=[b1k[hd], "cst"], writes=[b3k[hd]])
                    P.op("act", lambda e, sl=sl: e.activation(B1[:, sl], B3[:, sl], AF.Exp), reads=[b3k[hd]], writes=[b1k[hd]])
                    P.op("act", lambda e, sl=sl: e.activation(B3[:, sl], B3[:, sl], AF.Exp, scale=-1.0),
                         reads=[b3k[hd]], writes=[b3k[hd]])
                    P.op("dve", lambda e, hd=hd: e.tensor_copy(Ecol[:, hd * 8:(hd + 1) * 8], B1[:, hd * T + CH - 1:(hd + 1) * T:CH]),
                         reads=[b1k[hd]], writes=["Ecol%d" % hd])
                    P.op("dve", lambda e, sl=sl: e.tensor_tensor(B2[:, sl], B2[:, sl], B3[:, sl], ALU.mult),
                         reads=[b2k[hd], b3k[hd]], writes=[b2k[hd]])
                    P.op("act", lambda e, sl=sl: e.activation(kT[:, sl], B2[:, sl], AF.Copy), reads=[b2k[hd]], writes=["kT%d" % hd])
                    P.op("dve", lambda e, hd=hd, sl=sl: e.tensor_tensor(
                        kdT[:, sl].rearrange("p (c t) -> p c t", t=CH), B2[:, sl].rearrange("p (c t) -> p c t", t=CH),
                        Ecol[:, hd * 8:(hd + 1) * 8].unsqueeze(2).broadcast_to([128, 8, CH]), ALU.mult),
                        reads=[b2k[hd], "Ecol%d" % hd], writes=["osq%d" % hd])
                for cc in range(8):
                    half = cc % 2

                    trb = trbs[half]

                    def trf(e, cc=cc, trb=trb):
                        ins = None
                        for hd in range(NH):
                            ins = e.transpose(trb[0:64, hd * 128:(hd + 1) * 128],
                                              kdT[:, hd * T + cc * CH:hd * T + (cc + 1) * CH], self.ident_b)
                        return ins
                    P.op("pe", trf, reads=["osq%d" % hd for hd in range(NH)] + ["cst"], writes=[trk[half]])
                    if cc % 2 == 0:
                        P.op("act", lambda e, cc=cc, trb=trb: e.activation(
                            kdtm[0:64, cc * 512:(cc + 1) * 512], trb[0:64, 0:512], AF.Copy),
                            reads=[trk[half]], writes=["kdtm%d" % cc])
                    else:
                        P.op("dve", lambda e, cc=cc, trb=trb: e.tensor_copy(
                            kdtm[0:64, cc * 512:(cc + 1) * 512], trb[0:64, 0:512]),
                            reads=[trk[half]], writes=["kdtm%d" % cc])
                P.op("sp", lambda e, i=i: e.dma_start(out=self.sc_kT[i], in_=kT), reads=["kT%d" % h for h in range(NH)], dma="st_kT")
                P.op("sp", lambda e, i=i: e.dma_start(out=self.sc_kd[i], in_=kdtm[0:64, :]), reads=["kdtm%d" % c for c in range(8)], dma="st_kd")
                P.op("sp", lambda e, i=i: e.dma_start(out=self.sc_v[i], in_=vtm[0:64, :]), reads=["vtm%d" % c for c in range(8)], dma="st_v")
                P.op("sp", lambda e, i=i: e.dma_start(out=self.sc_ea[i], in_=B1), reads=b1k, dma="st_ea")
                P.op("sp", lambda e, i=i: e.dma_start(out=self.sc_E[i], in_=Ecol), reads=["Ecol%d" % h for h in range(NH)], dma="st_E")
            else:
                P.op("sp", lambda e, i=i: e.dma_start(out=kT, in_=self.sc_kT[i]), writes=["kT%d" % h for h in range(NH)], dma="ld_kT")
                P.op("sp", lambda e, i=i: e.dma_start(out=kdtm[0:64, :], in_=self.sc_kd[i]), writes=["kdtm%d" % c for c in range(8)], dma="ld_kd")
                P.op("sp", lambda e, i=i: e.dma_start(out=vtm[0:64, :], in_=self.sc_v[i]), writes=["vtm%d" % c for c in range(8)], dma="ld_v")
                P.op("sp", lambda e, i=i: e.dma_start(out=B1, in_=self.sc_ea[i]), writes=b1k, dma="ld_ea")
                P.op("sp", lambda e, i=i: e.dma_start(out=Ecol, in_=self.sc_E[i]), writes=["Ecol%d" % h for h in range(NH)], dma="ld_E")
            if full:
                for hd in range(NH):
                    sl = slice(hd * T, (hd + 1) * T)
                    bank, bk = proj_fm(1536 + hd * 128)
                    P.op("act", lambda e, sl=sl, bank=bank: e.activation(B3[:, sl], bank[:, :], AF.Tanh, scale=0.5),
                         reads=[bk], writes=[b3k[hd]])
                    P.op("dve", lambda e, sl=sl, bank=bank: e.scalar_tensor_tensor(
                        out=sg[:, sl], in0=B3[:, sl], scalar=1.0, in1=bank[:, :], op0=ALU.add, op1=ALU.mult),
                        reads=[bk, b3k[hd]], writes=["sg%d" % hd])
            if full:
                for hd in range(NH):
                    sl = slice(hd * T, (hd + 1) * T)
                    bank, bk = proj_fm(hd * 128)
                    P.op("dve", lambda e, sl=sl, bank=bank: e.tensor_tensor(qT[:, sl], bank[:, :], B1[:, sl], ALU.mult),
                         reads=[bk, b1k[hd]], writes=["qT%d" % hd])
            if full:
                for g in range(4):
                    bank, bk = proj_fm(2048 + g * 128)
                    P.op("act", lambda e, g=g, bank=bank: e.activation(uT[:, g * T:(g + 1) * T], bank[:, :], AF.Gelu_apprx_tanh),
                         reads=[bk], writes=["uT%d" % g])
                for s in range(4):
                    sl = slice(s * T, (s + 1) * T)
                    bank, bk = pjbank()
                    P.op("pe", lambda e, s=s, bank=bank: _mm_group(e, bank[:, :], [
                        (self.hb[:, c * T + s * 128:c * T + (s + 1) * 128], Win[:, c * WIN + 2560:c * WIN + 3072])
                        for c in range(DC)]),
                        reads=hkeys + ["W"], writes=[bk])
                    P.op("act", lambda e, sl=sl, bank=bank: e.activation(B1[:, sl], bank[:, :], AF.Gelu_apprx_tanh),
                         reads=[bk], writes=[b1k[s]])
                    P.op("dve", lambda e, s=s, sl=sl: e.bn_stats(stats[:, s * 6:(s + 1) * 6], B1[:, sl]),
                         reads=[b1k[s]], writes=["stats%d" % s])
                    P.op("dve", lambda e, s=s: e.bn_aggr(mv[:, s * 2:(s + 1) * 2], stats[:, s * 6:(s + 1) * 6]),
                         reads=["stats%d" % s], writes=["mv"])
                P.op("dve", lambda e: e.tensor_scalar(rs4, mv[:, 1:8:2], float(EPS), None, ALU.add), reads=["mv"], writes=["rs4"])
                P.op("act", lambda e: e.activation(rs4, rs4, AF.Ln), reads=["rs4"], writes=["rs4"])
                P.op("act", lambda e: e.activation(rs4, rs4, AF.Exp, scale=-0.5), reads=["rs4"], writes=["rs4"])
                if i + 1 < NT:
                    self.norm_stage(i + 1, gidx)
                for s in range(4):
                    sl = slice(s * T, (s + 1) * T)
                    P.op("dve", lambda e, s=s, sl=sl: e.tensor_scalar(vln[:, sl], B1[:, sl], mv[:, 2 * s:2 * s + 1], rs4[:, s:s + 1],
                                                                     ALU.subtract, ALU.mult),
                         reads=[b1k[s], "mv", "rs4"], writes=["vln%d" % s])
            if state_only and i + 1 < NT:
                self.norm_stage(i + 1, gidx)

            def post_b(g):
                bank, bk = pjbank()

                def spf(e, g=g, bank=bank):
                    ins = None
                    for s in range(4):
                        ins = e.matmul(bank[:, s * 128:(s + 1) * 128], vln[:, s * 512 + g * 128:s * 512 + (g + 1) * 128],
                                       WmT[:, g * 128:(g + 1) * 128], start=True, stop=True)
                    return ins
                P.op("pe", spf, reads=["vln%d" % s for s in range(4)] + ["WmT"], writes=[bk])
                gl = slice(g * T, (g + 1) * T)
                P.op("dve", lambda e, g=g, gl=gl, bank=bank: e.scalar_tensor_tensor(
                    out=B1[:, gl].rearrange("p (s t) -> p s t", s=4), in0=bank[:, :].rearrange("p (s t) -> p s t", s=4),
                    scalar=self.mixp[:, 16 + l * 4 + g:16 + l * 4 + g + 1],
                    in1=Bc[:, g * 128:(g + 1) * 128].unsqueeze(1).broadcast_to([128, 4, 128]),
                    op0=ALU.mult, op1=ALU.add),
                    reads=[bk, "Bc", "mixp"], writes=[b1k[g]])
                P.op("dve", lambda e, g=g, gl=gl: e.tensor_tensor(ocat[:, (4 + g) * T:(5 + g) * T], B1[:, gl], uT[:, gl], ALU.mult),
                     reads=[b1k[g], "uT%d" % g], writes=["ocat%d" % (4 + g)])

            PTv = B2.bitcast(BF16)
            PT = [PTv[0:64, cc * 256:(cc + 1) * 256] for cc in range(8)]
            scb = [(ps[4], "pb4"), (ps[5], "pb5")]
            dsb = [(ps[7], "pb7"), (ps[3], "pb3")]
            obb = [(ps[6], "pb6"), (ps[0], "pb0")]
            Sp = [S, S2]
            csl = lambda hd, cc: slice(hd * T + cc * CH, hd * T + (cc + 1) * CH)
            for cc in range(8):
                bank, bkey = scb[cc % 2]

                def scf(e, cc=cc, bank=bank):
                    ins = None
                    for hd in range(NH):
                        ins = e.matmul(bank[0:64, hd * CH:(hd + 1) * CH], kT[:, csl(hd, cc)], qT[:, csl(hd, cc)], start=True, stop=True)
                    return ins
                P.op("pe", scf, reads=["kT%d" % h for h in range(NH)] + ["qT%d" % h for h in range(NH)], writes=[bkey])
                P.op("dve", lambda e, cc=cc, bank=bank: e.tensor_tensor(PT[cc], bank[0:64, 0:256], self.tri64, ALU.mult),
                     reads=[bkey, "cst"], writes=["PT%d" % cc, b2k[0], b2k[1]])

            def emit_ds(cc):
                bank, bkey = dsb[cc % 2]

                def dsf(e, cc=cc, bank=bank):
                    ins = None
                    for hd in range(NH):
                        ins = e.matmul(bank[:, hd * 128:(hd + 1) * 128], kdtm[0:64, cc * 512 + hd * 128:cc * 512 + (hd + 1) * 128],
                                       vtm[0:64, cc * 512 + hd * 128:cc * 512 + (hd + 1) * 128], start=True, stop=True)
                    return ins
                P.op("pe", dsf, reads=["kdtm%d" % cc, "vtm%d" % cc], writes=[bkey])

            emit_ds(0)
            emit_ds(1)
            for cc in range(8):
                OB, okey = obb[cc % 2]
                sbp = Sbr[(cc - 1) % 2]

                def of(e, cc=cc, OB=OB, sbp=sbp):
                    ins = None
                    for hd in range(NH):
                        o = OB[:, hd * CH:(hd + 1) * CH]
                        e.matmul(o, sbp[:, hd * 128:(hd + 1) * 128], qT[:, csl(hd, cc)], start=True, stop=False)
                        ins = e.matmul(o, vtm[0:64, cc * 512 + hd * 128:cc * 512 + (hd + 1) * 128],
                                       PT[cc][:, hd * CH:(hd + 1) * CH], start=False, stop=True)
                    return ins
                P.op("pe", of, reads=["Sbr%d" % ((cc - 1) % 2), "vtm%d" % cc, "PT%d" % cc] + ["qT%d" % h for h in range(NH)],
                     writes=[okey])
                P.op("act", lambda e, cc=cc, OB=OB: e.activation(
                    B3.rearrange("p (h t) -> p h t", h=NH)[:, :, cc * CH:(cc + 1) * CH],
                    OB[:, 0:256].rearrange("p (h t) -> p h t", h=NH), AF.Copy),
                    reads=[okey], writes=b3k)
                bank, bkey = dsb[cc % 2]
                src, dst = Sp[cc % 2], Sp[(cc + 1) % 2]
                for hd in range(NH):
                    P.op("dve", lambda e, hd=hd, cc=cc, bank=bank, src=src, dst=dst: e.scalar_tensor_tensor(
                        out=dst[:, hd * 128:(hd + 1) * 128], in0=src[:, hd * 128:(hd + 1) * 128],
                        scalar=Ecol[:, hd * 8 + cc:hd * 8 + cc + 1], in1=bank[:, hd * 128:(hd + 1) * 128],
                        op0=ALU.mult, op1=ALU.add),
                        reads=["Sp%d_%d" % (cc % 2, hd), bkey, "Ecol%d" % hd], writes=["Sp%d_%d" % ((cc + 1) % 2, hd)])
                P.op("act", lambda e, cc=cc, dst=dst: e.activation(Sbr[cc % 2], dst, AF.Copy),
                     reads=["Sp%d_%d" % ((cc + 1) % 2, h) for h in range(NH)], writes=["Sbr%d" % (cc % 2)])
                if cc + 2 < 8:
                    emit_ds(cc + 2)
                if cc % 2 == 1:
                    post_b(cc // 2)
            if full:
                HS = [(hd, slice(hd * T, (hd + 1) * T)) for hd in range(NH)]
                ssb = [(ps[0], "pb0"), (ps[4], "pb4"), (ps[7], "pb7"), (ps[6], "pb6")]
                for hd, sl in HS:
                    P.op("act", lambda e, sl=sl: e.activation(osq[:, sl], B3[:, sl], AF.Square),
                         reads=[b3k[hd]], writes=["osq%d" % hd])
                for hd, sl in HS:
                    P.op("pe", lambda e, sl=sl, hd=hd: e.matmul(ssb[hd][0][:, :], self.ones_b, osq[:, sl], start=True, stop=True),
                         reads=["osq%d" % hd, "cst"], writes=[ssb[hd][1]])
                for hd, sl in HS:
                    P.op("dve", lambda e, sl=sl, hd=hd: e.tensor_scalar(B2[:, sl], ssb[hd][0][:, :], float(128 * EPS), None, ALU.add),
                         reads=[ssb[hd][1]], writes=[b2k[hd]])
                for hd, sl in HS:
                    P.op("act", lambda e, sl=sl: e.activation(B2[:, sl], B2[:, sl], AF.Ln), reads=[b2k[hd]], writes=[b2k[hd]])
                for hd, sl in HS:
                    P.op("act", lambda e, sl=sl: e.activation(B2[:, sl], B2[:, sl], AF.Exp, scale=-0.5), reads=[b2k[hd]], writes=[b2k[hd]])
                for hd, sl in HS:
                    P.op("dve", lambda e, sl=sl: e.tensor_tensor(B2[:, sl], B3[:, sl], B2[:, sl], ALU.mult),
                         reads=[b3k[hd], b2k[hd]], writes=[b2k[hd]])
                    P.op("dve", lambda e, sl=sl, hd=hd: e.scalar_tensor_tensor(
                        out=ocat[:, sl], in0=B2[:, sl], scalar=lcol(self.gnp, hd), in1=sg[:, sl], op0=ALU.mult, op1=ALU.mult),
                        reads=[b2k[hd], "sg%d" % hd, "lbq"], writes=["ocat%d" % hd])
                okeys = ["ocat%d" % c for c in range(DC)]
                for dc in range(DC):
                    bank, bk = pjbank()
                    P.op("pe", lambda e, dc=dc, bank=bank: _mm_group(e, bank[:, :], [
                        (Wout[:, c * D + dc * 128:c * D + (dc + 1) * 128], ocat[:, c * T:(c + 1) * T]) for c in range(DC)]),
                        reads=okeys + ["W"], writes=[bk])
                    P.op("dve", lambda e, dc=dc, bank=bank, xt=xt: e.tensor_tensor(
                        xt[:, dc * T:(dc + 1) * T], xt[:, dc * T:(dc + 1) * T], bank[:, :], ALU.add),
                        reads=[bk, xk], writes=[xk])
                self.store_x(i)
            if i + 2 < NT:
                self.load_x(i + 2, False)
        if state_only:
            if self.fused:
                groups = [[2 * k, 2 * k + 1] for k in range(NCORES // 2)]
                P.op("sp", lambda e: e.dma_start(out=self.cc_in[l], in_=S), reads=skeys, writes=["ccin%d" % l], dma="st_s")
                P.op("pool", lambda e: e.collective_compute("AllGather", ALU.bypass, replica_groups=groups,
                                                            ins=[self.cc_in[l]], outs=[self.cc_out[l]]),
                     reads=["ccin%d" % l], writes=["ccout%d" % l], dma="cc%d" % l, inc=1)
            else:
                P.op("sp", lambda e: e.dma_start(out=self.s_out, in_=S), reads=skeys, dma="st_s")

    def post_phase(self):
        A, P = self.A, self.P
        ps = self.ps
        xns = [A.f32(DC * T), A.f32(DC * T)]
        gidx = 6
        self.load_x(0, False)
        if NT > 1:
            self.load_x(1, False)
        for i in range(NT):
            b = i % 2
            xt = self.xt[b]
            xn = xns[b]
            if self.debug_raw_out:
                src = xt
                skey = "xt%d" % b
            else:
                hb = self.hb
                hkeys = ["hb%d" % c for c in range(DC)]
                P.op("act", lambda e, xt=xt: e.activation(hb, xt, AF.Square), reads=["xt%d" % b], writes=hkeys)
                P.op("pe", lambda e: _mm_group(e, ps[0][:, :], [
                    (self.ones_b, self.hb[:, c * T:(c + 1) * T]) for c in range(DC)]),
                    reads=hkeys + ["cst"], writes=["pb0"])
                self.rstd_from(ps[0][:, :], "pb0", float(D * EPS))
                for c in range(DC):
                    P.op("dve", lambda e, c=c, xt=xt, xn=xn: e.scalar_tensor_tensor(
                        out=xn[:, c * T:(c + 1) * T], in0=xt[:, c * T:(c + 1) * T],
                        scalar=self.g32[:, gidx * DC + c:gidx * DC + c + 1], in1=self.rstd,
                        op0=ALU.mult, op1=ALU.mult),
                        reads=["xt%d" % b, "rstd", "g32"], writes=["xn%d" % b])
                src = xn
                skey = "xn%d" % b
            P.op("sp", lambda e, i=i, src=src: e.dma_start(out=self.out[i], in_=src), reads=[skey], dma="sto%d" % b)
            if i + 2 < NT:
                self.load_x(i + 2, False)


def make_consts():
    import ml_dtypes
    c = np.zeros((128, 1152), np.float32)
    c[:, 0:128] = np.eye(128, dtype=np.float32)
    idb = np.eye(128, dtype=np.float32).astype(ml_dtypes.bfloat16)
    c[:, 128:192] = idb.view(np.float32)
    onb = np.ones((128, 128), np.float32).astype(ml_dtypes.bfloat16)
    c[:, 192:256] = onb.view(np.float32)
    tri = (np.arange(CH)[:, None] <= np.arange(CH)[None, :]).astype(np.float32)
    c[0:CH, 256:512] = np.tile(tri, (1, NH))
    c[:, 512:640] = (np.arange(128)[:, None] <= np.arange(128)[None, :]).astype(np.float32)
    rm = np.ones((T,), np.float32)
    rm[0::CH] = 0.0
    c[:, 640:1152] = rm[None, :]
    return c


def pack_gains(inp):
    g = np.zeros((128, 7, DC), np.float32)
    vecs = [inp["norm_ffn1"][0], inp["norm_mix"][0], inp["norm_ffn2"][0],
            inp["norm_ffn1"][1], inp["norm_mix"][1], inp["norm_ffn2"][1], inp["norm_final"]]
    for k, v in enumerate(vecs):
        g[:, k, :] = np.asarray(v, np.float32).reshape(DC, 128).T
    return np.ascontiguousarray(g.reshape(128, 7 * DC))


def pack_small(inp):
    f = lambda k: np.asarray(inp[k], np.float32)
    mixp = np.zeros((128, 32), np.float32)
    mixp[:, 0:8] = f("lb_param").reshape(DEPTH, NH, 128).transpose(2, 0, 1).reshape(128, 8)
    mixp[:, 8:16] = f("hgrn_norm").reshape(DEPTH, NH, 128).transpose(2, 0, 1).reshape(128, 8)
    mixp[:, 16:24] = f("ln_v_gain").reshape(DEPTH, 4, 128).transpose(2, 0, 1).reshape(128, 8)
    mixp[:, 24:32] = f("ln_v_bias").reshape(DEPTH, 4, 128).transpose(2, 0, 1).reshape(128, 8)
    wst = np.ascontiguousarray(f("w_spatial").transpose(0, 3, 1, 2).reshape(DEPTH, 128, 512))
    bsp = np.ascontiguousarray(np.broadcast_to(f("b_spatial").reshape(DEPTH, 1, 512), (DEPTH, 128, 512)))
    return {"mixp": mixp, "wst": wst, "bsp": bsp}


_CACHE = {}


def launch(phases, inputs, xs_in=None, s_in=None, xs_out=False, debug_raw_out=False, trace=False, fused=False, scopes=False):
    b = Builder(phases, debug_raw_out=debug_raw_out, xs_in=xs_in is not None, xs_out=xs_out, fused=fused, scopes=scopes)
    nc = b.build()
    f = lambda k: np.asarray(inputs[k], np.float32)
    common = {"gains": pack_gains(inputs), "cst": make_consts()}
    common.update(pack_small(inputs))
    if 1 in b.need_f:
        common.update({"wg1": f("ffn1_w_gate"), "wu1": f("ffn1_w_up"), "wd1": f("ffn1_w_down")})
    if 2 in b.need_f:
        common.update({"wg2": f("ffn2_w_gate"), "wu2": f("ffn2_w_up"), "wd2": f("ffn2_w_down")})
    if b.need_m:
        common.update({"w_in": f("w_in"), "w_out": f("w_out")})
    x = np.ascontiguousarray(f("x").reshape(NCORES, NT, T, DC, 128).transpose(0, 1, 4, 3, 2).reshape(NCORES, NT, 128, DC * T))
    in_maps = []
    for r in range(NCORES):
        m = dict(common)
        m["x_in"] = x[r]
        if fused:
            m["role"] = np.full((128, 1), float(r % 2), np.float32)
        else:
            m["s_in"] = np.zeros((128, 512), np.float32) if s_in is None else np.ascontiguousarray(s_in[r])
        if xs_in is not None:
            m["xs_in"] = np.ascontiguousarray(xs_in[r])
        in_maps.append(m)
    res = run_bass_kernel_spmd(nc, in_maps, core_ids=list(range(NCORES)), trace=trace)
    outs = {k: np.stack([np.asarray(r[k]) for r in res.results], 0) for k in res.results[0].keys()}
    return outs, res


def handoff(s_out):
    s_in = np.zeros_like(s_out)
    s_in[1::2] = s_out[0::2]
    return s_in


FUSED_PHASES = [("ffn", 0, 1, True), ("mix", 0, True), ("mix", 0, False), ("ffn", 0, 2, False),
                ("ffn", 1, 1, False), ("mix", 1, True), ("mix", 1, False), ("ffn", 1, 2, False), ("post",)]


def kernel(**inputs):
    o, _ = launch(FUSED_PHASES, inputs, fused=True)
    return from_fm(o["out"])


def from_fm(o):
    o = np.asarray(o, np.float32).reshape(NCORES, NT, 128, DC, T).transpose(0, 1, 4, 3, 2)
    return np.ascontiguousarray(o.reshape(4, 8192, D))
```

```python
import numpy as np
from contextlib import ExitStack

import concourse.bass as bass
import concourse.mybir as mybir
from concourse.bass_utils import run_bass_kernel_spmd

F32 = mybir.dt.float32
BF16 = mybir.dt.bfloat16
AF = mybir.ActivationFunctionType
ALU = mybir.AluOpType

NCORES = 8
D = 1024
DC = 8
FF = 2816
FC = 22
T = 512
NT = 8
TOK = NT * T
DEPTH = 2
EPS = 1e-6
F_MIN = 1e-20
NH = 4
CH = 64

ENGS = ("pe", "act", "dve", "pool", "sp")


class Prog:
    def __init__(self, nc, stack):
        self.nc = nc
        self.q = {e: [] for e in ENGS}
        self.sem = {}
        self.cnt = {}
        self.stack = stack
        for e in ("pe", "act", "dve", "pool"):
            self.sem[e] = stack.enter_context(nc.semaphore("sem_" + e))
            self.cnt[e] = 0
        self.lastw = {}
        self.readers = {}
        self.waited = {e: {} for e in ENGS}
        self.scope = None
        self.use_scopes = False
        self.alias = {}
        self._uid = 0
        self._rot = {}
        self.MAXFLY = 8

    def subkey(self, group):
        self._uid += 1
        lst = self.alias.setdefault(group, [])
        if len(lst) < self.MAXFLY:
            k = "%s#%d" % (group, self._uid)
            self.slot = len(lst)
            lst.append(k)
            self._rot[group] = 0
            return k
        i = self._rot.get(group, 0)
        self._rot[group] = (i + 1) % len(lst)
        self.slot = i
        return lst[i]

    def reset_group(self, group):
        self.alias[group] = []

    def _sem(self, key):
        if key not in self.sem:
            self.sem[key] = self.stack.enter_context(self.nc.semaphore("sem_" + key))
            self.cnt[key] = 0
        return self.sem[key]

    def _expand(self, keys):
        out = []
        for k in keys:
            if k in self.alias:
                out.extend(self.alias[k])
            else:
                out.append(k)
        return out

    def op(self, eng, fn, reads=(), writes=(), dma=None, inc=16):
        reads = self._expand(reads)
        writes = self._expand(writes)
        evs = []
        for k in reads:
            if k in self.lastw:
                evs.append(self.lastw[k])
        for k in writes:
            if k in self.lastw:
                evs.append(self.lastw[k])
            evs.extend(self.readers.get(k, ()))
        best = {}
        for (s, v, src) in evs:
            if eng == "pe" and src == "pe":
                continue
            if v > best.get(s, 0):
                best[s] = v
        waits = []
        for s, v in best.items():
            if v > self.waited[eng].get(s, 0):
                self.waited[eng][s] = v
                waits.append((s, v))
        if dma is None:
            self.cnt[eng] += 1
            ev = (eng, self.cnt[eng], eng)
        else:
            self._sem(dma)
            self.cnt[dma] += inc
            ev = (dma, self.cnt[dma], "dma%d" % inc)
        self.q[eng].append((fn, waits, ev, self.scope))
        for k in reads:
            self.readers.setdefault(k, []).append(ev)
        for k in writes:
            self.lastw[k] = ev
            self.readers[k] = []
        return ev

    def barrier(self):
        allev = []
        for e in ("pe", "act", "dve", "pool"):
            if self.cnt[e] > 0:
                allev.append((e, self.cnt[e]))
        for k in self.cnt:
            if k not in ("pe", "act", "dve", "pool") and self.cnt[k] > 0 and not k.startswith("bg_") and not k.startswith("cc"):
                allev.append((k, self.cnt[k]))
        for eng in ENGS:
            waits = []
            for s, v in allev:
                if s == eng:
                    continue
                if v > self.waited[eng].get(s, 0):
                    self.waited[eng][s] = v
                    waits.append((s, v))
            if waits:
                self.q[eng].append((None, waits, None, None))
        self.lastw = {k: v for k, v in self.lastw.items() if k.startswith("wsc") or k.startswith("ccout")}
        self.readers = {k: v for k, v in self.readers.items() if k.startswith("wsc") or k.startswith("ccout")}

    def final_wait(self, eng="sp"):
        waits = []
        for k in self.cnt:
            if self.cnt[k] > 0 and k != eng:
                waits.append((k, self.cnt[k]))
        self.q[eng].append((None, waits, None, None))

    def _run(self, e, engine):
        cur = None
        for fn, waits, ev, scope in self.q[e]:
            if self.use_scopes and fn is not None and scope != cur:
                if cur is not None:
                    self.nc.leave_named_scope(cur, cur_id, False)
                cur = scope
                if cur is not None:
                    cur_id, _ = self.nc.enter_named_scope(cur, False)
            for s, v in waits:
                engine.wait_ge(self.sem[s], v)
            if fn is None:
                continue
            ins = fn(engine)
            if ev[2] == "dma16":
                ins.then_inc(self.sem[ev[0]], 16)
            else:
                ins.then_inc(self.sem[ev[0]], 1)
        if self.use_scopes and cur is not None:
            self.nc.leave_named_scope(cur, cur_id, False)

    def emit(self):
        with self.nc.Block() as blk:

            @blk.tensor
            def _(e):
                self._run("pe", e)

            @blk.scalar
            def _(e):
                self._run("act", e)

            @blk.vector
            def _(e):
                self._run("dve", e)

            @blk.gpsimd
            def _(e):
                self._run("pool", e)

            @blk.sync
            def _(e):
                self._run("sp", e)


class Arena:
    def __init__(self, t, ncols):
        self.t = t
        self.n = ncols
        self.off = 0
        self.marks = []

    def f32(self, cols):
        assert self.off + cols <= self.n, ("arena overflow", self.off, cols, self.n)
        ap = self.t[:, self.off:self.off + cols]
        self.off += cols
        return ap

    def bf16(self, cols):
        c32 = (cols + 1) // 2
        ap = self.f32(c32)
        return ap.bitcast(BF16)

    def mark(self):
        self.marks.append(self.off)

    def release(self):
        self.off = self.marks.pop()


def _mm_group(pe, out, pairs):
    n = len(pairs)
    ins = None
    for idx, (l, r) in enumerate(pairs):
        ins = pe.matmul(out, l, r, start=(idx == 0), stop=(idx == n - 1))
    return ins


class Builder:
    def __init__(self, phases, debug_raw_out=False, xs_in=False, xs_out=False, fused=False, scopes=False, prefetch=True):
        self.phases = phases
        self.debug_raw_out = debug_raw_out
        self.use_xs_in = xs_in
        self.scopes = scopes
        self.prefetch = prefetch
        self.use_xs_out = xs_out
        self.fused = fused
        if fused:
            self.nc = bass.Bass("TRN2", target_bir_lowering=False, num_devices=NCORES)
        else:
            self.nc = bass.Bass("TRN2", target_bir_lowering=False)
        nc = self.nc
        self.stack = ExitStack()
        dt = nc.dram_tensor
        need_f = set(ph[2] for ph in phases if ph[0] == "ffn")
        need_m = any(ph[0] == "mix" for ph in phases)
        self.need_f, self.need_m = need_f, need_m
        self.x_in = dt("x_in", [NT, 128, DC * T], F32, kind="ExternalInput").ap()
        self.out = dt("out", [NT, 128, DC * T], F32, kind="ExternalOutput").ap()
        self.xs = dt("xs", [NT, 128, DC * T], F32, kind="ExternalOutput" if xs_out else "Internal").ap()
        self.xs_in = dt("xs_in", [NT, 128, DC * T], F32, kind="ExternalInput").ap() if xs_in else None
        self.xs_src = self.xs_in if xs_in else self.xs
        self.win = dt("w_in", [DEPTH, D, 3072], F32, kind="ExternalInput").ap() if need_m else None
        self.wout = dt("w_out", [DEPTH, D, D], F32, kind="ExternalInput").ap() if need_m else None
        self.mixp_d = dt("mixp", [128, 32], F32, kind="ExternalInput").ap()
        self.wst_d = dt("wst", [DEPTH, 128, 512], F32, kind="ExternalInput").ap()
        self.bsp_d = dt("bsp", [DEPTH, 128, 512], F32, kind="ExternalInput").ap()
        self.sc_kT = dt("sc_kT", [NT, 128, NH * T], BF16, kind="Internal").ap()
        self.sc_kd = dt("sc_kd", [NT, 64, 8 * 512], BF16, kind="Internal").ap()
        self.sc_v = dt("sc_v", [NT, 64, 8 * 512], BF16, kind="Internal").ap()
        self.sc_ea = dt("sc_ea", [NT, 128, NH * T], F32, kind="Internal").ap()
        self.sc_E = dt("sc_E", [NT, 128, NH * 8], F32, kind="Internal").ap()
        self.wsc_g = dt("wsc_g", [D, FF], BF16, kind="Internal").ap()
        self.wsc_u = dt("wsc_u", [D, FF], BF16, kind="Internal").ap()
        self.wsc_d = dt("wsc_d", [FF, D], BF16, kind="Internal").ap()
        self.wsc_in = dt("wsc_in", [D, 3072], BF16, kind="Internal").ap()
        self.wsc_out = dt("wsc_out", [D, D], BF16, kind="Internal").ap()
        if fused:
            self.cc_in = [dt("cc_in%d" % l, [128, 512], F32, kind="Internal").ap() for l in range(DEPTH)]
            self.cc_out = [dt("cc_out%d" % l, [256, 512], F32, kind="Internal").ap() for l in range(DEPTH)]
            self.role_d = dt("role", [128, 1], F32, kind="ExternalInput").ap()
        else:
            self.s_in = dt("s_in", [128, 512], F32, kind="ExternalInput").ap()
            self.s_out = dt("s_out", [128, 512], F32, kind="ExternalOutput").ap()
        self.wg = [dt("wg%d" % k, [DEPTH, D, FF], F32, kind="ExternalInput").ap() if k in need_f else None for k in (1, 2)]
        self.wu = [dt("wu%d" % k, [DEPTH, D, FF], F32, kind="ExternalInput").ap() if k in need_f else None for k in (1, 2)]
        self.wd = [dt("wd%d" % k, [DEPTH, FF, D], F32, kind="ExternalInput").ap() if k in need_f else None for k in (1, 2)]
        self.gains_d = dt("gains", [128, 7 * DC], F32, kind="ExternalInput").ap()
        self.cst_d = dt("cst", [128, 1152], F32, kind="ExternalInput").ap()

    def build(self):
        nc = self.nc
        st = self.stack
        with st:
            NCOL = 52000
            arena_t = st.enter_context(nc.sbuf_tensor("arena", [128, NCOL], F32))
            self.A = Arena(arena_t, NCOL)
            self.ps = [st.enter_context(nc.psum_tensor("ps%d" % k, [128, 512], F32)) for k in range(8)]
            self.P = Prog(nc, st)
            self.setup_persistent()
            self.P.use_scopes = self.scopes
            self.conv_done = set()
            for pi, ph in enumerate(self.phases):
                self.P.barrier()
                self.P.scope = "p%d_%s" % (pi, "_".join(str(v) for v in ph))
                self.P.reset_group("W")
                self.P.reset_group("Wa")
                self.P.reset_group("Wb")
                self.P.reset_group("Wo")
                self.A.mark()
                kind = ph[0]
                self.cur_pi = pi
                if kind == "ffn":
                    self.ffn_phase(ph[1], ph[2], first=ph[3])
                elif kind == "post":
                    self.post_phase()
                elif kind == "mix":
                    if ph[2]:
                        self.state_phase(ph[1])
                    else:
                        self.mix_phase2(ph[1])
                else:
                    raise ValueError(kind)
                self.A.release()
                if kind == "ffn" or (kind == "mix" and not ph[2]):
                    self.xs_src = self.xs
            self.P.final_wait("sp")
            self.P.emit()
        return nc

    def convert_ffn(self, l, which):
        P = self.P
        wg_d, wu_d, wd_d = self.wg[which - 1], self.wu[which - 1], self.wd[which - 1]
        oldk = list(P.alias.get("wscF", []))
        P.reset_group("wscF")
        for c in range(DC):
            r = slice(c * 128, (c + 1) * 128)
            P.op("pool", lambda e, r=r: e.dma_start(out=self.wsc_g[r, :], in_=wg_d[l, r, :]), reads=["W", "Wa", "Wb", "Wo"],
                 writes=[P.subkey("wscF")] + (oldk if c == 0 else []), dma="bg_F%d" % P.slot)
            P.op("pool", lambda e, r=r: e.dma_start(out=self.wsc_u[r, :], in_=wu_d[l, r, :]), reads=["W", "Wa", "Wb", "Wo"], writes=[P.subkey("wscF")], dma="bg_F%d" % P.slot)
        for j in range(0, FC, 2):
            r = slice(j * 128, (j + 2) * 128)
            P.op("pool", lambda e, r=r: e.dma_start(out=self.wsc_d[r, :], in_=wd_d[l, r, :]), reads=["W", "Wa", "Wb", "Wo"], writes=[P.subkey("wscF")], dma="bg_F%d" % P.slot)
        self.conv_done.add(("ffn", l, which))

    def convert_mix(self, l):
        P = self.P
        oldk = list(P.alias.get("wscM", []))
        P.reset_group("wscM")
        for c in range(DC):
            r = slice(c * 128, (c + 1) * 128)
            P.op("pool", lambda e, r=r: e.dma_start(out=self.wsc_in[r, :], in_=self.win[l, r, :]), reads=["W", "Wa", "Wb", "Wo"],
                 writes=[P.subkey("wscM")] + (oldk if c == 0 else []), dma="bg_M%d" % P.slot)
        for c in range(0, DC, 2):
            r = slice(c * 128, (c + 2) * 128)
            P.op("pool", lambda e, r=r: e.dma_start(out=self.wsc_out[r, :], in_=self.wout[l, r, :]), reads=["W", "Wa", "Wb", "Wo"], writes=[P.subkey("wscM")], dma="bg_M%d" % P.slot)
        self.conv_done.add(("mix", l))

    def schedule_conversion(self, pi):
        if not self.prefetch:
            return
        for ph in self.phases[pi + 1:]:
            if ph[0] == "ffn":
                key = ("ffn", ph[1], ph[2])
                if key not in self.conv_done:
                    if self.phases[pi][0] == "mix" and self.phases[pi][2]:
                        return
                    self.convert_ffn(ph[1], ph[2])
                return
            if ph[0] == "mix":
                key = ("mix", ph[1])
                if key not in self.conv_done:
                    self.convert_mix(ph[1])
                    return
                continue
            return

    def setup_persistent(self):
        A, P = self.A, self.P
        self.cst = A.f32(1152)
        self.tri64 = self.cst[0:64, 256:512]
        self.tri128 = self.cst[:, 512:640]
        self.rmask = self.cst[:, 640:1152]
        self.ident_f = self.cst[:, 0:128]
        self.ident_b = self.cst[:, 128:192].bitcast(BF16)
        self.ones_b = self.cst[:, 192:256].bitcast(BF16)
        self.gains = A.f32(7 * DC)
        self.g32 = A.f32(7 * DC)
        self.xt = [A.f32(DC * T), A.f32(DC * T)]
        self.hb = A.bf16(DC * T)
        self.rstd = A.f32(T)
        P.op("sp", lambda e: e.dma_start(out=self.cst, in_=self.cst_d), writes=["cst"], dma="ld_c")
        P.op("sp", lambda e: e.dma_start(out=self.gains, in_=self.gains_d), writes=["gains"], dma="ld_g")
        if self.fused:
            self.role = A.f32(1)
            P.op("sp", lambda e: e.dma_start(out=self.role, in_=self.role_d), writes=["role"], dma="ld_role")
        self.mixp = A.f32(32)
        sc = A.f32(64)
        self.lb = A.f32(8)
        self.homl = A.f32(8)
        self.nhoml = A.f32(8)
        self.lbh = A.f32(8)
        self.gnp = A.f32(8)
        P.op("sp", lambda e: e.dma_start(out=self.mixp, in_=self.mixp_d), writes=["mixp"], dma="ld_mp")
        l0, l1 = self.mixp[:, 0:4], self.mixp[:, 4:8]
        m, d0, d1, e0, e1, ss_, r_, p0, p1, c1 = [sc[:, 4 * k:4 * k + 4] for k in range(10)]
        dv = lambda fn, rd, wr: P.op("dve", fn, reads=rd, writes=wr)
        dv(lambda e: e.tensor_tensor(m, l0, l1, ALU.max), ["mixp"], ["sc"])
        dv(lambda e: e.tensor_tensor(d0, l0, m, ALU.subtract), ["mixp", "sc"], ["sc"])
        dv(lambda e: e.tensor_tensor(d1, l1, m, ALU.subtract), ["mixp", "sc"], ["sc"])
        P.op("act", lambda e: e.activation(e0, d0, AF.Exp), reads=["sc"], writes=["sc"])
        P.op("act", lambda e: e.activation(e1, d1, AF.Exp), reads=["sc"], writes=["sc"])
        dv(lambda e: e.tensor_tensor(ss_, e0, e1, ALU.add), ["sc"], ["sc"])
        dv(lambda e: e.reciprocal(r_, ss_), ["sc"], ["sc"])
        dv(lambda e: e.tensor_tensor(p0, e0, r_, ALU.mult), ["sc"], ["sc"])
        dv(lambda e: e.tensor_tensor(p1, e1, r_, ALU.mult), ["sc"], ["sc"])
        dv(lambda e: e.tensor_tensor(c1, p0, p1, ALU.add), ["sc"], ["sc"])
        dv(lambda e: e.tensor_tensor(self.lb[:, 0:4], p0, p0, ALU.subtract), ["sc"], ["lbp"])
        dv(lambda e: e.tensor_tensor(self.lb[:, 4:8], c1, p0, ALU.subtract), ["sc"], ["lbp"])
        dv(lambda e: e.tensor_scalar(self.homl, self.lb, -0.5, 0.5, ALU.mult, ALU.add), ["lbp"], ["lbq"])
        dv(lambda e: e.tensor_scalar(self.nhoml, self.lb, 0.5, -0.5, ALU.mult, ALU.add), ["lbp"], ["lbq"])
        dv(lambda e: e.tensor_scalar(self.lbh, self.lb, 0.5, 0.5, ALU.mult, ALU.add), ["lbp"], ["lbq"])
        dv(lambda e: e.tensor_scalar(self.gnp, self.mixp[:, 8:16], float(0.5 * np.sqrt(128.0)), None, ALU.mult),
           ["mixp"], ["lbq"])
        P.op("dve", lambda e: e.tensor_scalar(self.g32, self.gains, 32.0, None, ALU.mult),
             reads=["gains"], writes=["g32"])

    def load_x(self, i, first):
        P = self.P
        b = i % 2
        xt = self.xt[b]
        if not first:
            src = self.xs_src
            P.op("sp", lambda e: e.dma_start(out=xt, in_=src[i]), writes=["xt%d" % b], dma="ldx%d" % b)
            return
        P.op("sp", lambda e: e.dma_start(out=xt, in_=self.x_in[i]), writes=["xt%d" % b], dma="ldx%d" % b)

    def store_x(self, i):
        P = self.P
        b = i % 2
        xt = self.xt[b]
        P.op("sp", lambda e: e.dma_start(out=self.xs[i], in_=xt), reads=["xt%d" % b], dma="stx%d" % b)

    def rstd_from(self, src, skey, c, dst=None, dkey="rstd"):
        P = self.P
        if dst is None:
            dst = self.rstd
        P.op("dve", lambda e: e.tensor_scalar(dst, src, c, None, ALU.add), reads=[skey], writes=[dkey])
        P.op("act", lambda e: e.activation(dst, dst, AF.Ln), reads=[dkey], writes=[dkey])
        P.op("act", lambda e: e.activation(dst, dst, AF.Exp, scale=-0.5), reads=[dkey], writes=[dkey])

    def norm_stage(self, i, gidx):
        P = self.P
        b = i % 2
        xt = self.xt[b]
        hb = self.hb
        xk = "xt%d" % b
        hkeys = ["hb%d" % c for c in range(DC)]
        P.op("act", lambda e: e.activation(hb, xt, AF.Square), reads=[xk], writes=hkeys)
        P.op("pe", lambda e: _mm_group(e, self.ps[0][:, :], [
            (self.ones_b, hb[:, c * T:(c + 1) * T]) for c in range(DC)]),
            reads=hkeys + ["cst"], writes=["pb0"])
        self.rstd_from(self.ps[0][:, :], "pb0", float(D * EPS))
        for c in range(DC):
            P.op("dve", lambda e, c=c: e.scalar_tensor_tensor(
                out=hb[:, c * T:(c + 1) * T], in0=xt[:, c * T:(c + 1) * T],
                scalar=self.g32[:, gidx * DC + c:gidx * DC + c + 1], in1=self.rstd,
                op0=ALU.mult, op1=ALU.mult),
                reads=[xk, "rstd", "g32"], writes=["hb%d" % c])

    def ffn_phase(self, l, which, first=False):
        A, P = self.A, self.P
        gidx = l * 3 + (0 if which == 1 else 2)
        wg_d, wu_d, wd_d = self.wg[which - 1], self.wu[which - 1], self.wd[which - 1]
        Wg = A.bf16(DC * FF)
        Wu = A.bf16(DC * FF)
        Wd = A.bf16(FC * D)
        HF = FC // 2
        act = A.bf16(HF * T)
        stmp = [A.f32(T), A.f32(T)]
        self.load_x(0, first)
        if NT > 1:
            self.load_x(1, first)
        if ("ffn", l, which) in self.conv_done:
            for c in range(DC):
                P.op("sp", lambda e, c=c: e.dma_start(out=Wg[:, c * FF:(c + 1) * FF], in_=self.wsc_g[c * 128:(c + 1) * 128, :]),
                     reads=["wscF"], writes=[P.subkey("Wa")], dma="ldwh%d" % P.slot)
                P.op("sp", lambda e, c=c: e.dma_start(out=Wu[:, c * FF:(c + 1) * FF], in_=self.wsc_u[c * 128:(c + 1) * 128, :]),
                     reads=["wscF"], writes=[P.subkey("Wa")], dma="ldwh%d" % P.slot)
            for j in range(FC):
                P.op("sp", lambda e, j=j: e.dma_start(out=Wd[:, j * D:(j + 1) * D], in_=self.wsc_d[j * 128:(j + 1) * 128, :]),
                     reads=["wscF"], writes=[P.subkey("Wb")], dma="ldwhb%d" % P.slot)
        else:
            for c in range(DC):
                P.op("pool", lambda e, c=c: e.dma_start(out=Wg[:, c * FF:(c + 1) * FF], in_=wg_d[l, c * 128:(c + 1) * 128, :]),
                     writes=[P.subkey("Wa")], dma="ldw%d" % P.slot)
                P.op("pool", lambda e, c=c: e.dma_start(out=Wu[:, c * FF:(c + 1) * FF], in_=wu_d[l, c * 128:(c + 1) * 128, :]),
                     writes=[P.subkey("Wa")], dma="ldw%d" % P.slot)
            for j in range(FC):
                P.op("pool", lambda e, j=j: e.dma_start(out=Wd[:, j * D:(j + 1) * D], in_=wd_d[l, j * 128:(j + 1) * 128, :]),
                     writes=[P.subkey("Wb")], dma="ldwb%d" % P.slot)
        self.schedule_conversion(self.cur_pi)
        ps = self.ps
        G = [ps[1], ps[3]]
        U = [ps[2], ps[4]]
        Y = [ps[5], ps[6], ps[7]]
        hb = self.hb
        hkeys = ["hb%d" % c for c in range(DC)]

        def gate_up(i, j, jj):
            par = jj % 2
            P.op("pe", lambda e: _mm_group(e, G[par][:, :], [
                (Wg[:, c * FF + j * 128:c * FF + (j + 1) * 128], hb[:, c * T:(c + 1) * T]) for c in range(DC)]),
                reads=hkeys + ["Wa"], writes=["pb%d" % (1 + 2 * par)])
            P.op("pe", lambda e: _mm_group(e, U[par][:, :], [
                (Wu[:, c * FF + j * 128:c * FF + (j + 1) * 128], hb[:, c * T:(c + 1) * T]) for c in range(DC)]),
                reads=hkeys + ["Wa"], writes=["pb%d" % (2 + 2 * par)])
            P.op("act", lambda e: e.activation(stmp[par], G[par][:, :], AF.Silu),
                 reads=["pb%d" % (1 + 2 * par)], writes=["stmp%d" % par])
            P.op("dve", lambda e: e.tensor_tensor(act[:, jj * T:(jj + 1) * T], U[par][:, :], stmp[par], ALU.mult),
                 reads=["pb%d" % (2 + 2 * par), "stmp%d" % par], writes=["act%d" % jj])

        def down(i, half, dc):
            b = i % 2
            xt = self.xt[b]
            k = dc % 3
            P.op("pe", lambda e: _mm_group(e, Y[k][:, :], [
                (Wd[:, (half * HF + jj) * D + dc * 128:(half * HF + jj) * D + (dc + 1) * 128],
                 act[:, jj * T:(jj + 1) * T]) for jj in range(HF)]),
                reads=["act%d" % jj for jj in range(HF)] + ["Wb"], writes=["pb%d" % (5 + k)])
            P.op("dve", lambda e: e.scalar_tensor_tensor(
                out=xt[:, dc * T:(dc + 1) * T], in0=Y[k][:, :], scalar=0.5, in1=xt[:, dc * T:(dc + 1) * T],
                op0=ALU.mult, op1=ALU.add),
                reads=["pb%d" % (5 + k), "xt%d" % b], writes=["xt%d" % b])

        self.norm_stage(0, gidx)
        for i in range(NT):
            for half in range(2):
                for jj in range(HF):
                    gate_up(i, half * HF + jj, jj)
                for dc in range(DC):
                    down(i, half, dc)
                    if half == 1 and dc == 1 and i + 1 < NT:
                        self.norm_stage(i + 1, gidx)
            self.store_x(i)
            if i + 2 < NT:
                self.load_x(i + 2, first)

    def state_phase(self, l):
        A, P, ps = self.A, self.P, self.ps
        gidx = l * 3 + 1
        WIN = 3072
        Win = A.bf16(DC * WIN)
        B1 = [A.f32(NH * T), A.f32(NH * T)]
        B2 = [A.f32(NH * T), A.f32(NH * T)]
        B3 = [A.f32(NH * T), A.f32(NH * T)]
        Ecol = [A.f32(NH * 8), A.f32(NH * 8)]
        kT = [A.bf16(NH * T), A.bf16(NH * T)]
        kdT = [A.bf16(NH * T), A.bf16(NH * T)]
        vtm = [A.bf16(8 * 512), A.bf16(8 * 512)]
        kdtm = [A.bf16(8 * 512), A.bf16(8 * 512)]
        S = A.f32(512)
        hkeys = ["hb%d" % c for c in range(DC)]
        skeys = ["S%d" % h for h in range(NH)]
        lcol = lambda t, hd: t[:, l * 4 + hd:l * 4 + hd + 1]
        trbs = [ps[5][:, :].bitcast(BF16), ps[4][:, :].bitcast(BF16)]
        trk = ["pb5", "pb4"]
        dsb = [ps[7], ps[6]]
        dsk = ["pb7", "pb6"]

        pre = ("mix", l) in self.conv_done
        wq = "sp" if pre else "pool"
        win_src = self.wsc_in if pre else self.win[l]
        wrd = ["wscM"] if pre else []
        for c in range(DC):
            P.op(wq, lambda e, c=c: e.dma_start(out=Win[:, c * WIN + 512:c * WIN + 1536],
                                                in_=win_src[c * 128:(c + 1) * 128, 512:1536]),
                 reads=wrd, writes=[P.subkey("W")], dma=("ldwh%d" if pre else "ldw%d") % P.slot)
        P.op("dve", lambda e: e.memset(S, 0.0), writes=skeys)
        self.schedule_conversion(self.cur_pi)

        pj = [0]

        def pjbank():
            k = 1 + pj[0] % 3
            pj[0] += 1
            return ps[k], "pb%d" % k

        def stage_a(i):
            p = i % 2
            b1, b2, b3 = B1[p], B2[p], B3[p]
            k1 = ["sB1_%d_%d" % (p, h) for h in range(NH)]
            k2 = ["sB2_%d_%d" % (p, h) for h in range(NH)]
            k3 = ["sB3_%d_%d" % (p, h) for h in range(NH)]
            for hd in range(NH):
                sl = slice(hd * T, (hd + 1) * T)
                bank, bk = pjbank()
                P.op("pe", lambda e, hd=hd, bank=bank: _mm_group(e, bank[:, :], [
                    (Win[:, c * WIN + 512 + hd * 128:c * WIN + 512 + (hd + 1) * 128], self.hb[:, c * T:(c + 1) * T])
                    for c in range(DC)]), reads=hkeys + ["W"], writes=[bk])
                P.op("act", lambda e, sl=sl, bank=bank: e.activation(b1[:, sl], bank[:, :], AF.Tanh, scale=0.5),
                     reads=[bk], writes=[k1[hd]])
                P.op("dve", lambda e, sl=sl, hd=hd: e.tensor_scalar(b2[:, sl], b1[:, sl], lcol(self.nhoml, hd),
                                                                   lcol(self.homl, hd), ALU.mult, ALU.add),
                     reads=[k1[hd], "lbq"], writes=[k2[hd]])
            for cc in range(8):
                bank, bk = pjbank()
                P.op("pe", lambda e, cc=cc, bank=bank: _mm_group(e, bank[0:64, :], [
                    (self.hb[:, c * T + cc * CH:c * T + (cc + 1) * CH], Win[:, c * WIN + 1024:c * WIN + 1536])
                    for c in range(DC)]), reads=hkeys + ["W"], writes=[bk])
                P.op("act", lambda e, cc=cc, bank=bank: e.activation(vtm[p][0:64, cc * 512:(cc + 1) * 512], bank[0:64, :], AF.Copy),
                     reads=[bk], writes=["svtm%d_%d" % (p, cc)])
            if i + 1 < NT:
                self.norm_stage(i + 1, gidx)
            HS = [(hd, slice(hd * T, (hd + 1) * T)) for hd in range(NH)]
            for hd, sl in HS:
                P.op("act", lambda e, sl=sl, hd=hd: e.activation(b1[:, sl], b1[:, sl], AF.Ln, bias=lcol(self.lbh, hd),
                                                                scale=lcol(self.homl, hd)),
                     reads=[k1[hd], "lbq"], writes=[k1[hd]])
            for hd, sl in HS:
                P.op("dve", lambda e, sl=sl: e.tensor_scalar_max(b1[:, sl], b1[:, sl], float(np.log(F_MIN))),
                     reads=[k1[hd]], writes=[k1[hd]])
                P.op("dve", lambda e, sl=sl: e.tensor_tensor_scan(b3[:, sl], self.rmask, b1[:, sl], 0.0, ALU.mult, ALU.add),
                     reads=[k1[hd], "cst"], writes=[k3[hd]])
            for hd, sl in HS:
                P.op("act", lambda e, sl=sl: e.activation(b1[:, sl], b3[:, sl], AF.Exp), reads=[k3[hd]], writes=[k1[hd]])
                P.op("act", lambda e, sl=sl: e.activation(b3[:, sl], b3[:, sl], AF.Exp, scale=-1.0),
                     reads=[k3[hd]], writes=[k3[hd]])
            for hd, sl in HS:
                P.op("dve", lambda e, hd=hd: e.tensor_copy(Ecol[p][:, hd * 8:(hd + 1) * 8], b1[:, hd * T + CH - 1:(hd + 1) * T:CH]),
                     reads=[k1[hd]], writes=["sEcol%d_%d" % (p, hd)])
                P.op("dve", lambda e, sl=sl: e.tensor_tensor(b2[:, sl], b2[:, sl], b3[:, sl], ALU.mult),
                     reads=[k2[hd], k3[hd]], writes=[k2[hd]])
                P.op("dve", lambda e, hd=hd, sl=sl: e.tensor_tensor(
                    kdT[p][:, sl].rearrange("p (c t) -> p c t", t=CH), b2[:, sl].rearrange("p (c t) -> p c t", t=CH),
                    Ecol[p][:, hd * 8:(hd + 1) * 8].unsqueeze(2).broadcast_to([128, 8, CH]), ALU.mult),
                    reads=[k2[hd], "sEcol%d_%d" % (p, hd)], writes=["skdT%d_%d" % (p, hd)])
            for hd, sl in HS:
                P.op("act", lambda e, sl=sl: e.activation(kT[p][:, sl], b2[:, sl], AF.Copy), reads=[k2[hd]], writes=["skT%d_%d" % (p, hd)])
            P.op("sp", lambda e, i=i: e.dma_start(out=self.sc_kT[i], in_=kT[p]), reads=["skT%d_%d" % (p, h) for h in range(NH)], dma="st_kT%d" % p)
            P.op("sp", lambda e, i=i: e.dma_start(out=self.sc_v[i], in_=vtm[p][0:64, :]), reads=["svtm%d_%d" % (p, c) for c in range(8)], dma="st_v%d" % p)
            P.op("sp", lambda e, i=i: e.dma_start(out=self.sc_ea[i], in_=b1), reads=k1, dma="st_ea%d" % p)
            P.op("sp", lambda e, i=i: e.dma_start(out=self.sc_E[i], in_=Ecol[p]), reads=["sEcol%d_%d" % (p, h) for h in range(NH)], dma="st_E%d" % p)

        def stage_b(i):
            p = i % 2
            for cc in range(8):
                half = cc % 2
                trb = trbs[half]

                def trf(e, cc=cc, trb=trb):
                    ins = None
                    for hd in range(NH):
                        ins = e.transpose(trb[0:64, hd * 128:(hd + 1) * 128],
                                          kdT[p][:, hd * T + cc * CH:hd * T + (cc + 1) * CH], self.ident_b)
                    return ins
                P.op("pe", trf, reads=["skdT%d_%d" % (p, hd) for hd in range(NH)] + ["cst"], writes=[trk[half]])
                if cc % 2 == 0:
                    P.op("act", lambda e, cc=cc, trb=trb: e.activation(
                        kdtm[p][0:64, cc * 512:(cc + 1) * 512], trb[0:64, 0:512], AF.Copy),
                        reads=[trk[half]], writes=["skdtm%d_%d" % (p, cc)])
                else:
                    P.op("dve", lambda e, cc=cc, trb=trb: e.tensor_copy(
                        kdtm[p][0:64, cc * 512:(cc + 1) * 512], trb[0:64, 0:512]),
                        reads=[trk[half]], writes=["skdtm%d_%d" % (p, cc)])
            P.op("sp", lambda e, i=i: e.dma_start(out=self.sc_kd[i], in_=kdtm[p][0:64, :]),
                 reads=["skdtm%d_%d" % (p, c) for c in range(8)], dma="st_kd%d" % p)
            for cc in range(8):
                bank, bkey = dsb[cc % 2], dsk[cc % 2]

                def dsf(e, cc=cc, bank=bank):
                    ins = None
                    for hd in range(NH):
                        ins = e.matmul(bank[:, hd * 128:(hd + 1) * 128], kdtm[p][0:64, cc * 512 + hd * 128:cc * 512 + (hd + 1) * 128],
                                       vtm[p][0:64, cc * 512 + hd * 128:cc * 512 + (hd + 1) * 128], start=True, stop=True)
                    return ins
                P.op("pe", dsf, reads=["skdtm%d_%d" % (p, cc), "svtm%d_%d" % (p, cc)], writes=[bkey])
                for hd in range(NH):
                    P.op("dve", lambda e, hd=hd, cc=cc, bank=bank: e.scalar_tensor_tensor(
                        out=S[:, hd * 128:(hd + 1) * 128], in0=S[:, hd * 128:(hd + 1) * 128],
                        scalar=Ecol[p][:, hd * 8 + cc:hd * 8 + cc + 1], in1=bank[:, hd * 128:(hd + 1) * 128],
                        op0=ALU.mult, op1=ALU.add),
                        reads=["S%d" % hd, bkey, "sEcol%d_%d" % (p, hd)], writes=["S%d" % hd])

        self.load_x(0, False)
        if NT > 1:
            self.load_x(1, False)
        self.norm_stage(0, gidx)
        stage_a(0)
        for i in range(NT):
            if i + 2 < NT:
                self.load_x(i + 2, False)
            if i + 1 < NT:
                stage_a(i + 1)
            stage_b(i)
        if self.fused:
            groups = [[2 * k, 2 * k + 1] for k in range(NCORES // 2)]
            P.op("sp", lambda e: e.dma_start(out=self.cc_in[l], in_=S), reads=skeys, writes=["ccin%d" % l], dma="st_s")
            P.op("pool", lambda e: e.collective_compute("AllGather", ALU.bypass, replica_groups=groups,
                                                        ins=[self.cc_in[l]], outs=[self.cc_out[l]]),
                 reads=["ccin%d" % l], writes=["ccout%d" % l], dma="cc%d" % l, inc=1)
        else:
            P.op("sp", lambda e: e.dma_start(out=self.s_out, in_=S), reads=skeys, dma="st_s")

    def mix_phase2(self, l):
        A, P, ps = self.A, self.P, self.ps
        gidx = l * 3 + 1
        WIN = 3072
        Win = A.bf16(DC * WIN)
        B1 = A.f32(NH * T)
        B2 = A.f32(NH * T)
        B3 = A.f32(NH * T)
        Ecol = A.f32(NH * 8)
        kT = A.bf16(NH * T)
        osq = A.bf16(NH * T)
        vtm = A.bf16(8 * 512)
        kdtm = A.bf16(8 * 512)
        S = A.f32(512)
        S2 = A.f32(512)
        Sbr = [A.bf16(512), A.bf16(512)]
        Wout = A.bf16(DC * D)
        qT = A.bf16(NH * T)
        sgs = [A.bf16(NH * T), A.bf16(NH * T)]
        thg = [A.f32(T), A.f32(T)]
        uT = A.bf16(NH * T)
        vln = A.bf16(4 * 512)
        Bc = A.f32(512)
        WmT = A.bf16(512)
        stats = A.f32(24)
        mv = A.f32(8)
        rs4 = A.f32(4)
        ocat = A.bf16(DC * T)
        bsp = ocat.bitcast(F32)[:, 0:512]
        b1k = ["B1_%d" % h for h in range(NH)]
        b2k = ["B2_%d" % h for h in range(NH)]
        b3k = ["B3_%d" % h for h in range(NH)]
        hkeys = ["hb%d" % c for c in range(DC)]
        lcol = lambda t, hd: t[:, l * 4 + hd:l * 4 + hd + 1]

        pre = ("mix", l) in self.conv_done
        wq = "sp" if pre else "pool"
        win_src = self.wsc_in if pre else self.win[l]
        wout_src = self.wsc_out if pre else self.wout[l]
        wrd = ["wscM"] if pre else []
        for c in range(DC):
            P.op(wq, lambda e, c=c: e.dma_start(out=Win[:, c * WIN:(c + 1) * WIN], in_=win_src[c * 128:(c + 1) * 128, :]),
                 reads=wrd, writes=[P.subkey("W")], dma=("ldwh%d" if pre else "ldw%d") % P.slot)
        for c in range(DC):
            P.op(wq, lambda e, c=c: e.dma_start(out=Wout[:, c * D:(c + 1) * D], in_=wout_src[c * 128:(c + 1) * 128, :]),
                 reads=wrd, writes=[P.subkey("Wo")], dma=("ldwho%d" if pre else "ldwo%d") % P.slot)
        P.op("sp", lambda e: e.dma_start(out=bsp, in_=self.bsp_d[l]), writes=["bsp", "ocat0", "ocat1"], dma="ld_bsp")
        P.op("sp", lambda e: e.dma_start(out=B1[:, 0:512], in_=self.wst_d[l]), writes=[b1k[0]], dma="ld_wst")
        P.op("dve", lambda e: e.tensor_tensor(
            WmT.rearrange("p (g t) -> p g t", g=4), B1[:, 0:512].rearrange("p (g t) -> p g t", g=4),
            self.tri128.unsqueeze(1).broadcast_to([128, 4, 128]), ALU.mult),
            reads=[b1k[0], "cst"], writes=["WmT"])
        for g in range(4):
            P.op("pe", lambda e, g=g: e.matmul(ps[1][:, g * 128:(g + 1) * 128], self.ones_b, WmT[:, g * 128:(g + 1) * 128],
                                                start=True, stop=True),
                 reads=["WmT", "cst"], writes=["pb1"])
            P.op("dve", lambda e, g=g: e.scalar_tensor_tensor(
                out=Bc[:, g * 128:(g + 1) * 128], in0=ps[1][:, g * 128:(g + 1) * 128],
                scalar=self.mixp[:, 24 + l * 4 + g:24 + l * 4 + g + 1], in1=bsp[:, g * 128:(g + 1) * 128],
                op0=ALU.mult, op1=ALU.add),
                reads=["pb1", "bsp", "ocat0", "ocat1", "mixp"], writes=["Bc"])
        sk0 = ["Sp0_%d" % h for h in range(NH)]
        if self.fused:
            P.op("sp", lambda e: e.dma_start(out=S, in_=self.cc_out[l][0:128, :]), reads=["ccout%d" % l], writes=sk0, dma="ld_S")
            P.op("dve", lambda e: e.tensor_scalar(S, S, self.role[:, 0:1], None, ALU.mult), reads=sk0 + ["role"], writes=sk0)
        else:
            P.op("sp", lambda e: e.dma_start(out=S, in_=self.s_in), writes=sk0, dma="ld_S")
        P.op("act", lambda e: e.activation(Sbr[1], S, AF.Copy), reads=sk0, writes=["Sbr1"])
        self.schedule_conversion(self.cur_pi)

        pj = [0]

        def pjbank():
            k = 1 + pj[0] % 2
            pj[0] += 1
            return ps[k], "pb%d" % k

        def proj_fm(col0):
            bank, bk = pjbank()
            P.op("pe", lambda e: _mm_group(e, bank[:, :], [
                (Win[:, c * WIN + col0:c * WIN + col0 + 128], self.hb[:, c * T:(c + 1) * T]) for c in range(DC)]),
                reads=hkeys + ["W"], writes=[bk])
            return bank, bk

        def front(i):
            sg = sgs[i % 2]
            P.op("sp", lambda e: e.dma_start(out=kT, in_=self.sc_kT[i]), writes=["kT%d" % h for h in range(NH)], dma="ld_kT")
            P.op("sp", lambda e: e.dma_start(out=kdtm[0:64, :], in_=self.sc_kd[i]), writes=["kdtm%d" % c for c in range(8)], dma="ld_kd")
            P.op("sp", lambda e: e.dma_start(out=vtm[0:64, :], in_=self.sc_v[i]), writes=["vtm%d" % c for c in range(8)], dma="ld_v")
            P.op("sp", lambda e: e.dma_start(out=B1, in_=self.sc_ea[i]), writes=b1k, dma="ld_ea")
            P.op("sp", lambda e: e.dma_start(out=Ecol, in_=self.sc_E[i]), writes=["Ecol%d" % h for h in range(NH)], dma="ld_E")
            for hd in range(NH):
                sl = slice(hd * T, (hd + 1) * T)
                th = thg[hd % 2]
                bank, bk = proj_fm(1536 + hd * 128)
                P.op("act", lambda e, th=th, bank=bank: e.activation(th, bank[:, :], AF.Tanh, scale=0.5),
                     reads=[bk], writes=["thg%d" % (hd % 2)])
                P.op("dve", lambda e, sl=sl, th=th, bank=bank, sg=sg: e.scalar_tensor_tensor(
                    out=sg[:, sl], in0=th, scalar=1.0, in1=bank[:, :], op0=ALU.add, op1=ALU.mult),
                    reads=[bk, "thg%d" % (hd % 2)], writes=["sg%d_%d" % (i % 2, hd)])
            for hd in range(NH):
                sl = slice(hd * T, (hd + 1) * T)
                bank, bk = proj_fm(hd * 128)
                P.op("dve", lambda e, sl=sl, bank=bank: e.tensor_tensor(qT[:, sl], bank[:, :], B1[:, sl], ALU.mult),
                     reads=[bk, b1k[hd]], writes=["qT%d" % hd])
            for g in range(4):
                bank, bk = proj_fm(2048 + g * 128)
                P.op("act", lambda e, g=g, bank=bank: e.activation(uT[:, g * T:(g + 1) * T], bank[:, :], AF.Gelu_apprx_tanh),
                     reads=[bk], writes=["uT%d" % g])
            for s in range(4):
                sl = slice(s * T, (s + 1) * T)
                bank, bk = pjbank()
                P.op("pe", lambda e, s=s, bank=bank: _mm_group(e, bank[:, :], [
                    (self.hb[:, c * T + s * 128:c * T + (s + 1) * 128], Win[:, c * WIN + 2560:c * WIN + 3072])
                    for c in range(DC)]),
                    reads=hkeys + ["W"], writes=[bk])
                P.op("act", lambda e, sl=sl, bank=bank: e.activation(B1[:, sl], bank[:, :], AF.Gelu_apprx_tanh),
                     reads=[bk], writes=[b1k[s]])
                P.op("dve", lambda e, s=s, sl=sl: e.bn_stats(stats[:, s * 6:(s + 1) * 6], B1[:, sl]),
                     reads=[b1k[s]], writes=["stats%d" % s])
                P.op("dve", lambda e, s=s: e.bn_aggr(mv[:, s * 2:(s + 1) * 2], stats[:, s * 6:(s + 1) * 6]),
                     reads=["stats%d" % s], writes=["mv"])
            P.op("dve", lambda e: e.tensor_scalar(rs4, mv[:, 1:8:2], float(EPS), None, ALU.add), reads=["mv"], writes=["rs4"])
            P.op("act", lambda e: e.activation(rs4, rs4, AF.Ln), reads=["rs4"], writes=["rs4"])
            P.op("act", lambda e: e.activation(rs4, rs4, AF.Exp, scale=-0.5), reads=["rs4"], writes=["rs4"])
            for s in range(4):
                sl = slice(s * T, (s + 1) * T)
                P.op("dve", lambda e, s=s, sl=sl: e.tensor_scalar(vln[:, sl], B1[:, sl], mv[:, 2 * s:2 * s + 1], rs4[:, s:s + 1],
                                                                 ALU.subtract, ALU.mult),
                     reads=[b1k[s], "mv", "rs4"], writes=["vln%d" % s])

        def post_b(g):
            bank, bk = pjbank()

            def spf(e, g=g, bank=bank):
                ins = None
                for s in range(4):
                    ins = e.matmul(bank[:, s * 128:(s + 1) * 128], vln[:, s * 512 + g * 128:s * 512 + (g + 1) * 128],
                                   WmT[:, g * 128:(g + 1) * 128], start=True, stop=True)
                return ins
            P.op("pe", spf, reads=["vln%d" % s for s in range(4)] + ["WmT"], writes=[bk])
            gl = slice(g * T, (g + 1) * T)
            P.op("dve", lambda e, g=g, gl=gl, bank=bank: e.scalar_tensor_tensor(
                out=B1[:, gl].rearrange("p (s t) -> p s t", s=4), in0=bank[:, :].rearrange("p (s t) -> p s t", s=4),
                scalar=self.mixp[:, 16 + l * 4 + g:16 + l * 4 + g + 1],
                in1=Bc[:, g * 128:(g + 1) * 128].unsqueeze(1).broadcast_to([128, 4, 128]),
                op0=ALU.mult, op1=ALU.add),
                reads=[bk, "Bc", "mixp"], writes=[b1k[g]])
            P.op("dve", lambda e, g=g, gl=gl: e.tensor_tensor(ocat[:, (4 + g) * T:(5 + g) * T], B1[:, gl], uT[:, gl], ALU.mult),
                 reads=[b1k[g], "uT%d" % g], writes=["ocat%d" % (4 + g)])

        PTv = B2.bitcast(BF16)
        PT = [PTv[0:64, cc * 256:(cc + 1) * 256] for cc in range(8)]
        scb = [(ps[4], "pb4"), (ps[5], "pb5")]
        dsb = [(ps[7], "pb7"), (ps[3], "pb3")]
        obb = [(ps[6], "pb6"), (ps[0], "pb0")]
        Sp = [S, S2]
        csl = lambda hd, cc: slice(hd * T + cc * CH, hd * T + (cc + 1) * CH)

        def recur(i):
            for cc in range(8):
                bank, bkey = scb[cc % 2]

                def scf(e, cc=cc, bank=bank):
                    ins = None
                    for hd in range(NH):
                        ins = e.matmul(bank[0:64, hd * CH:(hd + 1) * CH], kT[:, csl(hd, cc)], qT[:, csl(hd, cc)], start=True, stop=True)
                    return ins
                P.op("pe", scf, reads=["kT%d" % h for h in range(NH)] + ["qT%d" % h for h in range(NH)], writes=[bkey])
                P.op("dve", lambda e, cc=cc, bank=bank: e.tensor_tensor(PT[cc], bank[0:64, 0:256], self.tri64, ALU.mult),
                     reads=[bkey, "cst"], writes=["PT%d" % cc, b2k[0], b2k[1]])

            def emit_ds(cc):
                bank, bkey = dsb[cc % 2]

                def dsf(e, cc=cc, bank=bank):
                    ins = None
                    for hd in range(NH):
                        ins = e.matmul(bank[:, hd * 128:(hd + 1) * 128], kdtm[0:64, cc * 512 + hd * 128:cc * 512 + (hd + 1) * 128],
                                       vtm[0:64, cc * 512 + hd * 128:cc * 512 + (hd + 1) * 128], start=True, stop=True)
                    return ins
                P.op("pe", dsf, reads=["kdtm%d" % cc, "vtm%d" % cc], writes=[bkey])

            emit_ds(0)
            emit_ds(1)
            for cc in range(8):
                OB, okey = obb[cc % 2]
                sbp = Sbr[(cc - 1) % 2]

                def of(e, cc=cc, OB=OB, sbp=sbp):
                    ins = None
                    for hd in range(NH):
                        o = OB[:, hd * CH:(hd + 1) * CH]
                        e.matmul(o, sbp[:, hd * 128:(hd + 1) * 128], qT[:, csl(hd, cc)], start=True, stop=False)
                        ins = e.matmul(o, vtm[0:64, cc * 512 + hd * 128:cc * 512 + (hd + 1) * 128],
                                       PT[cc][:, hd * CH:(hd + 1) * CH], start=False, stop=True)
                    return ins
                P.op("pe", of, reads=["Sbr%d" % ((cc - 1) % 2), "vtm%d" % cc, "PT%d" % cc] + ["qT%d" % h for h in range(NH)],
                     writes=[okey])
                P.op("act", lambda e, cc=cc, OB=OB: e.activation(
                    B3.rearrange("p (h t) -> p h t", h=NH)[:, :, cc * CH:(cc + 1) * CH],
                    OB[:, 0:256].rearrange("p (h t) -> p h t", h=NH), AF.Copy),
                    reads=[okey], writes=b3k)
                bank, bkey = dsb[cc % 2]
                src, dst = Sp[cc % 2], Sp[(cc + 1) % 2]
                for hd in range(NH):
                    P.op("dve", lambda e, hd=hd, cc=cc, bank=bank, src=src, dst=dst: e.scalar_tensor_tensor(
                        out=dst[:, hd * 128:(hd + 1) * 128], in0=src[:, hd * 128:(hd + 1) * 128],
                        scalar=Ecol[:, hd * 8 + cc:hd * 8 + cc + 1], in1=bank[:, hd * 128:(hd + 1) * 128],
                        op0=ALU.mult, op1=ALU.add),
                        reads=["Sp%d_%d" % (cc % 2, hd), bkey, "Ecol%d" % hd], writes=["Sp%d_%d" % ((cc + 1) % 2, hd)])
                P.op("act", lambda e, cc=cc, dst=dst: e.activation(Sbr[cc % 2], dst, AF.Copy),
                     reads=["Sp%d_%d" % ((cc + 1) % 2, h) for h in range(NH)], writes=["Sbr%d" % (cc % 2)])
                if cc + 2 < 8:
                    emit_ds(cc + 2)
                if cc % 2 == 1:
                    post_b(cc // 2)

        def back(i):
            b = i % 2
            xt = self.xt[b]
            xk = "xt%d" % b
            sg = sgs[i % 2]
            HS = [(hd, slice(hd * T, (hd + 1) * T)) for hd in range(NH)]
            ssb = [(ps[0], "pb0"), (ps[4], "pb4"), (ps[7], "pb7"), (ps[6], "pb6")]
            for hd, sl in HS:
                P.op("act", lambda e, sl=sl: e.activation(osq[:, sl], B3[:, sl], AF.Square), reads=[b3k[hd]], writes=["osq%d" % hd])
            for hd, sl in HS:
                P.op("pe", lambda e, sl=sl, hd=hd: e.matmul(ssb[hd][0][:, :], self.ones_b, osq[:, sl], start=True, stop=True),
                     reads=["osq%d" % hd, "cst"], writes=[ssb[hd][1]])
            for hd, sl in HS:
                P.op("dve", lambda e, sl=sl, hd=hd: e.tensor_scalar(B2[:, sl], ssb[hd][0][:, :], float(128 * EPS), None, ALU.add),
                     reads=[ssb[hd][1]], writes=[b2k[hd]])
            for hd, sl in HS:
                P.op("act", lambda e, sl=sl: e.activation(B2[:, sl], B2[:, sl], AF.Ln), reads=[b2k[hd]], writes=[b2k[hd]])
            for hd, sl in HS:
                P.op("act", lambda e, sl=sl: e.activation(B2[:, sl], B2[:, sl], AF.Exp, scale=-0.5), reads=[b2k[hd]], writes=[b2k[hd]])
            for hd, sl in HS:
                P.op("dve", lambda e, sl=sl: e.tensor_tensor(B2[:, sl], B3[:, sl], B2[:, sl], ALU.mult),
                     reads=[b3k[hd], b2k[hd]], writes=[b2k[hd]])
                P.op("dve", lambda e, sl=sl, hd=hd, sg=sg: e.scalar_tensor_tensor(
                    out=ocat[:, sl], in0=B2[:, sl], scalar=lcol(self.gnp, hd), in1=sg[:, sl], op0=ALU.mult, op1=ALU.mult),
                    reads=[b2k[hd], "sg%d_%d" % (i % 2, hd), "lbq"], writes=["ocat%d" % hd])
            okeys = ["ocat%d" % c for c in range(DC)]
            for dc in range(DC):
                bank, bk = pjbank()
                P.op("pe", lambda e, dc=dc, bank=bank: _mm_group(e, bank[:, :], [
                    (Wout[:, c * D + dc * 128:c * D + (dc + 1) * 128], ocat[:, c * T:(c + 1) * T]) for c in range(DC)]),
                    reads=okeys + ["Wo"], writes=[bk])
                P.op("dve", lambda e, dc=dc, bank=bank, xt=xt: e.tensor_tensor(
                    xt[:, dc * T:(dc + 1) * T], xt[:, dc * T:(dc + 1) * T], bank[:, :], ALU.add),
                    reads=[bk, xk], writes=[xk])
            self.store_x(i)

        self.load_x(0, False)
        if NT > 1:
            self.load_x(1, False)
        self.norm_stage(0, gidx)
        front(0)
        if NT > 1:
            self.norm_stage(1, gidx)
        for i in range(NT):
            recur(i)
            if i + 1 < NT:
                front(i + 1)
            back(i)
            if i + 2 < NT:
                self.load_x(i + 2, False)
                self.norm_stage(i + 2, gidx)

    def mix_phase(self, l, state_only=False):
        A, P, ps = self.A, self.P, self.ps
        full = not state_only
        gidx = l * 3 + 1
        WIN = 3072
        Win = A.bf16(DC * WIN)
        B1 = A.f32(NH * T)
        B2 = A.f32(NH * T)
        B3 = A.f32(NH * T)
        Ecol = A.f32(NH * 8)
        kT = A.bf16(NH * T)
        osq = A.bf16(NH * T)
        kdT = osq
        vtm = A.bf16(8 * 512)
        kdtm = A.bf16(8 * 512)
        S = A.f32(512)
        Sb = A.bf16(512)
        S2 = A.f32(512)
        Sb2 = A.bf16(512)
        if full:
            Wout = A.bf16(DC * D)
            qT = A.bf16(NH * T)
            sg = A.bf16(NH * T)
            uT = A.bf16(NH * T)
            vln = A.bf16(4 * 512)
            Bc = A.f32(512)
            WmT = A.bf16(512)
            stats = A.f32(24)
            mv = A.f32(8)
            rs4 = A.f32(4)
            ocat = A.bf16(DC * T)
            bsp = ocat.bitcast(F32)[:, 0:512]
            PTb = [A.bf16(256), A.bf16(256)]
        b1k = ["B1_%d" % h for h in range(NH)]
        b2k = ["B2_%d" % h for h in range(NH)]
        b3k = ["B3_%d" % h for h in range(NH)]
        hkeys = ["hb%d" % c for c in range(DC)]
        lcol = lambda t, hd: t[:, l * 4 + hd:l * 4 + hd + 1]
        OBs = [ps[6], ps[5]]
        obk = ["pb6", "pb5"]
        trbs = [ps[5][:, :].bitcast(BF16), ps[4][:, :].bitcast(BF16)]
        trk = ["pb5", "pb4"]

        pre = ("mix", l) in self.conv_done
        wq = "sp" if pre else "pool"
        win_src = self.wsc_in if pre else self.win[l]
        wout_src = self.wsc_out if pre else self.wout[l]
        wrd = ["wscM"] if pre else []
        if full:
            for c in range(DC):
                P.op(wq, lambda e, c=c: e.dma_start(out=Win[:, c * WIN:(c + 1) * WIN],
                                                    in_=win_src[c * 128:(c + 1) * 128, :]),
                     reads=wrd, writes=[P.subkey("W")], dma=("ldwh%d" if pre else "ldw%d") % P.slot)
            for c in range(DC):
                P.op(wq, lambda e, c=c: e.dma_start(out=Wout[:, c * D:(c + 1) * D],
                                                    in_=wout_src[c * 128:(c + 1) * 128, :]),
                     reads=wrd, writes=[P.subkey("W")], dma=("ldwh%d" if pre else "ldw%d") % P.slot)
            P.op("sp", lambda e: e.dma_start(out=bsp, in_=self.bsp_d[l]), writes=["bsp", "ocat0", "ocat1"], dma="ld_bsp")
            P.op("sp", lambda e: e.dma_start(out=B1[:, 0:512], in_=self.wst_d[l]), writes=[b1k[0]], dma="ld_wst")
            P.op("dve", lambda e: e.tensor_tensor(
                WmT.rearrange("p (g t) -> p g t", g=4), B1[:, 0:512].rearrange("p (g t) -> p g t", g=4),
                self.tri128.unsqueeze(1).broadcast_to([128, 4, 128]), ALU.mult),
                reads=[b1k[0], "cst"], writes=["WmT"])
            for g in range(4):
                P.op("pe", lambda e, g=g: e.matmul(ps[1][:, g * 128:(g + 1) * 128], self.ones_b, WmT[:, g * 128:(g + 1) * 128],
                                                    start=True, stop=True),
                     reads=["WmT", "cst"], writes=["pb1"])
                P.op("dve", lambda e, g=g: e.scalar_tensor_tensor(
                    out=Bc[:, g * 128:(g + 1) * 128], in0=ps[1][:, g * 128:(g + 1) * 128],
                    scalar=self.mixp[:, 24 + l * 4 + g:24 + l * 4 + g + 1], in1=bsp[:, g * 128:(g + 1) * 128],
                    op0=ALU.mult, op1=ALU.add),
                    reads=["pb1", "bsp", "ocat0", "ocat1", "mixp"], writes=["Bc"])
            if self.fused:
                P.op("sp", lambda e: e.dma_start(out=S, in_=self.cc_out[l][0:128, :]), reads=["ccout%d" % l],
                     writes=["Sp0_%d" % h for h in range(NH)], dma="ld_S")
                P.op("dve", lambda e: e.tensor_scalar(S, S, self.role[:, 0:1], None, ALU.mult),
                     reads=["Sp0_%d" % h for h in range(NH)] + ["role"], writes=["Sp0_%d" % h for h in range(NH)])
            else:
                P.op("sp", lambda e: e.dma_start(out=S, in_=self.s_in), writes=["Sp0_%d" % h for h in range(NH)], dma="ld_S")
        else:
            for c in range(DC):
                P.op(wq, lambda e, c=c: e.dma_start(out=Win[:, c * WIN + 512:c * WIN + 1536],
                                                    in_=win_src[c * 128:(c + 1) * 128, 512:1536]),
                     reads=wrd, writes=[P.subkey("W")], dma=("ldwh%d" if pre else "ldw%d") % P.slot)
            P.op("dve", lambda e: e.memset(S, 0.0), writes=["S%d" % h for h in range(NH)])
        skeys = ["Sp0_%d" % h for h in range(NH)]
        Sbr = [Sb, Sb2]
        P.op("act", lambda e: e.activation(Sbr[1], S, AF.Copy), reads=skeys, writes=["Sbr1"])
        self.schedule_conversion(self.cur_pi)

        pj = [0]

        def pjbank():
            k = 1 + pj[0] % 2
            pj[0] += 1
            return ps[k], "pb%d" % k

        def proj_fm(col0):
            bank, bk = pjbank()
            P.op("pe", lambda e: _mm_group(e, bank[:, :], [
                (Win[:, c * WIN + col0:c * WIN + col0 + 128], self.hb[:, c * T:(c + 1) * T]) for c in range(DC)]),
                reads=hkeys + ["W"], writes=[bk])
            return bank, bk

        self.load_x(0, False)
        if NT > 1:
            self.load_x(1, False)
        self.norm_stage(0, gidx)
        for i in range(NT):
            b = i % 2
            xt = self.xt[b]
            xk = "xt%d" % b
            if state_only:
                for hd in range(NH):
                    sl = slice(hd * T, (hd + 1) * T)
                    bank, bk = proj_fm(512 + hd * 128)
                    P.op("act", lambda e, sl=sl, bank=bank: e.activation(B1[:, sl], bank[:, :], AF.Tanh, scale=0.5),
                         reads=[bk], writes=[b1k[hd]])
                    P.op("dve", lambda e, sl=sl, hd=hd: e.tensor_scalar(B2[:, sl], B1[:, sl], lcol(self.nhoml, hd),
                                                                       lcol(self.homl, hd), ALU.mult, ALU.add),
                         reads=[b1k[hd], "lbq"], writes=[b2k[hd]])
                for cc in range(8):
                    bank, bk = pjbank()
                    P.op("pe", lambda e, cc=cc, bank=bank: _mm_group(e, bank[0:64, :], [
                        (self.hb[:, c * T + cc * CH:c * T + (cc + 1) * CH], Win[:, c * WIN + 1024:c * WIN + 1536])
                        for c in range(DC)]),
                        reads=hkeys + ["W"], writes=[bk])
                    P.op("act", lambda e, cc=cc, bank=bank: e.activation(vtm[0:64, cc * 512:(cc + 1) * 512], bank[0:64, :], AF.Copy),
                         reads=[bk], writes=["vtm%d" % cc])
                for hd in range(NH):
                    sl = slice(hd * T, (hd + 1) * T)
                    P.op("act", lambda e, sl=sl, hd=hd: e.activation(B1[:, sl], B1[:, sl], AF.Ln, bias=lcol(self.lbh, hd),
                                                                    scale=lcol(self.homl, hd)),
                         reads=[b1k[hd], "lbq"], writes=[b1k[hd]])
                    P.op("dve", lambda e, sl=sl: e.tensor_scalar_max(B1[:, sl], B1[:, sl], float(np.log(F_MIN))),
                         reads=[b1k[hd]], writes=[b1k[hd]])
                    P.op("dve", lambda e, sl=sl: e.tensor_tensor_scan(B3[:, sl], self.rmask, B1[:, sl], 0.0, ALU.mult, ALU.add),
                         reads=[b1k[hd], "cst"], writes=[b3k[hd]])
                    P.op("act", lambda e, sl=sl: e.activation(B1[:, sl], B3[:, sl], AF.Exp), reads=[b3k[hd]], writes=[b1k[hd]])
                    P.op("act", lambda e, sl=sl: e.activation(B3[:, sl], B3[:, sl], AF.Exp, scale=-1.0),
                         reads=[b3k[hd]], writes=[b3k[hd]])
                    P.op("dve", lambda e, hd=hd: e.tensor_copy(Ecol[:, hd * 8:(hd + 1) * 8], B1[:, hd * T + CH - 1:(hd + 1) * T:CH]),
                         reads=[b1k[hd]], writes=["Ecol%d" % hd])
                    P.op("dve", lambda e, sl=sl: e.tensor_tensor(B2[:, sl], B2[:, sl], B3[:, sl], ALU.mult),
                         reads=[b2k[hd], b3k[hd]], writes=[b2k[hd]])
                    P.op("act", lambda e, sl=sl: e.activation(kT[:, sl], B2[:, sl], AF.Copy), reads=[b2k[hd]], writes=["kT%d" % hd])
                    P.op("dve", lambda e, hd=hd, sl=sl: e.tensor_tensor(
                        kdT[:, sl].rearrange("p (c t) -> p c t", t=CH), B2[:, sl].rearrange("p (c t) -> p c t", t=CH),
                        Ecol[:, hd * 8:(hd + 1) * 8].unsqueeze(2).broadcast_to([128, 8, CH]), ALU.mult),
                        reads=[b2k[hd], "Ecol%d" % hd], writes=["osq%d" % hd])
                for cc in range(8):
                    half = cc % 2

                    trb = trbs[half]

                    def trf(e, cc=cc, trb=trb):
                        ins = None
                        for hd in range(NH):
                            ins = e.transpose(trb[0:64, hd * 128:(hd + 1) * 128],
                                              kdT[:, hd * T + cc * CH:hd * T + (cc + 1) * CH], self.ident_b)
                        return ins
                    P.op("pe", trf, reads=["osq%d" % hd for hd in range(NH)] + ["cst"], writes=[trk[half]])
                    if cc % 2 == 0:
                        P.op("act", lambda e, cc=cc, trb=trb: e.activation(
                            kdtm[0:64, cc * 512:(cc + 1) * 512], trb[0:64, 0:512], AF.Copy),
                            reads=[trk[half]], writes=["kdtm%d" % cc])
                    else:
                        P.op("dve", lambda e, cc=cc, trb=trb: e.tensor_copy(
                            kdtm[0:64, cc * 512:(cc + 1) * 512], trb[0:64, 0:512]),
                            reads=[trk[half]], writes=["kdtm%d" % cc])
                P.op("sp", lambda e, i=i: e.dma_start(out=self.sc_kT[i], in_=kT), reads=["kT%d" % h for h in range(NH)], dma="st_kT")
                P.op("sp", lambda e, i=i: e.dma_start(out=self.sc_kd[i], in_=kdtm[0:64, :]), reads=["kdtm%d" % c for c in range(8)], dma="st_kd")
                P.op("sp", lambda e, i=i: e.dma_start(out=self.sc_v[i], in_=vtm[0:64, :]), reads=["vtm%d" % c for c in range(8)], dma="st_v")
                P.op("sp", lambda e, i=i: e.dma_start(out=self.sc_ea[i], in_=B1), reads=b1k, dma="st_ea")
                P.op("sp", lambda e, i=i: e.dma_start(out=self.sc_E[i], in_=Ecol), reads=["Ecol%d" % h for h in range(NH)], dma="st_E")
            else:
                P.op("sp", lambda e, i=i: e.dma_start(out=kT, in_=self.sc_kT[i]), writes=["kT%d" % h for h in range(NH)], dma="ld_kT")
                P.op("sp", lambda e, i=i: e.dma_start(out=kdtm[0:64, :], in_=self.sc_kd[i]), writes=["kdtm%d" % c for c in range(8)], dma="ld_kd")
                P.op("sp", lambda e, i=i: e.dma_start(out=vtm[0:64, :], in_=self.sc_v[i]), writes=["vtm%d" % c for c in range(8)], dma="ld_v")
                P.op("sp", lambda e, i=i: e.dma_start(out=B1, in_=self.sc_ea[i]), writes=b1k, dma="ld_ea")
                P.op("sp", lambda e, i=i: e.dma_start(out=Ecol, in_=self.sc_E[i]), writes=["Ecol%d" % h for h in range(NH)], dma="ld_E")
            if full:
                for hd in range(NH):
                    sl = slice(hd * T, (hd + 1) * T)
                    bank, bk = proj_fm(1536 + hd * 128)
                    P.op("act", lambda e, sl=sl, bank=bank: e.activation(B3[:, sl], bank[:, :], AF.Tanh, scale=0.5),
                         reads=[bk], writes=[b3k[hd]])
                    P.op("dve", lambda e, sl=sl, bank=bank: e.scalar_tensor_tensor(
                        out=sg[:, sl], in0=B3[:, sl], scalar=1.0, in1=bank[:, :], op0=ALU.add, op1=ALU.mult),
                        reads=[bk, b3k[hd]], writes=["sg%d" % hd])
            if full:
                for hd in range(NH):
                    sl = slice(hd * T, (hd + 1) * T)
                    bank, bk = proj_fm(hd * 128)
                    P.op("dve", lambda e, sl=sl, bank=bank: e.tensor_tensor(qT[:, sl], bank[:, :], B1[:, sl], ALU.mult),
                         reads=[bk, b1k[hd]], writes=["qT%d" % hd])
            if full:
                for g in range(4):
                    bank, bk = proj_fm(2048 + g * 128)
                    P.op("act", lambda e, g=g, bank=bank: e.activation(uT[:, g * T:(g + 1) * T], bank[:, :], AF.Gelu_apprx_tanh),
                         reads=[bk], writes=["uT%d" % g])
                for s in range(4):
                    sl = slice(s * T, (s + 1) * T)
                    bank, bk = pjbank()
                    P.op("pe", lambda e, s=s, bank=bank: _mm_group(e, bank[:, :], [
                        (self.hb[:, c * T + s * 128:c * T + (s + 1) * 128], Win[:, c * WIN + 2560:c * WIN + 3072])
                        for c in range(DC)]),
                        reads=hkeys + ["W"], writes=[bk])
                    P.op("act", lambda e, sl=sl, bank=bank: e.activation(B1[:, sl], bank[:, :], AF.Gelu_apprx_tanh),
                         reads=[bk], writes=[b1k[s]])
                    P.op("dve", lambda e, s=s, sl=sl: e.bn_stats(stats[:, s * 6:(s + 1) * 6], B1[:, sl]),
                         reads=[b1k[s]], writes=["stats%d" % s])
                    P.op("dve", lambda e, s=s: e.bn_aggr(mv[:, s * 2:(s + 1) * 2], stats[:, s * 6:(s + 1) * 6]),
                         reads=["stats%d" % s], writes=["mv"])
                P.op("dve", lambda e: e.tensor_scalar(rs4, mv[:, 1:8:2], float(EPS), None, ALU.add), reads=["mv"], writes=["rs4"])
                P.op("act", lambda e: e.activation(rs4, rs4, AF.Ln), reads=["rs4"], writes=["rs4"])
                P.op("act", lambda e: e.activation(rs4, rs4, AF.Exp, scale=-0.5), reads=["rs4"], writes=["rs4"])
                if i + 1 < NT:
                    self.norm_stage(i + 1, gidx)
                for s in range(4):
                    sl = slice(s * T, (s + 1) * T)
                    P.op("dve", lambda e, s=s, sl=sl: e.tensor_scalar(vln[:, sl], B1[:, sl], mv[:, 2 * s:2 * s + 1], rs4[:, s:s + 1],
                                                                     ALU.subtract, ALU.mult),
                         reads=[b1k[s], "mv", "rs4"], writes=["vln%d" % s])
            if state_only and i + 1 < NT:
                self.norm_stage(i + 1, gidx)

            def post_b(g):
                bank, bk = pjbank()

                def spf(e, g=g, bank=bank):
                    ins = None
                    for s in range(4):
                        ins = e.matmul(bank[:, s * 128:(s + 1) * 128], vln[:, s * 512 + g * 128:s * 512 + (g + 1) * 128],
                                       WmT[:, g * 128:(g + 1) * 128], start=True, stop=True)
                    return ins
                P.op("pe", spf, reads=["vln%d" % s for s in range(4)] + ["WmT"], writes=[bk])
                gl = slice(g * T, (g + 1) * T)
                P.op("dve", lambda e, g=g, gl=gl, bank=bank: e.scalar_tensor_tensor(
                    out=B1[:, gl].rearrange("p (s t) -> p s t", s=4), in0=bank[:, :].rearrange("p (s t) -> p s t", s=4),
                    scalar=self.mixp[:, 16 + l * 4 + g:16 + l * 4 + g + 1],
                    in1=Bc[:, g * 128:(g + 1) * 128].unsqueeze(1).broadcast_to([128, 4, 128]),
                    op0=ALU.mult, op1=ALU.add),
                    reads=[bk, "Bc", "mixp"], writes=[b1k[g]])
                P.op("dve", lambda e, g=g, gl=gl: e.tensor_tensor(ocat[:, (4 + g) * T:(5 + g) * T], B1[:, gl], uT[:, gl], ALU.mult),
                     reads=[b1k[g], "uT%d" % g], writes=["ocat%d" % (4 + g)])

            PTv = B2.bitcast(BF16)
            PT = [PTv[0:64, cc * 256:(cc + 1) * 256] for cc in range(8)]
            scb = [(ps[4], "pb4"), (ps[5], "pb5")]
            dsb = [(ps[7], "pb7"), (ps[3], "pb3")]
            obb = [(ps[6], "pb6"), (ps[0], "pb0")]
            Sp = [S, S2]
            csl = lambda hd, cc: slice(hd * T + cc * CH, hd * T + (cc + 1) * CH)
            for cc in range(8):
                bank, bkey = scb[cc % 2]

                def scf(e, cc=cc, bank=bank):
                    ins = None
                    for hd in range(NH):
                        ins = e.matmul(bank[0:64, hd * CH:(hd + 1) * CH], kT[:, csl(hd, cc)], qT[:, csl(hd, cc)], start=True, stop=True)
                    return ins
                P.op("pe", scf, reads=["kT%d" % h for h in range(NH)] + ["qT%d" % h for h in range(NH)], writes=[bkey])
                P.op("dve", lambda e, cc=cc, bank=bank: e.tensor_tensor(PT[cc], bank[0:64, 0:256], self.tri64, ALU.mult),
                     reads=[bkey, "cst"], writes=["PT%d" % cc, b2k[0], b2k[1]])

            def emit_ds(cc):
                bank, bkey = dsb[cc % 2]

                def dsf(e, cc=cc, bank=bank):
                    ins = None
                    for hd in range(NH):
                        ins = e.matmul(bank[:, hd * 128:(hd + 1) * 128], kdtm[0:64, cc * 512 + hd * 128:cc * 512 + (hd + 1) * 128],
                                       vtm[0:64, cc * 512 + hd * 128:cc * 512 + (hd + 1) * 128], start=True, stop=True)
                    return ins
                P.op("pe", dsf, reads=["kdtm%d" % cc, "vtm%d" % cc], writes=[bkey])

            emit_ds(0)
            emit_ds(1)
            for cc in range(8):
                OB, okey = obb[cc % 2]
                sbp = Sbr[(cc - 1) % 2]

                def of(e, cc=cc, OB=OB, sbp=sbp):
                    ins = None
                    for hd in range(NH):
                        o = OB[:, hd * CH:(hd + 1) * CH]
                        e.matmul(o, sbp[:, hd * 128:(hd + 1) * 128], qT[:, csl(hd, cc)], start=True, stop=False)
                        ins = e.matmul(o, vtm[0:64, cc * 512 + hd * 128:cc * 512 + (hd + 1) * 128],
                                       PT[cc][:, hd * CH:(hd + 1) * CH], start=False, stop=True)
                    return ins
                P.op("pe", of, reads=["Sbr%d" % ((cc - 1) % 2), "vtm%d" % cc, "PT%d" % cc] + ["qT%d" % h for h in range(NH)],
                     writes=[okey])
                P.op("act", lambda e, cc=cc, OB=OB: e.activation(
                    B3.rearrange("p (h t) -> p h t", h=NH)[:, :, cc * CH:(cc + 1) * CH],
                    OB[:, 0:256].rearrange("p (h t) -> p h t", h=NH), AF.Copy),
                    reads=[okey], writes=b3k)
                bank, bkey = dsb[cc % 2]
                src, dst = Sp[cc % 2], Sp[(cc + 1) % 2]
                for hd in range(NH):
                    P.op("dve", lambda e, hd=hd, cc=cc, bank=bank, src=src, dst=dst: e.scalar_tensor_tensor(
                        out=dst[:, hd * 128:(hd + 1) * 128], in0=src[:, hd * 128:(hd + 1) * 128],
                        scalar=Ecol[:, hd * 8 + cc:hd * 8 + cc + 1], in1=bank[:, hd * 128:(hd + 1) * 128],
                        op0=ALU.mult, op1=ALU.add),
                        reads=["Sp%d_%d" % (cc % 2, hd), bkey, "Ecol%d" % hd], writes=["Sp%d_%d" % ((cc + 1) % 2, hd)])
                P.op("act", lambda e, cc=cc, dst=dst: e.activation(Sbr[cc % 2], dst, AF.Copy),
                     reads=["Sp%d_%d" % ((cc + 1) % 2, h) for h in range(NH)], writes=["Sbr%d" % (cc % 2)])
                if cc + 2 < 8:
                    emit_ds(cc + 2)
                if cc % 2 == 1:
                    post_b(cc // 2)
            if full:
                HS = [(hd, slice(hd * T, (hd + 1) * T)) for hd in range(NH)]
                ssb = [(ps[0], "pb0"), (ps[4], "pb4"), (ps[7], "pb7"), (ps[6], "pb6")]
                for hd, sl in HS:
                    P.op("act", lambda e, sl=sl: e.activation(osq[:, sl], B3[:, sl], AF.Square),
                         reads=[b3k[hd]], writes=["osq%d" % hd])
                for hd, sl in HS:
                    P.op("pe", lambda e, sl=sl, hd=hd: e.matmul(ssb[hd][0][:, :], self.ones_b, osq[:, sl], start=True, stop=True),
                         reads=["osq%d" % hd, "cst"], writes=[ssb[hd][1]])
                for hd, sl in HS:
                    P.op("dve", lambda e, sl=sl, hd=hd: e.tensor_scalar(B2[:, sl], ssb[hd][0][:, :], float(128 * EPS), None, ALU.add),
                         reads=[ssb[hd][1]], writes=[b2k[hd]])
                for hd, sl in HS:
                    P.op("act", lambda e, sl=sl: e.activation(B2[:, sl], B2[:, sl], AF.Ln), reads=[b2k[hd]], writes=[b2k[hd]])
                for hd, sl in HS:
                    P.op("act", lambda e, sl=sl: e.activation(B2[:, sl], B2[:, sl], AF.Exp, scale=-0.5), reads=[b2k[hd]], writes=[b2k[hd]])
                for hd, sl in HS:
                    P.op("dve", lambda e, sl=sl: e.tensor_tensor(B2[:, sl], B3[:, sl], B2[:, sl], ALU.mult),
                         reads=[b3k[hd], b2k[hd]], writes=[b2k[hd]])
                    P.op("dve", lambda e, sl=sl, hd=hd: e.scalar_tensor_tensor(
                        out=ocat[:, sl], in0=B2[:, sl], scalar=lcol(self.gnp, hd), in1=sg[:, sl], op0=ALU.mult, op1=ALU.mult),
                        reads=[b2k[hd], "sg%d" % hd, "lbq"], writes=["ocat%d" % hd])
                okeys = ["ocat%d" % c for c in range(DC)]
                for dc in range(DC):
                    bank, bk = pjbank()
                    P.op("pe", lambda e, dc=dc, bank=bank: _mm_group(e, bank[:, :], [
                        (Wout[:, c * D + dc * 128:c * D + (dc + 1) * 128], ocat[:, c * T:(c + 1) * T]) for c in range(DC)]),
                        reads=okeys + ["W"], writes=[bk])
                    P.op("dve", lambda e, dc=dc, bank=bank, xt=xt: e.tensor_tensor(
                        xt[:, dc * T:(dc + 1) * T], xt[:, dc * T:(dc + 1) * T], bank[:, :], ALU.add),
                        reads=[bk, xk], writes=[xk])
                self.store_x(i)
            if i + 2 < NT:
                self.load_x(i + 2, False)
        if state_only:
            if self.fused:
                groups = [[2 * k, 2 * k + 1] for k in range(NCORES // 2)]
                P.op("sp", lambda e: e.dma_start(out=self.cc_in[l], in_=S), reads=skeys, writes=["ccin%d" % l], dma="st_s")
                P.op("pool", lambda e: e.collective_compute("AllGather", ALU.bypass, replica_groups=groups,
                                                            ins=[self.cc_in[l]], outs=[self.cc_out[l]]),
                     reads=["ccin%d" % l], writes=["ccout%d" % l], dma="cc%d" % l, inc=1)
            else:
                P.op("sp", lambda e: e.dma_start(out=self.s_out, in_=S), reads=skeys, dma="st_s")

    def post_phase(self):
        A, P = self.A, self.P
        ps = self.ps
        xns = [A.f32(DC * T), A.f32(DC * T)]
        gidx = 6
        self.load_x(0, False)
        if NT > 1:
            self.load_x(1, False)
        for i in range(NT):
            b = i % 2
            xt = self.xt[b]
            xn = xns[b]
            if self.debug_raw_out:
                src = xt
                skey = "xt%d" % b
            else:
                hb = self.hb
                hkeys = ["hb%d" % c for c in range(DC)]
                P.op("act", lambda e, xt=xt: e.activation(hb, xt, AF.Square), reads=["xt%d" % b], writes=hkeys)
                P.op("pe", lambda e: _mm_group(e, ps[0][:, :], [
                    (self.ones_b, self.hb[:, c * T:(c + 1) * T]) for c in range(DC)]),
                    reads=hkeys + ["cst"], writes=["pb0"])
                self.rstd_from(ps[0][:, :], "pb0", float(D * EPS))
                for c in range(DC):
                    P.op("dve", lambda e, c=c, xt=xt, xn=xn: e.scalar_tensor_tensor(
                        out=xn[:, c * T:(c + 1) * T], in0=xt[:, c * T:(c + 1) * T],
                        scalar=self.g32[:, gidx * DC + c:gidx * DC + c + 1], in1=self.rstd,
                        op0=ALU.mult, op1=ALU.mult),
                        reads=["xt%d" % b, "rstd", "g32"], writes=["xn%d" % b])
                src = xn
                skey = "xn%d" % b
            P.op("sp", lambda e, i=i, src=src: e.dma_start(out=self.out[i], in_=src), reads=[skey], dma="sto%d" % b)
            if i + 2 < NT:
                self.load_x(i + 2, False)


def make_consts():
    import ml_dtypes
    c = np.zeros((128, 1152), np.float32)
    c[:, 0:128] = np.eye(128, dtype=np.float32)
    idb = np.eye(128, dtype=np.float32).astype(ml_dtypes.bfloat16)
    c[:, 128:192] = idb.view(np.float32)
    onb = np.ones((128, 128), np.float32).astype(ml_dtypes.bfloat16)
    c[:, 192:256] = onb.view(np.float32)
    tri = (np.arange(CH)[:, None] <= np.arange(CH)[None, :]).astype(np.float32)
    c[0:CH, 256:512] = np.tile(tri, (1, NH))
    c[:, 512:640] = (np.arange(128)[:, None] <= np.arange(128)[None, :]).astype(np.float32)
    rm = np.ones((T,), np.float32)
    rm[0::CH] = 0.0
    c[:, 640:1152] = rm[None, :]
    return c


def pack_gains(inp):
    g = np.zeros((128, 7, DC), np.float32)
    vecs = [inp["norm_ffn1"][0], inp["norm_mix"][0], inp["norm_ffn2"][0],
            inp["norm_ffn1"][1], inp["norm_mix"][1], inp["norm_ffn2"][1], inp["norm_final"]]
    for k, v in enumerate(vecs):
        g[:, k, :] = np.asarray(v, np.float32).reshape(DC, 128).T
    return np.ascontiguousarray(g.reshape(128, 7 * DC))


def pack_small(inp):
    f = lambda k: np.asarray(inp[k], np.float32)
    mixp = np.zeros((128, 32), np.float32)
    mixp[:, 0:8] = f("lb_param").reshape(DEPTH, NH, 128).transpose(2, 0, 1).reshape(128, 8)
    mixp[:, 8:16] = f("hgrn_norm").reshape(DEPTH, NH, 128).transpose(2, 0, 1).reshape(128, 8)
    mixp[:, 16:24] = f("ln_v_gain").reshape(DEPTH, 4, 128).transpose(2, 0, 1).reshape(128, 8)
    mixp[:, 24:32] = f("ln_v_bias").reshape(DEPTH, 4, 128).transpose(2, 0, 1).reshape(128, 8)
    wst = np.ascontiguousarray(f("w_spatial").transpose(0, 3, 1, 2).reshape(DEPTH, 128, 512))
    bsp = np.ascontiguousarray(np.broadcast_to(f("b_spatial").reshape(DEPTH, 1, 512), (DEPTH, 128, 512)))
    return {"mixp": mixp, "wst": wst, "bsp": bsp}


_CACHE = {}


def launch(phases, inputs, xs_in=None, s_in=None, xs_out=False, debug_raw_out=False, trace=False, fused=False, scopes=False):
    b = Builder(phases, debug_raw_out=debug_raw_out, xs_in=xs_in is not None, xs_out=xs_out, fused=fused, scopes=scopes)
    nc = b.build()
    f = lambda k: np.asarray(inputs[k], np.float32)
    common = {"gains": pack_gains(inputs), "cst": make_consts()}
    common.update(pack_small(inputs))
    if 1 in b.need_f:
        common.update({"wg1": f("ffn1_w_gate"), "wu1": f("ffn1_w_up"), "wd1": f("ffn1_w_down")})
    if 2 in b.need_f:
        common.update({"wg2": f("ffn2_w_gate"), "wu2": f("ffn2_w_up"), "wd2": f("ffn2_w_down")})
    if b.need_m:
        common.update({"w_in": f("w_in"), "w_out": f("w_out")})
    x = np.ascontiguousarray(f("x").reshape(NCORES, NT, T, DC, 128).transpose(0, 1, 4, 3, 2).reshape(NCORES, NT, 128, DC * T))
    in_maps = []
    for r in range(NCORES):
        m = dict(common)
        m["x_in"] = x[r]
        if fused:
            m["role"] = np.full((128, 1), float(r % 2), np.float32)
        else:
            m["s_in"] = np.zeros((128, 512), np.float32) if s_in is None else np.ascontiguousarray(s_in[r])
        if xs_in is not None:
            m["xs_in"] = np.ascontiguousarray(xs_in[r])
        in_maps.append(m)
    res = run_bass_kernel_spmd(nc, in_maps, core_ids=list(range(NCORES)), trace=trace)
    outs = {k: np.stack([np.asarray(r[k]) for r in res.results], 0) for k in res.results[0].keys()}
    return outs, res


def handoff(s_out):
    s_in = np.zeros_like(s_out)
    s_in[1::2] = s_out[0::2]
    return s_in


FUSED_PHASES = [("ffn", 0, 1, True), ("mix", 0, True), ("mix", 0, False), ("ffn", 0, 2, False),
                ("ffn", 1, 1, False), ("mix", 1, True), ("mix", 1, False), ("ffn", 1, 2, False), ("post",)]


def kernel(**inputs):
    o, _ = launch(FUSED_PHASES, inputs, fused=True)
    return from_fm(o["out"])


def from_fm(o):
    o = np.asarray(o, np.float32).reshape(NCORES, NT, 128, DC, T).transpose(0, 1, 4, 3, 2)
    return np.ascontiguousarray(o.reshape(4, 8192, D))
```

```python
import numpy as np
from contextlib import ExitStack

import concourse.bass as bass
import concourse.mybir as mybir
from concourse.bass_utils import run_bass_kernel_spmd

F32 = mybir.dt.float32
BF16 = mybir.dt.bfloat16
AF = mybir.ActivationFunctionType
ALU = mybir.AluOpType

NCORES = 8
D = 1024
DC = 8
FF = 2816
FC = 22
T = 512
NT = 8
TOK = NT * T
DEPTH = 2
EPS = 1e-6
F_MIN = 1e-20
NH = 4
CH = 64

ENGS = ("pe", "act", "dve", "pool", "sp")


class Prog:
    def __init__(self, nc, stack):
        self.nc = nc
        self.q = {e: [] for e in ENGS}
        self.sem = {}
        self.cnt = {}
        self.stack = stack
        for e in ("pe", "act", "dve", "pool"):
            self.sem[e] = stack.enter_context(nc.semaphore("sem_" + e))
            self.cnt[e] = 0
        self.lastw = {}
        self.readers = {}
        self.waited = {e: {} for e in ENGS}
        self.scope = None
        self.use_scopes = False
        self.alias = {}
        self._uid = 0
        self._rot = {}
        self.MAXFLY = 4

    def subkey(self, group):
        self._uid += 1
        lst = self.alias.setdefault(group, [])
        if len(lst) < self.MAXFLY:
            k = "%s#%d" % (group, self._uid)
            self.slot = len(lst)
            lst.append(k)
            self._rot[group] = 0
            return k
        i = self._rot.get(group, 0)
        self._rot[group] = (i + 1) % len(lst)
        self.slot = i
        return lst[i]

    def reset_group(self, group):
        self.alias[group] = []

    def _sem(self, key):
        if key not in self.sem:
            self.sem[key] = self.stack.enter_context(self.nc.semaphore("sem_" + key))
            self.cnt[key] = 0
        return self.sem[key]

    def _expand(self, keys):
        out = []
        for k in keys:
            if k in self.alias:
                out.extend(self.alias[k])
            else:
                out.append(k)
        return out

    def op(self, eng, fn, reads=(), writes=(), dma=None, inc=16):
        reads = self._expand(reads)
        writes = self._expand(writes)
        evs = []
        for k in reads:
            if k in self.lastw:
                evs.append(self.lastw[k])
        for k in writes:
            if k in self.lastw:
                evs.append(self.lastw[k])
            evs.extend(self.readers.get(k, ()))
        best = {}
        for (s, v, src) in evs:
            if eng == "pe" and src == "pe":
                continue
            if v > best.get(s, 0):
                best[s] = v
        waits = []
        for s, v in best.items():
            if v > self.waited[eng].get(s, 0):
                self.waited[eng][s] = v
                waits.append((s, v))
        if dma is None:
            self.cnt[eng] += 1
            ev = (eng, self.cnt[eng], eng)
        else:
            self._sem(dma)
            self.cnt[dma] += inc
            ev = (dma, self.cnt[dma], "dma%d" % inc)
        self.q[eng].append((fn, waits, ev, self.scope))
        for k in reads:
            self.readers.setdefault(k, []).append(ev)
        for k in writes:
            self.lastw[k] = ev
            self.readers[k] = []
        return ev

    def barrier(self):
        allev = []
        for e in ("pe", "act", "dve", "pool"):
            if self.cnt[e] > 0:
                allev.append((e, self.cnt[e]))
        for k in self.cnt:
            if k not in ("pe", "act", "dve", "pool") and self.cnt[k] > 0 and not k.startswith("bg_") and not k.startswith("cc"):
                allev.append((k, self.cnt[k]))
        for eng in ENGS:
            waits = []
            for s, v in allev:
                if s == eng:
                    continue
                if v > self.waited[eng].get(s, 0):
                    self.waited[eng][s] = v
                    waits.append((s, v))
            if waits:
                self.q[eng].append((None, waits, None, None))
        self.lastw = {k: v for k, v in self.lastw.items() if k.startswith("wsc") or k.startswith("ccout")}
        self.readers = {k: v for k, v in self.readers.items() if k.startswith("wsc") or k.startswith("ccout")}

    def final_wait(self, eng="sp"):
        waits = []
        for k in self.cnt:
            if self.cnt[k] > 0 and k != eng:
                waits.append((k, self.cnt[k]))
        self.q[eng].append((None, waits, None, None))

    def _run(self, e, engine):
        cur = None
        for fn, waits, ev, scope in self.q[e]:
            if self.use_scopes and fn is not None and scope != cur:
                if cur is not None:
                    self.nc.leave_named_scope(cur, cur_id, False)
                cur = scope
                if cur is not None:
                    cur_id, _ = self.nc.enter_named_scope(cur, False)
            for s, v in waits:
                engine.wait_ge(self.sem[s], v)
            if fn is None:
                continue
            ins = fn(engine)
            if ev[2] == "dma16":
                ins.then_inc(self.sem[ev[0]], 16)
            else:
                ins.then_inc(self.sem[ev[0]], 1)
        if self.use_scopes and cur is not None:
            self.nc.leave_named_scope(cur, cur_id, False)

    def emit(self):
        with self.nc.Block() as blk:

            @blk.tensor
            def _(e):
                self._run("pe", e)

            @blk.scalar
            def _(e):
                self._run("act", e)

            @blk.vector
            def _(e):
                self._run("dve", e)

            @blk.gpsimd
            def _(e):
                self._run("pool", e)

            @blk.sync
            def _(e):
                self._run("sp", e)


class Arena:
    def __init__(self, t, ncols):
        self.t = t
        self.n = ncols
        self.off = 0
        self.marks = []

    def f32(self, cols):
        assert self.off + cols <= self.n, ("arena overflow", self.off, cols, self.n)
        ap = self.t[:, self.off:self.off + cols]
        self.off += cols
        return ap

    def bf16(self, cols):
        c32 = (cols + 1) // 2
        ap = self.f32(c32)
        return ap.bitcast(BF16)

    def mark(self):
        self.marks.append(self.off)

    def release(self):
        self.off = self.marks.pop()


def _mm_group(pe, out, pairs):
    n = len(pairs)
    ins = None
    for idx, (l, r) in enumerate(pairs):
        ins = pe.matmul(out, l, r, start=(idx == 0), stop=(idx == n - 1))
    return ins


class Builder:
    def __init__(self, phases, debug_raw_out=False, xs_in=False, xs_out=False, fused=False, scopes=False, prefetch=True):
        self.phases = phases
        self.debug_raw_out = debug_raw_out
        self.use_xs_in = xs_in
        self.scopes = scopes
        self.prefetch = prefetch
        self.use_xs_out = xs_out
        self.fused = fused
        if fused:
            self.nc = bass.Bass("TRN2", target_bir_lowering=False, num_devices=NCORES)
        else:
            self.nc = bass.Bass("TRN2", target_bir_lowering=False)
        nc = self.nc
        self.stack = ExitStack()
        dt = nc.dram_tensor
        need_f = set(ph[2] for ph in phases if ph[0] == "ffn")
        need_m = any(ph[0] == "mix" for ph in phases)
        self.need_f, self.need_m = need_f, need_m
        self.x_in = dt("x_in", [NT, 128, DC * T], F32, kind="ExternalInput").ap()
        self.out = dt("out", [NT, 128, DC * T], F32, kind="ExternalOutput").ap()
        self.xs = dt("xs", [NT, 128, DC * T], F32, kind="ExternalOutput" if xs_out else "Internal").ap()
        self.xs_in = dt("xs_in", [NT, 128, DC * T], F32, kind="ExternalInput").ap() if xs_in else None
        self.xs_src = self.xs_in if xs_in else self.xs
        self.win = dt("w_in", [DEPTH, D, 3072], F32, kind="ExternalInput").ap() if need_m else None
        self.wout = dt("w_out", [DEPTH, D, D], F32, kind="ExternalInput").ap() if need_m else None
        self.mixp_d = dt("mixp", [128, 32], F32, kind="ExternalInput").ap()
        self.wst_d = dt("wst", [DEPTH, 128, 512], F32, kind="ExternalInput").ap()
        self.bsp_d = dt("bsp", [DEPTH, 128, 512], F32, kind="ExternalInput").ap()
        self.sc_kT = dt("sc_kT", [NT, 128, NH * T], BF16, kind="Internal").ap()
        self.sc_kd = dt("sc_kd", [NT, 64, 8 * 512], BF16, kind="Internal").ap()
        self.sc_v = dt("sc_v", [NT, 64, 8 * 512], BF16, kind="Internal").ap()
        self.sc_ea = dt("sc_ea", [NT, 128, NH * T], F32, kind="Internal").ap()
        self.sc_E = dt("sc_E", [NT, 128, NH * 8], F32, kind="Internal").ap()
        self.wsc_g = dt("wsc_g", [D, FF], BF16, kind="Internal").ap()
        self.wsc_u = dt("wsc_u", [D, FF], BF16, kind="Internal").ap()
        self.wsc_d = dt("wsc_d", [FF, D], BF16, kind="Internal").ap()
        self.wsc_in = dt("wsc_in", [D, 3072], BF16, kind="Internal").ap()
        self.wsc_out = dt("wsc_out", [D, D], BF16, kind="Internal").ap()
        if fused:
            self.cc_in = [dt("cc_in%d" % l, [128, 512], F32, kind="Internal").ap() for l in range(DEPTH)]
            self.cc_out = [dt("cc_out%d" % l, [256, 512], F32, kind="Internal").ap() for l in range(DEPTH)]
            self.role_d = dt("role", [128, 1], F32, kind="ExternalInput").ap()
        else:
            self.s_in = dt("s_in", [128, 512], F32, kind="ExternalInput").ap()
            self.s_out = dt("s_out", [128, 512], F32, kind="ExternalOutput").ap()
        self.wg = [dt("wg%d" % k, [DEPTH, D, FF], F32, kind="ExternalInput").ap() if k in need_f else None for k in (1, 2)]
        self.wu = [dt("wu%d" % k, [DEPTH, D, FF], F32, kind="ExternalInput").ap() if k in need_f else None for k in (1, 2)]
        self.wd = [dt("wd%d" % k, [DEPTH, FF, D], F32, kind="ExternalInput").ap() if k in need_f else None for k in (1, 2)]
        self.gains_d = dt("gains", [128, 7 * DC], F32, kind="ExternalInput").ap()
        self.cst_d = dt("cst", [128, 1152], F32, kind="ExternalInput").ap()

    def build(self):
        nc = self.nc
        st = self.stack
        with st:
            NCOL = 52000
            arena_t = st.enter_context(nc.sbuf_tensor("arena", [128, NCOL], F32))
            self.A = Arena(arena_t, NCOL)
            self.ps = [st.enter_context(nc.psum_tensor("ps%d" % k, [128, 512], F32)) for k in range(8)]
            self.P = Prog(nc, st)
            self.setup_persistent()
            self.P.use_scopes = self.scopes
            self.conv_done = set()
            for pi, ph in enumerate(self.phases):
                self.P.barrier()
                self.P.scope = "p%d_%s" % (pi, "_".join(str(v) for v in ph))
                self.P.reset_group("W")
                self.P.reset_group("Wa")
                self.P.reset_group("Wb")
                self.P.reset_group("Wo")
                self.A.mark()
                kind = ph[0]
                self.cur_pi = pi
                if kind == "ffn":
                    self.ffn_phase(ph[1], ph[2], first=ph[3])
                elif kind == "post":
                    self.post_phase()
                elif kind == "mix":
                    if ph[2]:
                        self.state_phase(ph[1])
                    else:
                        self.mix_phase2(ph[1])
                else:
                    raise ValueError(kind)
                self.A.release()
                if kind == "ffn" or (kind == "mix" and not ph[2]):
                    self.xs_src = self.xs
            self.P.final_wait("sp")
            self.P.emit()
        return nc

    def convert_ffn(self, l, which):
        P = self.P
        wg_d, wu_d, wd_d = self.wg[which - 1], self.wu[which - 1], self.wd[which - 1]
        oldk = list(P.alias.get("wscF", []))
        P.reset_group("wscF")
        for c in range(DC):
            r = slice(c * 128, (c + 1) * 128)
            P.op("pool", lambda e, r=r: e.dma_start(out=self.wsc_g[r, :], in_=wg_d[l, r, :]), reads=["W", "Wa", "Wb", "Wo"],
                 writes=[P.subkey("wscF")] + (oldk if c == 0 else []), dma="bg_F%d" % P.slot)
            P.op("pool", lambda e, r=r: e.dma_start(out=self.wsc_u[r, :], in_=wu_d[l, r, :]), reads=["W", "Wa", "Wb", "Wo"], writes=[P.subkey("wscF")], dma="bg_F%d" % P.slot)
        for j in range(0, FC, 2):
            r = slice(j * 128, (j + 2) * 128)
            P.op("pool", lambda e, r=r: e.dma_start(out=self.wsc_d[r, :], in_=wd_d[l, r, :]), reads=["W", "Wa", "Wb", "Wo"], writes=[P.subkey("wscF")], dma="bg_F%d" % P.slot)
        self.conv_done.add(("ffn", l, which))

    def convert_mix(self, l):
        P = self.P
        oldk = list(P.alias.get("wscM", []))
        P.reset_group("wscM")
        for c in range(DC):
            r = slice(c * 128, (c + 1) * 128)
            P.op("pool", lambda e, r=r: e.dma_start(out=self.wsc_in[r, :], in_=self.win[l, r, :]), reads=["W", "Wa", "Wb", "Wo"],
                 writes=[P.subkey("wscM")] + (oldk if c == 0 else []), dma="bg_M%d" % P.slot)
        for c in range(0, DC, 2):
            r = slice(c * 128, (c + 2) * 128)
            P.op("pool", lambda e, r=r: e.dma_start(out=self.wsc_out[r, :], in_=self.wout[l, r, :]), reads=["W", "Wa", "Wb", "Wo"], writes=[P.subkey("wscM")], dma="bg_M%d" % P.slot)
        self.conv_done.add(("mix", l))

    def schedule_conversion(self, pi):
        if not self.prefetch:
            return
        for ph in self.phases[pi + 1:]:
            if ph[0] == "ffn":
                key = ("ffn", ph[1], ph[2])
                if key not in self.conv_done:
                    if self.phases[pi][0] == "mix" and self.phases[pi][2]:
                        return
                    self.convert_ffn(ph[1], ph[2])
                return
            if ph[0] == "mix":
                key = ("mix", ph[1])
                if key not in self.conv_done:
                    self.convert_mix(ph[1])
                    return
                continue
            return

    def setup_persistent(self):
        A, P = self.A, self.P
        self.cst = A.f32(1152)
        self.tri64 = self.cst[0:64, 256:512]
        self.tri128 = self.cst[:, 512:640]
        self.rmask = self.cst[:, 640:1152]
        self.ident_f = self.cst[:, 0:128]
        self.ident_b = self.cst[:, 128:192].bitcast(BF16)
        self.ones_b = self.cst[:, 192:256].bitcast(BF16)
        self.gains = A.f32(7 * DC)
        self.g32 = A.f32(7 * DC)
        self.xt = [A.f32(DC * T), A.f32(DC * T)]
        self.hb = A.bf16(DC * T)
        self.rstd = A.f32(T)
        P.op("sp", lambda e: e.dma_start(out=self.cst, in_=self.cst_d), writes=["cst"], dma="ld_c")
        P.op("sp", lambda e: e.dma_start(out=self.gains, in_=self.gains_d), writes=["gains"], dma="ld_g")
        if self.fused:
            self.role = A.f32(1)
            P.op("sp", lambda e: e.dma_start(out=self.role, in_=self.role_d), writes=["role"], dma="ld_role")
        self.mixp = A.f32(32)
        sc = A.f32(64)
        self.lb = A.f32(8)
        self.homl = A.f32(8)
        self.nhoml = A.f32(8)
        self.lbh = A.f32(8)
        self.gnp = A.f32(8)
        P.op("sp", lambda e: e.dma_start(out=self.mixp, in_=self.mixp_d), writes=["mixp"], dma="ld_mp")
        l0, l1 = self.mixp[:, 0:4], self.mixp[:, 4:8]
        m, d0, d1, e0, e1, ss_, r_, p0, p1, c1 = [sc[:, 4 * k:4 * k + 4] for k in range(10)]
        dv = lambda fn, rd, wr: P.op("dve", fn, reads=rd, writes=wr)
        dv(lambda e: e.tensor_tensor(m, l0, l1, ALU.max), ["mixp"], ["sc"])
        dv(lambda e: e.tensor_tensor(d0, l0, m, ALU.subtract), ["mixp", "sc"], ["sc"])
        dv(lambda e: e.tensor_tensor(d1, l1, m, ALU.subtract), ["mixp", "sc"], ["sc"])
        P.op("act", lambda e: e.activation(e0, d0, AF.Exp), reads=["sc"], writes=["sc"])
        P.op("act", lambda e: e.activation(e1, d1, AF.Exp), reads=["sc"], writes=["sc"])
        dv(lambda e: e.tensor_tensor(ss_, e0, e1, ALU.add), ["sc"], ["sc"])
        dv(lambda e: e.reciprocal(r_, ss_), ["sc"], ["sc"])
        dv(lambda e: e.tensor_tensor(p0, e0, r_, ALU.mult), ["sc"], ["sc"])
        dv(lambda e: e.tensor_tensor(p1, e1, r_, ALU.mult), ["sc"], ["sc"])
        dv(lambda e: e.tensor_tensor(c1, p0, p1, ALU.add), ["sc"], ["sc"])
        dv(lambda e: e.tensor_tensor(self.lb[:, 0:4], p0, p0, ALU.subtract), ["sc"], ["lbp"])
        dv(lambda e: e.tensor_tensor(self.lb[:, 4:8], c1, p0, ALU.subtract), ["sc"], ["lbp"])
        dv(lambda e: e.tensor_scalar(self.homl, self.lb, -0.5, 0.5, ALU.mult, ALU.add), ["lbp"], ["lbq"])
        dv(lambda e: e.tensor_scalar(self.nhoml, self.lb, 0.5, -0.5, ALU.mult, ALU.add), ["lbp"], ["lbq"])
        dv(lambda e: e.tensor_scalar(self.lbh, self.lb, 0.5, 0.5, ALU.mult, ALU.add), ["lbp"], ["lbq"])
        dv(lambda e: e.tensor_scalar(self.gnp, self.mixp[:, 8:16], float(0.5 * np.sqrt(128.0)), None, ALU.mult),
           ["mixp"], ["lbq"])
        P.op("dve", lambda e: e.tensor_scalar(self.g32, self.gains, 32.0, None, ALU.mult),
             reads=["gains"], writes=["g32"])

    def load_x(self, i, first):
        P = self.P
        b = i % 2
        xt = self.xt[b]
        if not first:
            src = self.xs_src
            P.op("sp", lambda e: e.dma_start(out=xt, in_=src[i]), writes=["xt%d" % b], dma="ldx%d" % b)
            return
        P.op("sp", lambda e: e.dma_start(out=xt, in_=self.x_in[i]), writes=["xt%d" % b], dma="ldx%d" % b)

    def store_x(self, i):
        P = self.P
        b = i % 2
        xt = self.xt[b]
        P.op("sp", lambda e: e.dma_start(out=self.xs[i], in_=xt), reads=["xt%d" % b], dma="stx%d" % b)

    def rstd_from(self, src, skey, c, dst=None, dkey="rstd"):
        P = self.P
        if dst is None:
            dst = self.rstd
        P.op("dve", lambda e: e.tensor_scalar(dst, src, c, None, ALU.add), reads=[skey], writes=[dkey])
        P.op("act", lambda e: e.activation(dst, dst, AF.Ln), reads=[dkey], writes=[dkey])
        P.op("act", lambda e: e.activation(dst, dst, AF.Exp, scale=-0.5), reads=[dkey], writes=[dkey])

    def norm_stage(self, i, gidx, hb=None, hpre="hb"):
        P = self.P
        b = i % 2
        xt = self.xt[b]
        if hb is None:
            hb = self.hb
        xk = "xt%d" % b
        hkeys = ["%s%d" % (hpre, c) for c in range(DC)]
        P.op("act", lambda e: e.activation(hb, xt, AF.Square), reads=[xk], writes=hkeys)
        P.op("pe", lambda e: _mm_group(e, self.ps[0][:, :], [
            (self.ones_b, hb[:, c * T:(c + 1) * T]) for c in range(DC)]),
            reads=hkeys + ["cst"], writes=["pb0"])
        self.rstd_from(self.ps[0][:, :], "pb0", float(D * EPS))
        for c in range(DC):
            P.op("dve", lambda e, c=c: e.scalar_tensor_tensor(
                out=hb[:, c * T:(c + 1) * T], in0=xt[:, c * T:(c + 1) * T],
                scalar=self.g32[:, gidx * DC + c:gidx * DC + c + 1], in1=self.rstd,
                op0=ALU.mult, op1=ALU.mult),
                reads=[xk, "rstd", "g32"], writes=["%s%d" % (hpre, c)])

    def ffn_phase(self, l, which, first=False):
        A, P = self.A, self.P
        gidx = l * 3 + (0 if which == 1 else 2)
        wg_d, wu_d, wd_d = self.wg[which - 1], self.wu[which - 1], self.wd[which - 1]
        Wg = A.bf16(DC * FF)
        Wu = A.bf16(DC * FF)
        Wd = A.bf16(FC * D)
        HF = FC // 2
        act = A.bf16(HF * T)
        stmp = [A.f32(T), A.f32(T)]
        if ("ffn", l, which) in self.conv_done:
            for c in range(DC):
                P.op("sp", lambda e, c=c: e.dma_start(out=Wg[:, c * FF:(c + 1) * FF], in_=self.wsc_g[c * 128:(c + 1) * 128, :]),
                     reads=["wscF"], writes=[P.subkey("Wa")], dma="ldwh%d" % P.slot)
                P.op("sp", lambda e, c=c: e.dma_start(out=Wu[:, c * FF:(c + 1) * FF], in_=self.wsc_u[c * 128:(c + 1) * 128, :]),
                     reads=["wscF"], writes=[P.subkey("Wa")], dma="ldwh%d" % P.slot)
            for j in range(FC):
                P.op("sp", lambda e, j=j: e.dma_start(out=Wd[:, j * D:(j + 1) * D], in_=self.wsc_d[j * 128:(j + 1) * 128, :]),
                     reads=["wscF"], writes=[P.subkey("Wb")], dma="ldwhb%d" % P.slot)
        else:
            for c in range(DC):
                P.op("pool", lambda e, c=c: e.dma_start(out=Wg[:, c * FF:(c + 1) * FF], in_=wg_d[l, c * 128:(c + 1) * 128, :]),
                     writes=[P.subkey("Wa")], dma="ldw%d" % P.slot)
                P.op("pool", lambda e, c=c: e.dma_start(out=Wu[:, c * FF:(c + 1) * FF], in_=wu_d[l, c * 128:(c + 1) * 128, :]),
                     writes=[P.subkey("Wa")], dma="ldw%d" % P.slot)
            for j in range(FC):
                P.op("pool", lambda e, j=j: e.dma_start(out=Wd[:, j * D:(j + 1) * D], in_=wd_d[l, j * 128:(j + 1) * 128, :]),
                     writes=[P.subkey("Wb")], dma="ldwb%d" % P.slot)
        self.schedule_conversion(self.cur_pi)
        ps = self.ps
        G = [ps[1], ps[3]]
        U = [ps[2], ps[4]]
        Y = [ps[5], ps[6], ps[7]]
        hb = self.hb
        hkeys = ["hb%d" % c for c in range(DC)]

        def gate_up(i, j, jj):
            par = jj % 2
            P.op("pe", lambda e: _mm_group(e, G[par][:, :], [
                (Wg[:, c * FF + j * 128:c * FF + (j + 1) * 128], hb[:, c * T:(c + 1) * T]) for c in range(DC)]),
                reads=hkeys + ["Wa"], writes=["pb%d" % (1 + 2 * par)])
            P.op("pe", lambda e: _mm_group(e, U[par][:, :], [
                (Wu[:, c * FF + j * 128:c * FF + (j + 1) * 128], hb[:, c * T:(c + 1) * T]) for c in range(DC)]),
                reads=hkeys + ["Wa"], writes=["pb%d" % (2 + 2 * par)])
            P.op("act", lambda e: e.activation(stmp[par], G[par][:, :], AF.Silu),
                 reads=["pb%d" % (1 + 2 * par)], writes=["stmp%d" % par])
            P.op("dve", lambda e: e.tensor_tensor(act[:, jj * T:(jj + 1) * T], U[par][:, :], stmp[par], ALU.mult),
                 reads=["pb%d" % (2 + 2 * par), "stmp%d" % par], writes=["act%d" % jj])

        def down(i, half, dc):
            b = i % 2
            xt = self.xt[b]
            k = dc % 3
            P.op("pe", lambda e: _mm_group(e, Y[k][:, :], [
                (Wd[:, (half * HF + jj) * D + dc * 128:(half * HF + jj) * D + (dc + 1) * 128],
                 act[:, jj * T:(jj + 1) * T]) for jj in range(HF)]),
                reads=["act%d" % jj for jj in range(HF)] + ["Wb"], writes=["pb%d" % (5 + k)])
            P.op("dve", lambda e: e.scalar_tensor_tensor(
                out=xt[:, dc * T:(dc + 1) * T], in0=Y[k][:, :], scalar=0.5, in1=xt[:, dc * T:(dc + 1) * T],
                op0=ALU.mult, op1=ALU.add),
                reads=["pb%d" % (5 + k), "xt%d" % b], writes=["xt%d" % b])

        self.load_x(0, first)
        if NT > 1:
            self.load_x(1, first)
        self.norm_stage(0, gidx)
        for i in range(NT):
            for half in range(2):
                for jj in range(HF):
                    gate_up(i, half * HF + jj, jj)
                for dc in range(DC):
                    down(i, half, dc)
                    if half == 1 and dc == 1 and i + 1 < NT:
                        self.norm_stage(i + 1, gidx)
            self.store_x(i)
            if i + 2 < NT:
                self.load_x(i + 2, first)

    def state_phase(self, l):
        A, P, ps = self.A, self.P, self.ps
        gidx = l * 3 + 1
        WIN = 3072
        Win = A.bf16(DC * WIN)
        B1 = [A.f32(NH * T), A.f32(NH * T)]
        B2 = [A.f32(NH * T), A.f32(NH * T)]
        B3 = [A.f32(NH * T), A.f32(NH * T)]
        Ecol = [A.f32(NH * 8), A.f32(NH * 8)]
        kT = [A.bf16(NH * T), A.bf16(NH * T)]
        kdT = [A.bf16(NH * T), A.bf16(NH * T)]
        vtm = [A.bf16(8 * 512), A.bf16(8 * 512)]
        kdtm = [A.bf16(8 * 512), A.bf16(8 * 512)]
        S = A.f32(512)
        hbs = [self.hb, A.bf16(DC * T)]
        hpres = ["hb", "hbB"]
        skeys = ["S%d" % h for h in range(NH)]
        lcol = lambda t, hd: t[:, l * 4 + hd:l * 4 + hd + 1]
        trbs = [ps[5][:, :].bitcast(BF16), ps[4][:, :].bitcast(BF16)]
        trk = ["pb5", "pb4"]
        dsb = [ps[7], ps[6]]
        dsk = ["pb7", "pb6"]

        pre = ("mix", l) in self.conv_done
        wq = "sp" if pre else "pool"
        win_src = self.wsc_in if pre else self.win[l]
        wrd = ["wscM"] if pre else []
        for c in range(DC):
            P.op(wq, lambda e, c=c: e.dma_start(out=Win[:, c * WIN + 512:c * WIN + 1536],
                                                in_=win_src[c * 128:(c + 1) * 128, 512:1536]),
                 reads=wrd, writes=[P.subkey("W")], dma=("ldwh%d" if pre else "ldw%d") % P.slot)
        P.op("dve", lambda e: e.memset(S, 0.0), writes=skeys)
        self.schedule_conversion(self.cur_pi)

        pj = [0]

        def pjbank():
            k = 1 + pj[0] % 3
            pj[0] += 1
            return ps[k], "pb%d" % k

        def stage_a(i):
            p = i % 2
            hbuf = hbs[p]
            hkeys = ["%s%d" % (hpres[p], c) for c in range(DC)]
            if i + 1 < NT:
                self.norm_stage(i + 1, gidx, hb=hbs[(i + 1) % 2], hpre=hpres[(i + 1) % 2])
            b1, b2, b3 = B1[p], B2[p], B3[p]
            k1 = ["sB1_%d_%d" % (p, h) for h in range(NH)]
            k2 = ["sB2_%d_%d" % (p, h) for h in range(NH)]
            k3 = ["sB3_%d_%d" % (p, h) for h in range(NH)]
            for hd in range(NH):
                sl = slice(hd * T, (hd + 1) * T)
                bank, bk = pjbank()
                P.op("pe", lambda e, hd=hd, bank=bank: _mm_group(e, bank[:, :], [
                    (Win[:, c * WIN + 512 + hd * 128:c * WIN + 512 + (hd + 1) * 128], hbuf[:, c * T:(c + 1) * T])
                    for c in range(DC)]), reads=hkeys + ["W"], writes=[bk])
                P.op("act", lambda e, sl=sl, bank=bank: e.activation(b1[:, sl], bank[:, :], AF.Tanh, scale=0.5),
                     reads=[bk], writes=[k1[hd]])
                P.op("dve", lambda e, sl=sl, hd=hd: e.tensor_scalar(b2[:, sl], b1[:, sl], lcol(self.nhoml, hd),
                                                                   lcol(self.homl, hd), ALU.mult, ALU.add),
                     reads=[k1[hd], "lbq"], writes=[k2[hd]])
            for cc in range(8):
                bank, bk = pjbank()
                P.op("pe", lambda e, cc=cc, bank=bank: _mm_group(e, bank[0:64, :], [
                    (hbuf[:, c * T + cc * CH:c * T + (cc + 1) * CH], Win[:, c * WIN + 1024:c * WIN + 1536])
                    for c in range(DC)]), reads=hkeys + ["W"], writes=[bk])
                P.op("act", lambda e, cc=cc, bank=bank: e.activation(vtm[p][0:64, cc * 512:(cc + 1) * 512], bank[0:64, :], AF.Copy),
                     reads=[bk], writes=["svtm%d_%d" % (p, cc)])
            HS = [(hd, slice(hd * T, (hd + 1) * T)) for hd in range(NH)]
            for hd, sl in HS:
                P.op("act", lambda e, sl=sl, hd=hd: e.activation(b1[:, sl], b1[:, sl], AF.Ln, bias=lcol(self.lbh, hd),
                                                                scale=lcol(self.homl, hd)),
                     reads=[k1[hd], "lbq"], writes=[k1[hd]])
            for hd, sl in HS:
                P.op("dve", lambda e, sl=sl: e.tensor_scalar_max(b1[:, sl], b1[:, sl], float(np.log(F_MIN))),
                     reads=[k1[hd]], writes=[k1[hd]])
                P.op("dve", lambda e, sl=sl: e.tensor_tensor_scan(b3[:, sl], self.rmask, b1[:, sl], 0.0, ALU.mult, ALU.add),
                     reads=[k1[hd], "cst"], writes=[k3[hd]])
            for hd, sl in HS:
                P.op("act", lambda e, sl=sl: e.activation(b1[:, sl], b3[:, sl], AF.Exp), reads=[k3[hd]], writes=[k1[hd]])
                P.op("act", lambda e, sl=sl: e.activation(b3[:, sl], b3[:, sl], AF.Exp, scale=-1.0),
                     reads=[k3[hd]], writes=[k3[hd]])
            for hd, sl in HS:
                P.op("dve", lambda e, hd=hd: e.tensor_copy(Ecol[p][:, hd * 8:(hd + 1) * 8], b1[:, hd * T + CH - 1:(hd + 1) * T:CH]),
                     reads=[k1[hd]], writes=["sEcol%d_%d" % (p, hd)])
                P.op("dve", lambda e, sl=sl: e.tensor_tensor(b2[:, sl], b2[:, sl], b3[:, sl], ALU.mult),
                     reads=[k2[hd], k3[hd]], writes=[k2[hd]])
                P.op("dve", lambda e, hd=hd, sl=sl: e.tensor_tensor(
                    kdT[p][:, sl].rearrange("p (c t) -> p c t", t=CH), b2[:, sl].rearrange("p (c t) -> p c t", t=CH),
                    Ecol[p][:, hd * 8:(hd + 1) * 8].unsqueeze(2).broadcast_to([128, 8, CH]), ALU.mult),
                    reads=[k2[hd], "sEcol%d_%d" % (p, hd)], writes=["skdT%d_%d" % (p, hd)])
            for hd, sl in HS:
                P.op("act", lambda e, sl=sl: e.activation(kT[p][:, sl], b2[:, sl], AF.Copy), reads=[k2[hd]], writes=["skT%d_%d" % (p, hd)])
            P.op("sp", lambda e, i=i: e.dma_start(out=self.sc_kT[i], in_=kT[p]), reads=["skT%d_%d" % (p, h) for h in range(NH)], dma="st_kT%d" % p)
            P.op("sp", lambda e, i=i: e.dma_start(out=self.sc_v[i], in_=vtm[p][0:64, :]), reads=["svtm%d_%d" % (p, c) for c in range(8)], dma="st_v%d" % p)
            P.op("sp", lambda e, i=i: e.dma_start(out=self.sc_ea[i], in_=b1), reads=k1, dma="st_ea%d" % p)
            P.op("sp", lambda e, i=i: e.dma_start(out=self.sc_E[i], in_=Ecol[p]), reads=["sEcol%d_%d" % (p, h) for h in range(NH)], dma="st_E%d" % p)

        def stage_b(i):
            p = i % 2
            for cc in range(8):
                half = cc % 2
                trb = trbs[half]

                def trf(e, cc=cc, trb=trb):
                    ins = None
                    for hd in range(NH):
                        ins = e.transpose(trb[0:64, hd * 128:(hd + 1) * 128],
                                          kdT[p][:, hd * T + cc * CH:hd * T + (cc + 1) * CH], self.ident_b)
                    return ins
                P.op("pe", trf, reads=["skdT%d_%d" % (p, hd) for hd in range(NH)] + ["cst"], writes=[trk[half]])
                if cc % 2 == 0:
                    P.op("act", lambda e, cc=cc, trb=trb: e.activation(
                        kdtm[p][0:64, cc * 512:(cc + 1) * 512], trb[0:64, 0:512], AF.Copy),
                        reads=[trk[half]], writes=["skdtm%d_%d" % (p, cc)])
                else:
                    P.op("dve", lambda e, cc=cc, trb=trb: e.tensor_copy(
                        kdtm[p][0:64, cc * 512:(cc + 1) * 512], trb[0:64, 0:512]),
                        reads=[trk[half]], writes=["skdtm%d_%d" % (p, cc)])
            P.op("sp", lambda e, i=i: e.dma_start(out=self.sc_kd[i], in_=kdtm[p][0:64, :]),
                 reads=["skdtm%d_%d" % (p, c) for c in range(8)], dma="st_kd%d" % p)
            for cc in range(8):
                bank, bkey = dsb[cc % 2], dsk[cc % 2]

                def dsf(e, cc=cc, bank=bank):
                    ins = None
                    for hd in range(NH):
                        ins = e.matmul(bank[:, hd * 128:(hd + 1) * 128], kdtm[p][0:64, cc * 512 + hd * 128:cc * 512 + (hd + 1) * 128],
                                       vtm[p][0:64, cc * 512 + hd * 128:cc * 512 + (hd + 1) * 128], start=True, stop=True)
                    return ins
                P.op("pe", dsf, reads=["skdtm%d_%d" % (p, cc), "svtm%d_%d" % (p, cc)], writes=[bkey])
                for hd in range(NH):
                    P.op("dve", lambda e, hd=hd, cc=cc, bank=bank: e.scalar_tensor_tensor(
                        out=S[:, hd * 128:(hd + 1) * 128], in0=S[:, hd * 128:(hd + 1) * 128],
                        scalar=Ecol[p][:, hd * 8 + cc:hd * 8 + cc + 1], in1=bank[:, hd * 128:(hd + 1) * 128],
                        op0=ALU.mult, op1=ALU.add),
                        reads=["S%d" % hd, bkey, "sEcol%d_%d" % (p, hd)], writes=["S%d" % hd])

        self.load_x(0, False)
        if NT > 1:
            self.load_x(1, False)
        self.norm_stage(0, gidx)
        stage_a(0)
        for i in range(NT):
            if i + 2 < NT:
                self.load_x(i + 2, False)
            if i + 1 < NT:
                stage_a(i + 1)
            stage_b(i)
        if self.fused:
            groups = [[2 * k, 2 * k + 1] for k in range(NCORES // 2)]
            P.op("sp", lambda e: e.dma_start(out=self.cc_in[l], in_=S), reads=skeys, writes=["ccin%d" % l], dma="st_s")
            P.op("pool", lambda e: e.collective_compute("AllGather", ALU.bypass, replica_groups=groups,
                                                        ins=[self.cc_in[l]], outs=[self.cc_out[l]]),
                 reads=["ccin%d" % l], writes=["ccout%d" % l], dma="cc%d" % l, inc=1)
        else:
            P.op("sp", lambda e: e.dma_start(out=self.s_out, in_=S), reads=skeys, dma="st_s")

    def mix_phase2(self, l):
        A, P, ps = self.A, self.P, self.ps
        gidx = l * 3 + 1
        WIN = 3072
        Win = A.bf16(DC * WIN)
        B1 = A.f32(NH * T)
        B2 = A.f32(NH * T)
        B3 = A.f32(NH * T)
        Ecol = A.f32(NH * 8)
        kT = A.bf16(NH * T)
        osq = A.bf16(NH * T)
        vtm = A.bf16(8 * 512)
        kdtm = A.bf16(8 * 512)
        S = A.f32(512)
        S2 = A.f32(512)
        Sbr = [A.bf16(512), A.bf16(512)]
        Wout = A.bf16(DC * D)
        qT = A.bf16(NH * T)
        sgs = [A.bf16(NH * T), A.bf16(NH * T)]
        thg = [A.f32(T), A.f32(T)]
        uT = A.bf16(NH * T)
        vln = A.bf16(4 * 512)
        Bc = A.f32(512)
        WmT = A.bf16(512)
        stats = A.f32(24)
        mv = A.f32(8)
        rs4 = A.f32(4)
        ocat = A.bf16(DC * T)
        bsp = ocat.bitcast(F32)[:, 0:512]
        b1k = ["B1_%d" % h for h in range(NH)]
        b2k = ["B2_%d" % h for h in range(NH)]
        b3k = ["B3_%d" % h for h in range(NH)]
        hkeys = ["hb%d" % c for c in range(DC)]
        lcol = lambda t, hd: t[:, l * 4 + hd:l * 4 + hd + 1]

        pre = ("mix", l) in self.conv_done
        wq = "sp" if pre else "pool"
        win_src = self.wsc_in if pre else self.win[l]
        wout_src = self.wsc_out if pre else self.wout[l]
        wrd = ["wscM"] if pre else []
        for c in range(DC):
            P.op(wq, lambda e, c=c: e.dma_start(out=Win[:, c * WIN:(c + 1) * WIN], in_=win_src[c * 128:(c + 1) * 128, :]),
                 reads=wrd, writes=[P.subkey("W")], dma=("ldwh%d" if pre else "ldw%d") % P.slot)
        for c in range(DC):
            P.op(wq, lambda e, c=c: e.dma_start(out=Wout[:, c * D:(c + 1) * D], in_=wout_src[c * 128:(c + 1) * 128, :]),
                 reads=wrd, writes=[P.subkey("Wo")], dma=("ldwho%d" if pre else "ldwo%d") % P.slot)
        P.op("sp", lambda e: e.dma_start(out=bsp, in_=self.bsp_d[l]), writes=["bsp", "ocat0", "ocat1"], dma="ld_bsp")
        P.op("sp", lambda e: e.dma_start(out=B1[:, 0:512], in_=self.wst_d[l]), writes=[b1k[0]], dma="ld_wst")
        P.op("dve", lambda e: e.tensor_tensor(
            WmT.rearrange("p (g t) -> p g t", g=4), B1[:, 0:512].rearrange("p (g t) -> p g t", g=4),
            self.tri128.unsqueeze(1).broadcast_to([128, 4, 128]), ALU.mult),
            reads=[b1k[0], "cst"], writes=["WmT"])
        for g in range(4):
            P.op("pe", lambda e, g=g: e.matmul(ps[1][:, g * 128:(g + 1) * 128], self.ones_b, WmT[:, g * 128:(g + 1) * 128],
                                                start=True, stop=True),
                 reads=["WmT", "cst"], writes=["pb1"])
            P.op("dve", lambda e, g=g: e.scalar_tensor_tensor(
                out=Bc[:, g * 128:(g + 1) * 128], in0=ps[1][:, g * 128:(g + 1) * 128],
                scalar=self.mixp[:, 24 + l * 4 + g:24 + l * 4 + g + 1], in1=bsp[:, g * 128:(g + 1) * 128],
                op0=ALU.mult, op1=ALU.add),
                reads=["pb1", "bsp", "ocat0", "ocat1", "mixp"], writes=["Bc"])
        sk0 = ["Sp0_%d" % h for h in range(NH)]
        if self.fused:
            P.op("sp", lambda e: e.dma_start(out=S, in_=self.cc_out[l][0:128, :]), reads=["ccout%d" % l], writes=sk0, dma="ld_S")
            P.op("dve", lambda e: e.tensor_scalar(S, S, self.role[:, 0:1], None, ALU.mult), reads=sk0 + ["role"], writes=sk0)
        else:
            P.op("sp", lambda e: e.dma_start(out=S, in_=self.s_in), writes=sk0, dma="ld_S")
        P.op("act", lambda e: e.activation(Sbr[1], S, AF.Copy), reads=sk0, writes=["Sbr1"])
        self.schedule_conversion(self.cur_pi)

        pj = [0]

        def pjbank():
            k = 1 + pj[0] % 2
            pj[0] += 1
            return ps[k], "pb%d" % k

        def proj_fm(col0):
            bank, bk = pjbank()
            P.op("pe", lambda e: _mm_group(e, bank[:, :], [
                (Win[:, c * WIN + col0:c * WIN + col0 + 128], self.hb[:, c * T:(c + 1) * T]) for c in range(DC)]),
                reads=hkeys + ["W"], writes=[bk])
            return bank, bk

        def front(i):
            sg = sgs[i % 2]
            P.op("sp", lambda e: e.dma_start(out=kT, in_=self.sc_kT[i]), writes=["kT%d" % h for h in range(NH)], dma="ld_kT")
            P.op("sp", lambda e: e.dma_start(out=kdtm[0:64, :], in_=self.sc_kd[i]), writes=["kdtm%d" % c for c in range(8)], dma="ld_kd")
            P.op("sp", lambda e: e.dma_start(out=vtm[0:64, :], in_=self.sc_v[i]), writes=["vtm%d" % c for c in range(8)], dma="ld_v")
            P.op("sp", lambda e: e.dma_start(out=B1, in_=self.sc_ea[i]), writes=b1k, dma="ld_ea")
            P.op("sp", lambda e: e.dma_start(out=Ecol, in_=self.sc_E[i]), writes=["Ecol%d" % h for h in range(NH)], dma="ld_E")
            for hd in range(NH):
                sl = slice(hd * T, (hd + 1) * T)
                th = thg[hd % 2]
                bank, bk = proj_fm(1536 + hd * 128)
                P.op("act", lambda e, th=th, bank=bank: e.activation(th, bank[:, :], AF.Tanh, scale=0.5),
                     reads=[bk], writes=["thg%d" % (hd % 2)])
                P.op("dve", lambda e, sl=sl, th=th, bank=bank, sg=sg: e.scalar_tensor_tensor(
                    out=sg[:, sl], in0=th, scalar=1.0, in1=bank[:, :], op0=ALU.add, op1=ALU.mult),
                    reads=[bk, "thg%d" % (hd % 2)], writes=["sg%d_%d" % (i % 2, hd)])
            for hd in range(NH):
                sl = slice(hd * T, (hd + 1) * T)
                bank, bk = proj_fm(hd * 128)
                P.op("dve", lambda e, sl=sl, bank=bank: e.tensor_tensor(qT[:, sl], bank[:, :], B1[:, sl], ALU.mult),
                     reads=[bk, b1k[hd]], writes=["qT%d" % hd])
            for g in range(4):
                bank, bk = proj_fm(2048 + g * 128)
                P.op("act", lambda e, g=g, bank=bank: e.activation(uT[:, g * T:(g + 1) * T], bank[:, :], AF.Gelu_apprx_tanh),
                     reads=[bk], writes=["uT%d" % g])
            for s in range(4):
                sl = slice(s * T, (s + 1) * T)
                bank, bk = pjbank()
                P.op("pe", lambda e, s=s, bank=bank: _mm_group(e, bank[:, :], [
                    (self.hb[:, c * T + s * 128:c * T + (s + 1) * 128], Win[:, c * WIN + 2560:c * WIN + 3072])
                    for c in range(DC)]),
                    reads=hkeys + ["W"], writes=[bk])
                P.op("act", lambda e, sl=sl, bank=bank: e.activation(B1[:, sl], bank[:, :], AF.Gelu_apprx_tanh),
                     reads=[bk], writes=[b1k[s]])
                P.op("dve", lambda e, s=s, sl=sl: e.bn_stats(stats[:, s * 6:(s + 1) * 6], B1[:, sl]),
                     reads=[b1k[s]], writes=["stats%d" % s])
                P.op("dve", lambda e, s=s: e.bn_aggr(mv[:, s * 2:(s + 1) * 2], stats[:, s * 6:(s + 1) * 6]),
                     reads=["stats%d" % s], writes=["mv"])
            P.op("dve", lambda e: e.tensor_scalar(rs4, mv[:, 1:8:2], float(EPS), None, ALU.add), reads=["mv"], writes=["rs4"])
            P.op("act", lambda e: e.activation(rs4, rs4, AF.Ln), reads=["rs4"], writes=["rs4"])
            P.op("act", lambda e: e.activation(rs4, rs4, AF.Exp, scale=-0.5), reads=["rs4"], writes=["rs4"])
            for s in range(4):
                sl = slice(s * T, (s + 1) * T)
                P.op("dve", lambda e, s=s, sl=sl: e.tensor_scalar(vln[:, sl], B1[:, sl], mv[:, 2 * s:2 * s + 1], rs4[:, s:s + 1],
                                                                 ALU.subtract, ALU.mult),
                     reads=[b1k[s], "mv", "rs4"], writes=["vln%d" % s])

        def post_b(g):
            bank, bk = pjbank()

            def spf(e, g=g, bank=bank):
                ins = None
                for s in range(4):
                    ins = e.matmul(bank[:, s * 128:(s + 1) * 128], vln[:, s * 512 + g * 128:s * 512 + (g + 1) * 128],
                                   WmT[:, g * 128:(g + 1) * 128], start=True, stop=True)
                return ins
            P.op("pe", spf, reads=["vln%d" % s for s in range(4)] + ["WmT"], writes=[bk])
            gl = slice(g * T, (g + 1) * T)
            P.op("dve", lambda e, g=g, gl=gl, bank=bank: e.scalar_tensor_tensor(
                out=B1[:, gl].rearrange("p (s t) -> p s t", s=4), in0=bank[:, :].rearrange("p (s t) -> p s t", s=4),
                scalar=self.mixp[:, 16 + l * 4 + g:16 + l * 4 + g + 1],
                in1=Bc[:, g * 128:(g + 1) * 128].unsqueeze(1).broadcast_to([128, 4, 128]),
                op0=ALU.mult, op1=ALU.add),
                reads=[bk, "Bc", "mixp"], writes=[b1k[g]])
            P.op("dve", lambda e, g=g, gl=gl: e.tensor_tensor(ocat[:, (4 + g) * T:(5 + g) * T], B1[:, gl], uT[:, gl], ALU.mult),
                 reads=[b1k[g], "uT%d" % g], writes=["ocat%d" % (4 + g)])

        PTv = B2.bitcast(BF16)
        PT = [PTv[0:64, cc * 256:(cc + 1) * 256] for cc in range(8)]
        scb = [(ps[4], "pb4"), (ps[5], "pb5")]
        dsb = [(ps[7], "pb7"), (ps[3], "pb3")]
        obb = [(ps[6], "pb6"), (ps[0], "pb0")]
        Sp = [S, S2]
        csl = lambda hd, cc: slice(hd * T + cc * CH, hd * T + (cc + 1) * CH)

        def recur(i):
            for cc in range(8):
                bank, bkey = scb[cc % 2]

                def scf(e, cc=cc, bank=bank):
                    ins = None
                    for hd in range(NH):
                        ins = e.matmul(bank[0:64, hd * CH:(hd + 1) * CH], kT[:, csl(hd, cc)], qT[:, csl(hd, cc)], start=True, stop=True)
                    return ins
                P.op("pe", scf, reads=["kT%d" % h for h in range(NH)] + ["qT%d" % h for h in range(NH)], writes=[bkey])
                P.op("dve", lambda e, cc=cc, bank=bank: e.tensor_tensor(PT[cc], bank[0:64, 0:256], self.tri64, ALU.mult),
                     reads=[bkey, "cst"], writes=["PT%d" % cc, b2k[0], b2k[1]])

            def emit_ds(cc):
                bank, bkey = dsb[cc % 2]

                def dsf(e, cc=cc, bank=bank):
                    ins = None
                    for hd in range(NH):
                        ins = e.matmul(bank[:, hd * 128:(hd + 1) * 128], kdtm[0:64, cc * 512 + hd * 128:cc * 512 + (hd + 1) * 128],
                                       vtm[0:64, cc * 512 + hd * 128:cc * 512 + (hd + 1) * 128], start=True, stop=True)
                    return ins
                P.op("pe", dsf, reads=["kdtm%d" % cc, "vtm%d" % cc], writes=[bkey])

            emit_ds(0)
            emit_ds(1)
            for cc in range(8):
                OB, okey = obb[cc % 2]
                sbp = Sbr[(cc - 1) % 2]

                def of(e, cc=cc, OB=OB, sbp=sbp):
                    ins = None
                    for hd in range(NH):
                        o = OB[:, hd * CH:(hd + 1) * CH]
                        e.matmul(o, sbp[:, hd * 128:(hd + 1) * 128], qT[:, csl(hd, cc)], start=True, stop=False)
                        ins = e.matmul(o, vtm[0:64, cc * 512 + hd * 128:cc * 512 + (hd + 1) * 128],
                                       PT[cc][:, hd * CH:(hd + 1) * CH], start=False, stop=True)
                    return ins
                P.op("pe", of, reads=["Sbr%d" % ((cc - 1) % 2), "vtm%d" % cc, "PT%d" % cc] + ["qT%d" % h for h in range(NH)],
                     writes=[okey])
                P.op("act", lambda e, cc=cc, OB=OB: e.activation(
                    B3.rearrange("p (h t) -> p h t", h=NH)[:, :, cc * CH:(cc + 1) * CH],
                    OB[:, 0:256].rearrange("p (h t) -> p h t", h=NH), AF.Copy),
                    reads=[okey], writes=b3k)
                bank, bkey = dsb[cc % 2]
                src, dst = Sp[cc % 2], Sp[(cc + 1) % 2]
                for hd in range(NH):
                    P.op("dve", lambda e, hd=hd, cc=cc, bank=bank, src=src, dst=dst: e.scalar_tensor_tensor(
                        out=dst[:, hd * 128:(hd + 1) * 128], in0=src[:, hd * 128:(hd + 1) * 128],
                        scalar=Ecol[:, hd * 8 + cc:hd * 8 + cc + 1], in1=bank[:, hd * 128:(hd + 1) * 128],
                        op0=ALU.mult, op1=ALU.add),
                        reads=["Sp%d_%d" % (cc % 2, hd), bkey, "Ecol%d" % hd], writes=["Sp%d_%d" % ((cc + 1) % 2, hd)])
                P.op("act", lambda e, cc=cc, dst=dst: e.activation(Sbr[cc % 2], dst, AF.Copy),
                     reads=["Sp%d_%d" % ((cc + 1) % 2, h) for h in range(NH)], writes=["Sbr%d" % (cc % 2)])
                if cc + 2 < 8:
                    emit_ds(cc + 2)
                if cc % 2 == 1:
                    post_b(cc // 2)

        def back(i):
            b = i % 2
            xt = self.xt[b]
            xk = "xt%d" % b
            sg = sgs[i % 2]
            HS = [(hd, slice(hd * T, (hd + 1) * T)) for hd in range(NH)]
            ssb = [(ps[0], "pb0"), (ps[4], "pb4"), (ps[7], "pb7"), (ps[6], "pb6")]
            for hd, sl in HS:
                P.op("act", lambda e, sl=sl: e.activation(osq[:, sl], B3[:, sl], AF.Square), reads=[b3k[hd]], writes=["osq%d" % hd])
            for hd, sl in HS:
                P.op("pe", lambda e, sl=sl, hd=hd: e.matmul(ssb[hd][0][:, :], self.ones_b, osq[:, sl], start=True, stop=True),
                     reads=["osq%d" % hd, "cst"], writes=[ssb[hd][1]])
            for hd, sl in HS:
                P.op("dve", lambda e, sl=sl, hd=hd: e.tensor_scalar(B2[:, sl], ssb[hd][0][:, :], float(128 * EPS), None, ALU.add),
                     reads=[ssb[hd][1]], writes=[b2k[hd]])
            for hd, sl in HS:
                P.op("act", lambda e, sl=sl: e.activation(B2[:, sl], B2[:, sl], AF.Ln), reads=[b2k[hd]], writes=[b2k[hd]])
            for hd, sl in HS:
                P.op("act", lambda e, sl=sl: e.activation(B2[:, sl], B2[:, sl], AF.Exp, scale=-0.5), reads=[b2k[hd]], writes=[b2k[hd]])
            for hd, sl in HS:
                P.op("dve", lambda e, sl=sl: e.tensor_tensor(B2[:, sl], B3[:, sl], B2[:, sl], ALU.mult),
                     reads=[b3k[hd], b2k[hd]], writes=[b2k[hd]])
                P.op("dve", lambda e, sl=sl, hd=hd, sg=sg: e.scalar_tensor_tensor(
                    out=ocat[:, sl], in0=B2[:, sl], scalar=lcol(self.gnp, hd), in1=sg[:, sl], op0=ALU.mult, op1=ALU.mult),
                    reads=[b2k[hd], "sg%d_%d" % (i % 2, hd), "lbq"], writes=["ocat%d" % hd])
            okeys = ["ocat%d" % c for c in range(DC)]
            for dc in range(DC):
                bank, bk = pjbank()
                P.op("pe", lambda e, dc=dc, bank=bank: _mm_group(e, bank[:, :], [
                    (Wout[:, c * D + dc * 128:c * D + (dc + 1) * 128], ocat[:, c * T:(c + 1) * T]) for c in range(DC)]),
                    reads=okeys + ["Wo"], writes=[bk])
                P.op("dve", lambda e, dc=dc, bank=bank, xt=xt: e.tensor_tensor(
                    xt[:, dc * T:(dc + 1) * T], xt[:, dc * T:(dc + 1) * T], bank[:, :], ALU.add),
                    reads=[bk, xk], writes=[xk])
            self.store_x(i)

        self.load_x(0, False)
        if NT > 1:
            self.load_x(1, False)
        self.norm_stage(0, gidx)
        front(0)
        if NT > 1:
            self.norm_stage(1, gidx)
        for i in range(NT):
            recur(i)
            if i + 1 < NT:
                front(i + 1)
            back(i)
            if i + 2 < NT:
                self.load_x(i + 2, False)
                self.norm_stage(i + 2, gidx)

    def mix_phase(self, l, state_only=False):
        A, P, ps = self.A, self.P, self.ps
        full = not state_only
        gidx = l * 3 + 1
        WIN = 3072
        Win = A.bf16(DC * WIN)
        B1 = A.f32(NH * T)
        B2 = A.f32(NH * T)
        B3 = A.f32(NH * T)
        Ecol = A.f32(NH * 8)
        kT = A.bf16(NH * T)
        osq = A.bf16(NH * T)
        kdT = osq
        vtm = A.bf16(8 * 512)
        kdtm = A.bf16(8 * 512)
        S = A.f32(512)
        Sb = A.bf16(512)
        S2 = A.f32(512)
        Sb2 = A.bf16(512)
        if full:
            Wout = A.bf16(DC * D)
            qT = A.bf16(NH * T)
            sg = A.bf16(NH * T)
            uT = A.bf16(NH * T)
            vln = A.bf16(4 * 512)
            Bc = A.f32(512)
            WmT = A.bf16(512)
            stats = A.f32(24)
            mv = A.f32(8)
            rs4 = A.f32(4)
            ocat = A.bf16(DC * T)
            bsp = ocat.bitcast(F32)[:, 0:512]
            PTb = [A.bf16(256), A.bf16(256)]
        b1k = ["B1_%d" % h for h in range(NH)]
        b2k = ["B2_%d" % h for h in range(NH)]
        b3k = ["B3_%d" % h for h in range(NH)]
        hkeys = ["hb%d" % c for c in range(DC)]
        lcol = lambda t, hd: t[:, l * 4 + hd:l * 4 + hd + 1]
        OBs = [ps[6], ps[5]]
        obk = ["pb6", "pb5"]
        trbs = [ps[5][:, :].bitcast(BF16), ps[4][:, :].bitcast(BF16)]
        trk = ["pb5", "pb4"]

        pre = ("mix", l) in self.conv_done
        wq = "sp" if pre else "pool"
        win_src = self.wsc_in if pre else self.win[l]
        wout_src = self.wsc_out if pre else self.wout[l]
        wrd = ["wscM"] if pre else []
        if full:
            for c in range(DC):
                P.op(wq, lambda e, c=c: e.dma_start(out=Win[:, c * WIN:(c + 1) * WIN],
                                                    in_=win_src[c * 128:(c + 1) * 128, :]),
                     reads=wrd, writes=[P.subkey("W")], dma=("ldwh%d" if pre else "ldw%d") % P.slot)
            for c in range(DC):
                P.op(wq, lambda e, c=c: e.dma_start(out=Wout[:, c * D:(c + 1) * D],
                                                    in_=wout_src[c * 128:(c + 1) * 128, :]),
                     reads=wrd, writes=[P.subkey("W")], dma=("ldwh%d" if pre else "ldw%d") % P.slot)
            P.op("sp", lambda e: e.dma_start(out=bsp, in_=self.bsp_d[l]), writes=["bsp", "ocat0", "ocat1"], dma="ld_bsp")
            P.op("sp", lambda e: e.dma_start(out=B1[:, 0:512], in_=self.wst_d[l]), writes=[b1k[0]], dma="ld_wst")
            P.op("dve", lambda e: e.tensor_tensor(
                WmT.rearrange("p (g t) -> p g t", g=4), B1[:, 0:512].rearrange("p (g t) -> p g t", g=4),
                self.tri128.unsqueeze(1).broadcast_to([128, 4, 128]), ALU.mult),
                reads=[b1k[0], "cst"], writes=["WmT"])
            for g in range(4):
                P.op("pe", lambda e, g=g: e.matmul(ps[1][:, g * 128:(g + 1) * 128], self.ones_b, WmT[:, g * 128:(g + 1) * 128],
                                                    start=True, stop=True),
                     reads=["WmT", "cst"], writes=["pb1"])
                P.op("dve", lambda e, g=g: e.scalar_tensor_tensor(
                    out=Bc[:, g * 128:(g + 1) * 128], in0=ps[1][:, g * 128:(g + 1) * 128],
                    scalar=self.mixp[:, 24 + l * 4 + g:24 + l * 4 + g + 1], in1=bsp[:, g * 128:(g + 1) * 128],
                    op0=ALU.mult, op1=ALU.add),
                    reads=["pb1", "bsp", "ocat0", "ocat1", "mixp"], writes=["Bc"])
            if self.fused:
                P.op("sp", lambda e: e.dma_start(out=S, in_=self.cc_out[l][0:128, :]), reads=["ccout%d" % l],
                     writes=["Sp0_%d" % h for h in range(NH)], dma="ld_S")
                P.op("dve", lambda e: e.tensor_scalar(S, S, self.role[:, 0:1], None, ALU.mult),
                     reads=["Sp0_%d" % h for h in range(NH)] + ["role"], writes=["Sp0_%d" % h for h in range(NH)])
            else:
                P.op("sp", lambda e: e.dma_start(out=S, in_=self.s_in), writes=["Sp0_%d" % h for h in range(NH)], dma="ld_S")
        else:
            for c in range(DC):
                P.op(wq, lambda e, c=c: e.dma_start(out=Win[:, c * WIN + 512:c * WIN + 1536],
                                                    in_=win_src[c * 128:(c + 1) * 128, 512:1536]),
                     reads=wrd, writes=[P.subkey("W")], dma=("ldwh%d" if pre else "ldw%d") % P.slot)
            P.op("dve", lambda e: e.memset(S, 0.0), writes=["S%d" % h for h in range(NH)])
        skeys = ["Sp0_%d" % h for h in range(NH)]
        Sbr = [Sb, Sb2]
        P.op("act", lambda e: e.activation(Sbr[1], S, AF.Copy), reads=skeys, writes=["Sbr1"])
        self.schedule_conversion(self.cur_pi)

        pj = [0]

        def pjbank():
            k = 1 + pj[0] % 2
            pj[0] += 1
            return ps[k], "pb%d" % k

        def proj_fm(col0):
            bank, bk = pjbank()
            P.op("pe", lambda e: _mm_group(e, bank[:, :], [
                (Win[:, c * WIN + col0:c * WIN + col0 + 128], self.hb[:, c * T:(c + 1) * T]) for c in range(DC)]),
                reads=hkeys + ["W"], writes=[bk])
            return bank, bk

        self.load_x(0, False)
        if NT > 1:
            self.load_x(1, False)
        self.norm_stage(0, gidx)
        for i in range(NT):
            b = i % 2
            xt = self.xt[b]
            xk = "xt%d" % b
            if state_only:
                for hd in range(NH):
                    sl = slice(hd * T, (hd + 1) * T)
                    bank, bk = proj_fm(512 + hd * 128)
                    P.op("act", lambda e, sl=sl, bank=bank: e.activation(B1[:, sl], bank[:, :], AF.Tanh, scale=0.5),
                         reads=[bk], writes=[b1k[hd]])
                    P.op("dve", lambda e, sl=sl, hd=hd: e.tensor_scalar(B2[:, sl], B1[:, sl], lcol(self.nhoml, hd),
                                                                       lcol(self.homl, hd), ALU.mult, ALU.add),
                         reads=[b1k[hd], "lbq"], writes=[b2k[hd]])
                for cc in range(8):
                    bank, bk = pjbank()
                    P.op("pe", lambda e, cc=cc, bank=bank: _mm_group(e, bank[0:64, :], [
                        (self.hb[:, c * T + cc * CH:c * T + (cc + 1) * CH], Win[:, c * WIN + 1024:c * WIN + 1536])
                        for c in range(DC)]),
                        reads=hkeys + ["W"], writes=[bk])
                    P.op("act", lambda e, cc=cc, bank=bank: e.activation(vtm[0:64, cc * 512:(cc + 1) * 512], bank[0:64, :], AF.Copy),
                         reads=[bk], writes=["vtm%d" % cc])
                for hd in range(NH):
                    sl = slice(hd * T, (hd + 1) * T)
                    P.op("act", lambda e, sl=sl, hd=hd: e.activation(B1[:, sl], B1[:, sl], AF.Ln, bias=lcol(self.lbh, hd),
                                                                    scale=lcol(self.homl, hd)),
                         reads=[b1k[hd], "lbq"], writes=[b1k[hd]])
                    P.op("dve", lambda e, sl=sl: e.tensor_scalar_max(B1[:, sl], B1[:, sl], float(np.log(F_MIN))),
                         reads=[b1k[hd]], writes=[b1k[hd]])
                    P.op("dve", lambda e, sl=sl: e.tensor_tensor_scan(B3[:, sl], self.rmask, B1[:, sl], 0.0, ALU.mult, ALU.add),
                         reads=[b1k[hd], "cst"], writes=[b3k[hd]])
                    P.op("act", lambda e, sl=sl: e.activation(B1[:, sl], B3[:, sl], AF.Exp), reads=[b3k[hd]], writes=[b1k[hd]])
                    P.op("act", lambda e, sl=sl: e.activation(B3[:, sl], B3[:, sl], AF.Exp, scale=-1.0),
                         reads=[b3k[hd]], writes=[b3k[hd]])
                    P.op("dve", lambda e, hd=hd: e.tensor_copy(Ecol[:, hd * 8:(hd + 1) * 8], B1[:, hd * T + CH - 1:(hd + 1) * T:CH]),
                         reads=[b1k[hd]], writes=["Ecol%d" % hd])
                    P.op("dve", lambda e, sl=sl: e.tensor_tensor(B2[:, sl], B2[:, sl], B3[:, sl], ALU.mult),
                         reads=[b2k[hd], b3k[hd]], writes=[b2k[hd]])
                    P.op("act", lambda e, sl=sl: e.activation(kT[:, sl], B2[:, sl], AF.Copy), reads=[b2k[hd]], writes=["kT%d" % hd])
                    P.op("dve", lambda e, hd=hd, sl=sl: e.tensor_tensor(
                        kdT[:, sl].rearrange("p (c t) -> p c t", t=CH), B2[:, sl].rearrange("p (c t) -> p c t", t=CH),
                        Ecol[:, hd * 8:(hd + 1) * 8].unsqueeze(2).broadcast_to([128, 8, CH]), ALU.mult),
                        reads=[b2k[hd], "Ecol%d" % hd], writes=["osq%d" % hd])
                for cc in range(8):
                    half = cc % 2

                    trb = trbs[half]

                    def trf(e, cc=cc, trb=trb):
                        ins = None
                        for hd in range(NH):
                            ins = e.transpose(trb[0:64, hd * 128:(hd + 1) * 128],
                                              kdT[:, hd * T + cc * CH:hd * T + (cc + 1) * CH], self.ident_b)
                        return ins
                    P.op("pe", trf, reads=["osq%d" % hd for hd in range(NH)] + ["cst"], writes=[trk[half]])
                    if cc % 2 == 0:
                        P.op("act", lambda e, cc=cc, trb=trb: e.activation(
                            kdtm[0:64, cc * 512:(cc + 1) * 512], trb[0:64, 0:512], AF.Copy),
                            reads=[trk[half]], writes=["kdtm%d" % cc])
                    else:
                        P.op("dve", lambda e, cc=cc, trb=trb: e.tensor_copy(
                            kdtm[0:64, cc * 512:(cc + 1) * 512], trb[0:64, 0:512]),
                            reads=[trk[half]], writes=["kdtm%d" % cc])
                P.op("sp", lambda e, i=i: e.dma_start(out=self.sc_kT[i], in_=kT), reads=["kT%d" % h for h in range(NH)], dma="st_kT")
                P.op("sp", lambda e, i=i: e.dma_start(out=self.sc_kd[i], in_=kdtm[0:64, :]), reads=["kdtm%d" % c for c in range(8)], dma="st_kd")
                P.op("sp", lambda e, i=i: e.dma_start(out=self.sc_v[i], in_=vtm[0:64, :]), reads=["vtm%d" % c for c in range(8)], dma="st_v")
                P.op("sp", lambda e, i=i: e.dma_start(out=self.sc_ea[i], in_=B1), reads=b1k, dma="st_ea")
                P.op("sp", lambda e, i=i: e.dma_start(out=self.sc_E[i], in_=Ecol), reads=["Ecol%d" % h for h in range(NH)], dma="st_E")
            else:
                P.op("sp", lambda e, i=i: e.dma_start(out=kT, in_=self.sc_kT[i]), writes=["kT%d" % h for h in range(NH)], dma="ld_kT")
                P.op("sp", lambda e, i=i: e.dma_start(out=kdtm[0:64, :], in_=self.sc_kd[i]), writes=["kdtm%d" % c for c in range(8)], dma="ld_kd")
                P.op("sp", lambda e, i=i: e.dma_start(out=vtm[0:64, :], in_=self.sc_v[i]), writes=["vtm%d" % c for c in range(8)], dma="ld_v")
                P.op("sp", lambda e, i=i: e.dma_start(out=B1, in_=self.sc_ea[i]), writes=b1k, dma="ld_ea")
                P.op("sp", lambda e, i=i: e.dma_start(out=Ecol, in_=self.sc_E[i]), writes=["Ecol%d" % h for h in range(NH)], dma="ld_E")
            if full:
                for hd in range(NH):
                    sl = slice(hd * T, (hd + 1) * T)
                    bank, bk = proj_fm(1536 + hd * 128)
                    P.op("act", lambda e, sl=sl, bank=bank: e.activation(B3[:, sl], bank[:, :], AF.Tanh, scale=0.5),
                         reads=[bk], writes=[b3k[hd]])
                    P.op("dve", lambda e, sl=sl, bank=bank: e.scalar_tensor_tensor(
                        out=sg[:, sl], in0=B3[:, sl], scalar=1.0, in1=bank[:, :], op0=ALU.add, op1=ALU.mult),
                        reads=[bk, b3k[hd]], writes=["sg%d" % hd])
            if full:
                for hd in range(NH):
                    sl = slice(hd * T, (hd + 1) * T)
                    bank, bk = proj_fm(hd * 128)
                    P.op("dve", lambda e, sl=sl, bank=bank: e.tensor_tensor(qT[:, sl], bank[:, :], B1[:, sl], ALU.mult),
                         reads=[bk, b1k[hd]], writes=["qT%d" % hd])
            if full:
                for g in range(4):
                    bank, bk = proj_fm(2048 + g * 128)
                    P.op("act", lambda e, g=g, bank=bank: e.activation(uT[:, g * T:(g + 1) * T], bank[:, :], AF.Gelu_apprx_tanh),
                         reads=[bk], writes=["uT%d" % g])
                for s in range(4):
                    sl = slice(s * T, (s + 1) * T)
                    bank, bk = pjbank()
                    P.op("pe", lambda e, s=s, bank=bank: _mm_group(e, bank[:, :], [
                        (self.hb[:, c * T + s * 128:c * T + (s + 1) * 128], Win[:, c * WIN + 2560:c * WIN + 3072])
                        for c in range(DC)]),
                        reads=hkeys + ["W"], writes=[bk])
                    P.op("act", lambda e, sl=sl, bank=bank: e.activation(B1[:, sl], bank[:, :], AF.Gelu_apprx_tanh),
                         reads=[bk], writes=[b1k[s]])
                    P.op("dve", lambda e, s=s, sl=sl: e.bn_stats(stats[:, s * 6:(s + 1) * 6], B1[:, sl]),
                         reads=[b1k[s]], writes=["stats%d" % s])
                    P.op("dve", lambda e, s=s: e.bn_aggr(mv[:, s * 2:(s + 1) * 2], stats[:, s * 6:(s + 1) * 6]),
                         reads=["stats%d" % s], writes=["mv"])
                P.op("dve", lambda e: e.tensor_scalar(rs4, mv[:, 1:8:2], float(EPS), None, ALU.add), reads=["mv"], writes=["rs4"])
                P.op("act", lambda e: e.activation(rs4, rs4, AF.Ln), reads=["rs4"], writes=["rs4"])
                P.op("act", lambda e: e.activation(rs4, rs4, AF.Exp, scale=-0.5), reads=["rs4"], writes=["rs4"])
                if i + 1 < NT:
                    self.norm_stage(i + 1, gidx)
                for s in range(4):
                    sl = slice(s * T, (s + 1) * T)
                    P.op("dve", lambda e, s=s, sl=sl: e.tensor_scalar(vln[:, sl], B1[:, sl], mv[:, 2 * s:2 * s + 1], rs4[:, s:s + 1],
                                                                     ALU.subtract, ALU.mult),
                         reads=[b1k[s], "mv", "rs4"], writes=["vln%d" % s])
            if state_only and i + 1 < NT:
                self.norm_stage(i + 1, gidx)

            def post_b(g):
                bank, bk = pjbank()

                def spf(e, g=g, bank=bank):
                    ins = None
                    for s in range(4):
                        ins = e.matmul(bank[:, s * 128:(s + 1) * 128], vln[:, s * 512 + g * 128:s * 512 + (g + 1) * 128],
                                       WmT[:, g * 128:(g + 1) * 128], start=True, stop=True)
                    return ins
                P.op("pe", spf, reads=["vln%d" % s for s in range(4)] + ["WmT"], writes=[bk])
                gl = slice(g * T, (g + 1) * T)
                P.op("dve", lambda e, g=g, gl=gl, bank=bank: e.scalar_tensor_tensor(
                    out=B1[:, gl].rearrange("p (s t) -> p s t", s=4), in0=bank[:, :].rearrange("p (s t) -> p s t", s=4),
                    scalar=self.mixp[:, 16 + l * 4 + g:16 + l * 4 + g + 1],
                    in1=Bc[:, g * 128:(g + 1) * 128].unsqueeze(1).broadcast_to([128, 4, 128]),
                    op0=ALU.mult, op1=ALU.add),
                    reads=[bk, "Bc", "mixp"], writes=[b1k[g]])
                P.op("dve", lambda e, g=g, gl=gl: e.tensor_tensor(ocat[:, (4 + g) * T:(5 + g) * T], B1[:, gl], uT[:, gl], ALU.mult),
                     reads=[b1k[g], "uT%d" % g], writes=["ocat%d" % (4 + g)])

            PTv = B2.bitcast(BF16)
            PT = [PTv[0:64, cc * 256:(cc + 1) * 256] for cc in range(8)]
            scb = [(ps[4], "pb4"), (ps[5], "pb5")]
            dsb = [(ps[7], "pb7"), (ps[3], "pb3")]
            obb = [(ps[6], "pb6"), (ps[0], "pb0")]
            Sp = [S, S2]
            csl = lambda hd, cc: slice(hd * T + cc * CH, hd * T + (cc + 1) * CH)
            for cc in range(8):
                bank, bkey = scb[cc % 2]

                def scf(e, cc=cc, bank=bank):
                    ins = None
                    for hd in range(NH):
                        ins = e.matmul(bank[0:64, hd * CH:(hd + 1) * CH], kT[:, csl(hd, cc)], qT[:, csl(hd, cc)], start=True, stop=True)
                    return ins
                P.op("pe", scf, reads=["kT%d" % h for h in range(NH)] + ["qT%d" % h for h in range(NH)], writes=[bkey])
                P.op("dve", lambda e, cc=cc, bank=bank: e.tensor_tensor(PT[cc], bank[0:64, 0:256], self.tri64, ALU.mult),
                     reads=[bkey, "cst"], writes=["PT%d" % cc, b2k[0], b2k[1]])

            def emit_ds(cc):
                bank, bkey = dsb[cc % 2]

                def dsf(e, cc=cc, bank=bank):
                    ins = None
                    for hd in range(NH):
                        ins = e.matmul(bank[:, hd * 128:(hd + 1) * 128], kdtm[0:64, cc * 512 + hd * 128:cc * 512 + (hd + 1) * 128],
                                       vtm[0:64, cc * 512 + hd * 128:cc * 512 + (hd + 1) * 128], start=True, stop=True)
                    return ins
                P.op("pe", dsf, reads=["kdtm%d" % cc, "vtm%d" % cc], writes=[bkey])

            emit_ds(0)
            emit_ds(1)
            for cc in range(8):
                OB, okey = obb[cc % 2]
                sbp = Sbr[(cc - 1) % 2]

                def of(e, cc=cc, OB=OB, sbp=sbp):
                    ins = None
                    for hd in range(NH):
                        o = OB[:, hd * CH:(hd + 1) * CH]
                        e.matmul(o, sbp[:, hd * 128:(hd + 1) * 128], qT[:, csl(hd, cc)], start=True, stop=False)
                        ins = e.matmul(o, vtm[0:64, cc * 512 + hd * 128:cc * 512 + (hd + 1) * 128],
                                       PT[cc][:, hd * CH:(hd + 1) * CH], start=False, stop=True)
                    return ins
                P.op("pe", of, reads=["Sbr%d" % ((cc - 1) % 2), "vtm%d" % cc, "PT%d" % cc] + ["qT%d" % h for h in range(NH)],
                     writes=[okey])
                P.op("act", lambda e, cc=cc, OB=OB: e.activation(
                    B3.rearrange("p (h t) -> p h t", h=NH)[:, :, cc * CH:(cc + 1) * CH],
                    OB[:, 0:256].rearrange("p (h t) -> p h t", h=NH), AF.Copy),
                    reads=[okey], writes=b3k)
                bank, bkey = dsb[cc % 2]
                src, dst = Sp[cc % 2], Sp[(cc + 1) % 2]
                for hd in range(NH):
                    P.op("dve", lambda e, hd=hd, cc=cc, bank=bank, src=src, dst=dst: e.scalar_tensor_tensor(
                        out=dst[:, hd * 128:(hd + 1) * 128], in0=src[:, hd * 128:(hd + 1) * 128],
                        scalar=Ecol[:, hd * 8 + cc:hd * 8 + cc + 1], in1=bank[:, hd * 128:(hd + 1) * 128],
                        op0=ALU.mult, op1=ALU.add),
                        reads=["Sp%d_%d" % (cc % 2, hd), bkey, "Ecol%d" % hd], writes=["Sp%d_%d" % ((cc + 1) % 2, hd)])
                P.op("act", lambda e, cc=cc, dst=dst: e.activation(Sbr[cc % 2], dst, AF.Copy),
                     reads=["Sp%d_%d" % ((cc + 1) % 2, h) for h in range(NH)], writes=["Sbr%d" % (cc % 2)])
                if cc + 2 < 8:
                    emit_ds(cc + 2)
                if cc % 2 == 1:
                    post_b(cc // 2)
            if full:
                HS = [(hd, slice(hd * T, (hd + 1) * T)) for hd in range(NH)]
                ssb = [(ps[0], "pb0"), (ps[4], "pb4"), (ps[7], "pb7"), (ps[6], "pb6")]
                for hd, sl in HS:
                    P.op("act", lambda e, sl=sl: e.activation(osq[:, sl], B3[:, sl], AF.Square),
                         reads=[b3k[hd]], writes=["osq%d" % hd])
                for hd, sl in HS:
                    P.op("pe", lambda e, sl=sl, hd=hd: e.matmul(ssb[hd][0][:, :], self.ones_b, osq[:, sl], start=True, stop=True),
                         reads=["osq%d" % hd, "cst"], writes=[ssb[hd][1]])
                for hd, sl in HS:
                    P.op("dve", lambda e, sl=sl, hd=hd: e.tensor_scalar(B2[:, sl], ssb[hd][0][:, :], float(128 * EPS), None, ALU.add),
                         reads=[ssb[hd][1]], writes=[b2k[hd]])
                for hd, sl in HS:
                    P.op("act", lambda e, sl=sl: e.activation(B2[:, sl], B2[:, sl], AF.Ln), reads=[b2k[hd]], writes=[b2k[hd]])
                for hd, sl in HS:
                    P.op("act", lambda e, sl=sl: e.activation(B2[:, sl], B2[:, sl], AF.Exp, scale=-0.5), reads=[b2k[hd]], writes=[b2k[hd]])
                for hd, sl in HS:
                    P.op("dve", lambda e, sl=sl: e.tensor_tensor(B2[:, sl], B3[:, sl], B2[:, sl], ALU.mult),
                         reads=[b3k[hd], b2k[hd]], writes=[b2k[hd]])
                    P.op("dve", lambda e, sl=sl, hd=hd: e.scalar_tensor_tensor(
                        out=ocat[:, sl], in0=B2[:, sl], scalar=lcol(self.gnp, hd), in1=sg[:, sl], op0=ALU.mult, op1=ALU.mult),
                        reads=[b2k[hd], "sg%d" % hd, "lbq"], writes=["ocat%d" % hd])
                okeys = ["ocat%d" % c for c in range(DC)]
                for dc in range(DC):
                    bank, bk = pjbank()
                    P.op("pe", lambda e, dc=dc, bank=bank: _mm_group(e, bank[:, :], [
                        (Wout[:, c * D + dc * 128:c * D + (dc + 1) * 128], ocat[:, c * T:(c + 1) * T]) for c in range(DC)]),
                        reads=okeys + ["W"], writes=[bk])
                    P.op("dve", lambda e, dc=dc, bank=bank, xt=xt: e.tensor_tensor(
                        xt[:, dc * T:(dc + 1) * T], xt[:, dc * T:(dc + 1) * T], bank[:, :], ALU.add),
                        reads=[bk, xk], writes=[xk])
                self.store_x(i)
            if i + 2 < NT:
                self.load_x(i + 2, False)
        if state_only:
            if self.fused:
                groups = [[2 * k, 2 * k + 1] for k in range(NCORES // 2)]
                P.op("sp", lambda e: e.dma_start(out=self.cc_in[l], in_=S), reads=skeys, writes=["ccin%d" % l], dma="st_s")
                P.op("pool", lambda e: e.collective_compute("AllGather", ALU.bypass, replica_groups=groups,
                                                            ins=[self.cc_in[l]], outs=[self.cc_out[l]]),
                     reads=["ccin%d" % l], writes=["ccout%d" % l], dma="cc%d" % l, inc=1)
            else:
                P.op("sp", lambda e: e.dma_start(out=self.s_out, in_=S), reads=skeys, dma="st_s")

    def post_phase(self):
        A, P = self.A, self.P
        ps = self.ps
        xns = [A.f32(DC * T), A.f32(DC * T)]
        gidx = 6
        self.load_x(0, False)
        if NT > 1:
            self.load_x(1, False)
        for i in range(NT):
            b = i % 2
            xt = self.xt[b]
            xn = xns[b]
            if self.debug_raw_out:
                src = xt
                skey = "xt%d" % b
            else:
                hb = self.hb
                hkeys = ["hb%d" % c for c in range(DC)]
                P.op("act", lambda e, xt=xt: e.activation(hb, xt, AF.Square), reads=["xt%d" % b], writes=hkeys)
                P.op("pe", lambda e: _mm_group(e, ps[0][:, :], [
                    (self.ones_b, self.hb[:, c * T:(c + 1) * T]) for c in range(DC)]),
                    reads=hkeys + ["cst"], writes=["pb0"])
                self.rstd_from(ps[0][:, :], "pb0", float(D * EPS))
                for c in range(DC):
                    P.op("dve", lambda e, c=c, xt=xt, xn=xn: e.scalar_tensor_tensor(
                        out=xn[:, c * T:(c + 1) * T], in0=xt[:, c * T:(c + 1) * T],
                        scalar=self.g32[:, gidx * DC + c:gidx * DC + c + 1], in1=self.rstd,
                        op0=ALU.mult, op1=ALU.mult),
                        reads=["xt%d" % b, "rstd", "g32"], writes=["xn%d" % b])
                src = xn
                skey = "xn%d" % b
            P.op("sp", lambda e, i=i, src=src: e.dma_start(out=self.out[i], in_=src), reads=[skey], dma="sto%d" % b)
            if i + 2 < NT:
                self.load_x(i + 2, False)


def make_consts():
    import ml_dtypes
    c = np.zeros((128, 1152), np.float32)
    c[:, 0:128] = np.eye(128, dtype=np.float32)
    idb = np.eye(128, dtype=np.float32).astype(ml_dtypes.bfloat16)
    c[:, 128:192] = idb.view(np.float32)
    onb = np.ones((128, 128), np.float32).astype(ml_dtypes.bfloat16)
    c[:, 192:256] = onb.view(np.float32)
    tri = (np.arange(CH)[:, None] <= np.arange(CH)[None, :]).astype(np.float32)
    c[0:CH, 256:512] = np.tile(tri, (1, NH))
    c[:, 512:640] = (np.arange(128)[:, None] <= np.arange(128)[None, :]).astype(np.float32)
    rm = np.ones((T,), np.float32)
    rm[0::CH] = 0.0
    c[:, 640:1152] = rm[None, :]
    return c


def pack_gains(inp):
    g = np.zeros((128, 7, DC), np.float32)
    vecs = [inp["norm_ffn1"][0], inp["norm_mix"][0], inp["norm_ffn2"][0],
            inp["norm_ffn1"][1], inp["norm_mix"][1], inp["norm_ffn2"][1], inp["norm_final"]]
    for k, v in enumerate(vecs):
        g[:, k, :] = np.asarray(v, np.float32).reshape(DC, 128).T
    return np.ascontiguousarray(g.reshape(128, 7 * DC))


def pack_small(inp):
    f = lambda k: np.asarray(inp[k], np.float32)
    mixp = np.zeros((128, 32), np.float32)
    mixp[:, 0:8] = f("lb_param").reshape(DEPTH, NH, 128).transpose(2, 0, 1).reshape(128, 8)
    mixp[:, 8:16] = f("hgrn_norm").reshape(DEPTH, NH, 128).transpose(2, 0, 1).reshape(128, 8)
    mixp[:, 16:24] = f("ln_v_gain").reshape(DEPTH, 4, 128).transpose(2, 0, 1).reshape(128, 8)
    mixp[:, 24:32] = f("ln_v_bias").reshape(DEPTH, 4, 128).transpose(2, 0, 1).reshape(128, 8)
    wst = np.ascontiguousarray(f("w_spatial").transpose(0, 3, 1, 2).reshape(DEPTH, 128, 512))
    bsp = np.ascontiguousarray(np.broadcast_to(f("b_spatial").reshape(DEPTH, 1, 512), (DEPTH, 128, 512)))
    return {"mixp": mixp, "wst": wst, "bsp": bsp}


_CACHE = {}


def launch(phases, inputs, xs_in=None, s_in=None, xs_out=False, debug_raw_out=False, trace=False, fused=False, scopes=False):
    b = Builder(phases, debug_raw_out=debug_raw_out, xs_in=xs_in is not None, xs_out=xs_out, fused=fused, scopes=scopes)
    nc = b.build()
    f = lambda k: np.asarray(inputs[k], np.float32)
    common = {"gains": pack_gains(inputs), "cst": make_consts()}
    common.update(pack_small(inputs))
    if 1 in b.need_f:
        common.update({"wg1": f("ffn1_w_gate"), "wu1": f("ffn1_w_up"), "wd1": f("ffn1_w_down")})
    if 2 in b.need_f:
        common.update({"wg2": f("ffn2_w_gate"), "wu2": f("ffn2_w_up"), "wd2": f("ffn2_w_down")})
    if b.need_m:
        common.update({"w_in": f("w_in"), "w_out": f("w_out")})
    x = np.ascontiguousarray(f("x").reshape(NCORES, NT, T, DC, 128).transpose(0, 1, 4, 3, 2).reshape(NCORES, NT, 128, DC * T))
    in_maps = []
    for r in range(NCORES):
        m = dict(common)
        m["x_in"] = x[r]
        if fused:
            m["role"] = np.full((128, 1), float(r % 2), np.float32)
        else:
            m["s_in"] = np.zeros((128, 512), np.float32) if s_in is None else np.ascontiguousarray(s_in[r])
        if xs_in is not None:
            m["xs_in"] = np.ascontiguousarray(xs_in[r])
        in_maps.append(m)
    res = run_bass_kernel_spmd(nc, in_maps, core_ids=list(range(NCORES)), trace=trace)
    outs = {k: np.stack([np.asarray(r[k]) for r in res.results], 0) for k in res.results[0].keys()}
    return outs, res


def handoff(s_out):
    s_in = np.zeros_like(s_out)
    s_in[1::2] = s_out[0::2]
    return s_in


FUSED_PHASES = [("ffn", 0, 1, True), ("mix", 0, True), ("mix", 0, False), ("ffn", 0, 2, False),
                ("ffn", 1, 1, False), ("mix", 1, True), ("mix", 1, False), ("ffn", 1, 2, False), ("post",)]


def kernel(**inputs):
    o, _ = launch(FUSED_PHASES, inputs, fused=True)
    return from_fm(o["out"])


def from_fm(o):
    o = np.asarray(o, np.float32).reshape(NCORES, NT, 128, DC, T).transpose(0, 1, 4, 3, 2)
    return np.ascontiguousarray(o.reshape(4, 8192, D))
```
